# Optimizing a Trainium2 kernel written in Bass

```python
import math
import jax
import jax.numpy as jnp
from jax import lax
import numpy as np

D_MODEL = 1024
BATCH = 16
SEQ = 4096
DEPTH = 4
DEC_BATCH = 8
DEC_SEQ = 32
PAST_LEN = 4096

CHUNK = 64
Q_BLOCK = 128
N_META = 16
RMS_EPS = 1e-6
NEG_INF = -1e30
HEAD_DIM = 64
GDN_HEADS = D_MODEL // 128
GDN_DK = 128
GDN_DV = 128
GDN_QKV = GDN_HEADS * (2 * GDN_DK + GDN_DV)
CONV_W = 4
SSM_WIDTH = D_MODEL // 2
SSM_GROUP = 16
SSM_GROUPS = SSM_WIDTH // SSM_GROUP
SSM_STATE = 64
FOX_HEADS = D_MODEL // 128
DIFF_HEADS = D_MODEL // 256
DIFF_V = 2 * HEAD_DIM
ROT_DIM = HEAD_DIM // 4
ROPE_THETA = 500000.0
FFN_HIDDEN = ((8 * D_MODEL + 3 * 256 - 1) // (3 * 256)) * 256
EVEN_PROJ_SIZES = (GDN_QKV, GDN_HEADS * GDN_DV, GDN_HEADS, GDN_HEADS, SSM_WIDTH)
EVEN_PROJ = sum(EVEN_PROJ_SIZES)
EVEN_MIX = GDN_HEADS * GDN_DV + SSM_WIDTH
ODD_PROJ_SIZES = (FOX_HEADS * HEAD_DIM,) * 3 + (FOX_HEADS,) + (2 * DIFF_HEADS * HEAD_DIM,) * 2 + (DIFF_HEADS * DIFF_V,)
ODD_PROJ = sum(ODD_PROJ_SIZES)
ODD_MIX = FOX_HEADS * HEAD_DIM + DIFF_HEADS * DIFF_V

kernel_name = 'hybrid_streaming_encoder_step'


def split_cols(a, sizes):
    cuts = [int(c) for c in np.cumsum(sizes)[:-1]]
    return jnp.split(a, cuts, axis=-1)


def rms_norm(x, gain):
    xf = x.astype(jnp.float32)
    y = xf * lax.rsqrt(jnp.mean(xf * xf, axis=-1, keepdims=True) + RMS_EPS)
    return (y * gain.astype(jnp.float32)).astype(x.dtype)


def l2_normalize(x):
    return x * lax.rsqrt(jnp.sum(x * x, axis=-1, keepdims=True) + 1e-6)


def swiglu(h, w1, w3, w2):
    return (jax.nn.silu(h @ w1) * (h @ w3)) @ w2


def partial_rope(x, pos):
    half = ROT_DIM // 2
    inv_freq = ROPE_THETA ** (-jnp.arange(0, ROT_DIM, 2, dtype=jnp.float32) / ROT_DIM)
    ang = pos.astype(jnp.float32)[:, None] * inv_freq[None, :]
    cos = jnp.cos(ang)[None, :, None, :]
    sin = jnp.sin(ang)[None, :, None, :]
    xr = x[..., :ROT_DIM].astype(jnp.float32)
    x1, x2 = xr[..., :half], xr[..., half:]
    rot = jnp.concatenate([x1 * cos - x2 * sin, x2 * cos + x1 * sin], axis=-1).astype(x.dtype)
    return jnp.concatenate([rot, x[..., ROT_DIM:]], axis=-1)


def causal_short_conv(x, buf, w):
    T = x.shape[1]
    xp = jnp.concatenate([buf.astype(x.dtype), x], axis=1)
    y = sum(xp[:, j:j + T] * w[j] for j in range(CONV_W))
    return jax.nn.silu(y), xp[:, T:]


def gated_delta_rule(q, k, v, beta, logg, s0):
    bsz, T, H, _ = q.shape
    dv = v.shape[-1]
    n = -(-T // CHUNK)
    pad = n * CHUNK - T

    def blocks(a):
        a = a.reshape(a.shape[:3] + (-1,))
        a = jnp.pad(a, ((0, 0), (0, pad), (0, 0), (0, 0)))
        return jnp.transpose(a.reshape(bsz, n, CHUNK, H, -1), (1, 0, 3, 2, 4))

    qc, kc, vc = blocks(q), blocks(k), blocks(v)
    bc = blocks(beta)[..., 0]
    gc = jnp.cumsum(blocks(logg)[..., 0], axis=-1)
    causal = jnp.tril(jnp.ones((CHUNK, CHUNK), dtype=bool))
    strict = jnp.tril(jnp.ones((CHUNK, CHUNK), dtype=bool), -1)
    decay = jnp.exp(jnp.where(causal, gc[..., :, None] - gc[..., None, :], -jnp.inf))
    kk = jnp.einsum('nbhid,nbhjd->nbhij', kc, kc)
    a_low = jnp.where(strict, bc[..., :, None] * kk * decay, 0.0)
    rhs = jnp.concatenate([vc * bc[..., None], kc * (bc * jnp.exp(gc))[..., None]], axis=-1)
    sol = lax.linalg.triangular_solve(a_low + jnp.eye(CHUNK, dtype=a_low.dtype), rhs,
                                      left_side=True, lower=True)
    u, w = sol[..., :dv], sol[..., dv:]
    qk = jnp.where(causal, jnp.einsum('nbhid,nbhjd->nbhij', qc, kc) * decay, 0.0)

    def step(S, blk):
        q_, k_, u_, w_, qk_, g_ = blk
        v_new = u_ - jnp.einsum('bhcd,bhde->bhce', w_, S)
        o = (jnp.einsum('bhcd,bhde->bhce', q_ * jnp.exp(g_)[..., None], S)
             + jnp.einsum('bhij,bhje->bhie', qk_, v_new))
        g_last = g_[..., -1:]
        S = (S * jnp.exp(g_last)[..., None]
             + jnp.einsum('bhcd,bhce->bhde', k_ * jnp.exp(g_last - g_)[..., None], v_new))
        return S, o

    s_final, o = lax.scan(step, s0, (qc, kc, u, w, qk, gc))
    o = jnp.transpose(o, (1, 0, 3, 2, 4)).reshape(bsz, n * CHUNK, H, dv)[:, :T]
    return o, s_final


def s5_mixer(u, re0, im0, lam_re, lam_im, log_dt, b_re, b_im, c_re, c_im, d_skip, glu_w, glu_b):
    bsz, T, _ = u.shape
    uf = u.astype(jnp.float32).reshape(bsz, T, SSM_GROUPS, SSM_GROUP)
    lr = jnp.minimum(lam_re.astype(jnp.float32), -1e-4)
    li = lam_im.astype(jnp.float32)
    dt = jnp.exp(log_dt.astype(jnp.float32))[:, None]
    mag = jnp.exp(lr * dt)
    ar, ai = mag * jnp.cos(li * dt), mag * jnp.sin(li * dt)
    den = lr * lr + li * li
    nr, ni = ar - 1.0, ai
    cr, ci = (nr * lr + ni * li) / den, (ni * lr - nr * li) / den
    br, bi = b_re.astype(jnp.float32), b_im.astype(jnp.float32)
    bbr = cr[..., None] * br - ci[..., None] * bi
    bbi = cr[..., None] * bi + ci[..., None] * br
    bur = jnp.einsum('btgm,gpm->btgp', uf, bbr)
    bui = jnp.einsum('btgm,gpm->btgp', uf, bbi)
    r0, i0 = re0.astype(jnp.float32), im0.astype(jnp.float32)
    bur = bur.at[:, 0].add(ar * r0 - ai * i0)
    bui = bui.at[:, 0].add(ar * i0 + ai * r0)
    a_r = jnp.broadcast_to(ar, (1, T) + ar.shape)
    a_i = jnp.broadcast_to(ai, (1, T) + ai.shape)

    def combine(e1, e2):
        a1r, a1i, b1r, b1i = e1
        a2r, a2i, b2r, b2i = e2
        return (a1r * a2r - a1i * a2i, a1r * a2i + a1i * a2r,
                a2r * b1r - a2i * b1i + b2r, a2r * b1i + a2i * b1r + b2i)

    _, _, xr, xi = lax.associative_scan(combine, (a_r, a_i, bur, bui), axis=1)
    y = (jnp.einsum('btgp,gmp->btgm', xr, c_re.astype(jnp.float32))
         - jnp.einsum('btgp,gmp->btgm', xi, c_im.astype(jnp.float32))
         + uf * d_skip.astype(jnp.float32).reshape(SSM_GROUPS, SSM_GROUP))
    hg = jax.nn.gelu(y.reshape(bsz, T, SSM_WIDTH))
    out = hg * jax.nn.sigmoid(hg @ glu_w.astype(jnp.float32) + glu_b.astype(jnp.float32))
    return out.astype(u.dtype), xr[:, -1], xi[:, -1]


def even_mixer(h, conv_buf, s_delta, s_re, s_im, w_in, w_out, conv_w, a_log, dt_bias, out_gain,
               lam_re, lam_im, log_dt, b_re, b_im, c_re, c_im, d_skip, glu_w, glu_b):
    bsz, T, _ = h.shape
    qkv, z, beta_in, a_in, u = split_cols(h @ w_in, EVEN_PROJ_SIZES)
    qkv, new_buf = causal_short_conv(qkv, conv_buf, conv_w)
    q, k, v = split_cols(qkv.astype(jnp.float32), (GDN_HEADS * GDN_DK, GDN_HEADS * GDN_DK, GDN_HEADS * GDN_DV))
    q = l2_normalize(q.reshape(bsz, T, GDN_HEADS, GDN_DK)) * (GDN_DK ** -0.5)
    k = l2_normalize(k.reshape(bsz, T, GDN_HEADS, GDN_DK))
    v = v.reshape(bsz, T, GDN_HEADS, GDN_DV)
    beta = jax.nn.sigmoid(beta_in.astype(jnp.float32))
    logg = -jnp.exp(a_log.astype(jnp.float32)) * jax.nn.softplus(a_in.astype(jnp.float32) + dt_bias.astype(jnp.float32))
    o, s_new = gated_delta_rule(q, k, v, beta, logg, s_delta.astype(jnp.float32))
    o = rms_norm(o, out_gain) * jax.nn.silu(z.astype(jnp.float32).reshape(bsz, T, GDN_HEADS, GDN_DV))
    out_a = o.reshape(bsz, T, GDN_HEADS * GDN_DV).astype(h.dtype)
    out_b, re_new, im_new = s5_mixer(u, s_re, s_im, lam_re, lam_im, log_dt, b_re, b_im, c_re, c_im, d_skip, glu_w, glu_b)
    y = jnp.concatenate([out_a, out_b], axis=-1) @ w_out
    return (y, new_buf, s_new.astype(h.dtype), re_new.astype(h.dtype), im_new.astype(h.dtype))


def sweep_query_blocks(fn, *q_arrays):
    T = q_arrays[0].shape[1]
    nb = -(-T // Q_BLOCK)
    pad = nb * Q_BLOCK - T

    def split(a):
        a = jnp.pad(a, [(0, 0), (0, pad)] + [(0, 0)] * (a.ndim - 2))
        return jnp.swapaxes(a.reshape((a.shape[0], nb, Q_BLOCK) + a.shape[2:]), 0, 1)

    out = lax.map(lambda blk: fn(*blk), tuple(split(a) for a in q_arrays))
    out = jnp.swapaxes(out, 0, 1)
    return out.reshape((out.shape[0], nb * Q_BLOCK) + out.shape[3:])[:, :T]


def odd_mixer(h, kf_past, vf_past, lf_past, kd_past, vd_past, pos, cid, past_pos, past_cid,
              w_in, w_out, f_bias, diff_lambda, diff_gain, lam_init):
    bsz, T, _ = h.shape
    qf, kf, vf, f_in, qd, kd, vd = split_cols(h @ w_in, ODD_PROJ_SIZES)

    def heads(a, n):
        return a.reshape(bsz, T, n, -1)

    qf, kf, vf = heads(qf, FOX_HEADS), heads(kf, FOX_HEADS), heads(vf, FOX_HEADS)
    logf = jax.nn.log_sigmoid(f_in.astype(jnp.float32) + f_bias.astype(jnp.float32))
    qd = partial_rope(heads(qd, 2 * DIFF_HEADS), pos)
    kd = partial_rope(heads(kd, 2 * DIFF_HEADS), pos)
    vd = heads(vd, DIFF_HEADS)
    kf_all = jnp.concatenate([kf_past.astype(h.dtype), kf], axis=1)
    vf_all = jnp.concatenate([vf_past.astype(h.dtype), vf], axis=1)
    kd_all = jnp.concatenate([kd_past.astype(h.dtype), kd], axis=1)
    vd_all = jnp.concatenate([vd_past.astype(h.dtype), vd], axis=1)
    cum_f = jnp.cumsum(jnp.concatenate([lf_past.astype(jnp.float32), logf], axis=1), axis=1)
    fk = jnp.transpose(cum_f, (0, 2, 1))
    fq = cum_f[:, cum_f.shape[1] - T:]
    kpos = jnp.concatenate([past_pos, pos])
    kcid = jnp.concatenate([past_cid, cid])
    lq1, lk1, lq2, lk2 = diff_lambda.astype(jnp.float32)
    lam = jnp.exp(jnp.sum(lq1 * lk1)) - jnp.exp(jnp.sum(lq2 * lk2)) + lam_init
    scale = HEAD_DIM ** -0.5

    def attend(qf_b, fq_b, qd_b, pos_b, cid_b):
        nq = qf_b.shape[1]
        s = jnp.einsum('bqhd,bkhd->bhqk', qf_b, kf_all, preferred_element_type=jnp.float32) * scale
        s = s + jnp.transpose(fq_b, (0, 2, 1))[..., None] - fk[:, :, None, :]
        p = jax.nn.softmax(jnp.where(kpos[None, :] <= pos_b[0][:, None], s, NEG_INF), axis=-1)
        o_fox = jnp.einsum('bhqk,bkhd->bqhd', p.astype(vf_all.dtype), vf_all).reshape(bsz, nq, -1)
        s2 = jnp.einsum('bqhd,bkhd->bhqk', qd_b, kd_all, preferred_element_type=jnp.float32) * scale
        p2 = jax.nn.softmax(jnp.where(kcid[None, :] <= cid_b[0][:, None], s2, NEG_INF), axis=-1)
        p2 = p2.reshape(bsz, DIFF_HEADS, 2, nq, -1)
        pd = p2[:, :, 0] - lam * p2[:, :, 1]
        o_diff = jnp.einsum('bhqk,bkhe->bqhe', pd.astype(vd_all.dtype), vd_all)
        o_diff = rms_norm(o_diff, diff_gain) * (1.0 - lam_init)
        return jnp.concatenate([o_fox, o_diff.reshape(bsz, nq, -1)], axis=-1)

    mixed = sweep_query_blocks(attend, qf, fq, qd, pos[None], cid[None])
    return (mixed @ w_out, kf, vf, logf.astype(h.dtype), kd, vd)


def setup_inputs(seed: int = 0) -> dict:
    key = jax.random.key(seed)
    keys = iter(jax.random.split(key, 64))
    f32 = jnp.float32

    def normal(shape, scale=1.0):
        return jax.random.normal(next(keys), shape, f32) * scale

    def uniform(shape, lo, hi):
        return jax.random.uniform(next(keys), shape, f32, lo, hi)

    ne, no = (DEPTH + 1) // 2, DEPTH // 2
    gdn_dt = jnp.exp(uniform((ne, GDN_HEADS), math.log(1e-3), math.log(1e-1)))
    return {
        'x_prompt': normal((BATCH, SEQ, D_MODEL)),
        'x_sample': normal((DEC_BATCH, DEC_SEQ, D_MODEL)),
        'state_conv': normal((ne, DEC_BATCH, CONV_W - 1, GDN_QKV)),
        'state_delta': normal((ne, DEC_BATCH, GDN_HEADS, GDN_DK, GDN_DV), 0.1),
        'state_ssm_re': normal((ne, DEC_BATCH, SSM_GROUPS, SSM_STATE), 0.1),
        'state_ssm_im': normal((ne, DEC_BATCH, SSM_GROUPS, SSM_STATE), 0.1),
        'cache_fox_k': normal((no, DEC_BATCH, PAST_LEN, FOX_HEADS, HEAD_DIM)),
        'cache_fox_v': normal((no, DEC_BATCH, PAST_LEN, FOX_HEADS, HEAD_DIM)),
        'cache_fox_logf': jax.nn.log_sigmoid(3.0 + normal((no, DEC_BATCH, PAST_LEN, FOX_HEADS))),
        'cache_diff_k': normal((no, DEC_BATCH, PAST_LEN, 2 * DIFF_HEADS, HEAD_DIM)),
        'cache_diff_v': normal((no, DEC_BATCH, PAST_LEN, DIFF_HEADS, DIFF_V)),
        'meta_tokens': normal((N_META, D_MODEL)),
        'norm_mix': 1.0 + normal((DEPTH, D_MODEL), 0.02),
        'norm_ffn': 1.0 + normal((DEPTH, D_MODEL), 0.02),
        'norm_final': 1.0 + normal((D_MODEL,), 0.02),
        'w_in_even': normal((ne, D_MODEL, EVEN_PROJ), D_MODEL ** -0.5),
        'w_out_even': normal((ne, EVEN_MIX, D_MODEL), EVEN_MIX ** -0.5),
        'conv_w': normal((ne, CONV_W, GDN_QKV), CONV_W ** -0.5),
        'gdn_a_log': jnp.log(uniform((ne, GDN_HEADS), 1.0, 16.0)),
        'gdn_dt_bias': gdn_dt + jnp.log(-jnp.expm1(-gdn_dt)),
        'gdn_out_norm': 1.0 + normal((ne, GDN_DV), 0.02),
        'ssm_lambda_re': -0.5 + normal((ne, SSM_GROUPS, SSM_STATE), 0.01),
        'ssm_lambda_im': jnp.pi * jnp.arange(SSM_STATE, dtype=f32) + normal((ne, SSM_GROUPS, SSM_STATE), 0.01),
        'ssm_log_dt': uniform((ne, SSM_GROUPS), math.log(1e-3), math.log(1e-1)),
        'ssm_b_re': normal((ne, SSM_GROUPS, SSM_STATE, SSM_GROUP), (2 * SSM_GROUP) ** -0.5),
        'ssm_b_im': normal((ne, SSM_GROUPS, SSM_STATE, SSM_GROUP), (2 * SSM_GROUP) ** -0.5),
        'ssm_c_re': normal((ne, SSM_GROUPS, SSM_GROUP, SSM_STATE), SSM_STATE ** -0.5),
        'ssm_c_im': normal((ne, SSM_GROUPS, SSM_GROUP, SSM_STATE), SSM_STATE ** -0.5),
        'ssm_d': normal((ne, SSM_WIDTH)),
        'ssm_glu_w': normal((ne, SSM_WIDTH, SSM_WIDTH), SSM_WIDTH ** -0.5),
        'ssm_glu_b': normal((ne, SSM_WIDTH), 0.01),
        'w_in_odd': normal((no, D_MODEL, ODD_PROJ), D_MODEL ** -0.5),
        'w_out_odd': normal((no, ODD_MIX, D_MODEL), ODD_MIX ** -0.5),
        'fox_f_bias': 3.0 + normal((no, FOX_HEADS), 0.5),
        'diff_lambda': normal((no, 4, HEAD_DIM), 0.1),
        'diff_out_norm': 1.0 + normal((no, DIFF_V), 0.02),
        'ffn_w1': normal((DEPTH, D_MODEL, FFN_HIDDEN), D_MODEL ** -0.5),
        'ffn_w3': normal((DEPTH, D_MODEL, FFN_HIDDEN), D_MODEL ** -0.5),
        'ffn_w2': normal((DEPTH, FFN_HIDDEN, D_MODEL), FFN_HIDDEN ** -0.5),
    }


def reference(x_prompt, x_sample, state_conv, state_delta, state_ssm_re, state_ssm_im,
              cache_fox_k, cache_fox_v, cache_fox_logf, cache_diff_k, cache_diff_v,
              meta_tokens, norm_mix, norm_ffn, norm_final,
              w_in_even, w_out_even, conv_w, gdn_a_log, gdn_dt_bias, gdn_out_norm,
              ssm_lambda_re, ssm_lambda_im, ssm_log_dt, ssm_b_re, ssm_b_im, ssm_c_re, ssm_c_im,
              ssm_d, ssm_glu_w, ssm_glu_b,
              w_in_odd, w_out_odd, fox_f_bias, diff_lambda, diff_out_norm,
              ffn_w1, ffn_w3, ffn_w2):
    dtp = x_prompt.dtype
    bp = x_prompt.shape[0]
    meta = jnp.broadcast_to(meta_tokens.astype(dtp)[None], (bp, N_META, D_MODEL))
    hp = jnp.concatenate([meta, x_prompt], axis=1)
    hs = x_sample
    lp, ls = hp.shape[1], hs.shape[1]
    pos_p = jnp.arange(lp, dtype=jnp.int32)
    cid_p = jnp.where(pos_p < N_META, -1, (pos_p - N_META) // CHUNK)
    pos_s = PAST_LEN + jnp.arange(ls, dtype=jnp.int32)
    cid_s = pos_s // CHUNK
    past_pos = jnp.arange(PAST_LEN, dtype=jnp.int32)
    past_cid = past_pos // CHUNK
    no_pos = jnp.zeros((0,), jnp.int32)

    conv_p, conv_s, delta_p, delta_s = [], [], [], []
    re_p, re_s, im_p, im_s = [], [], [], []
    fk_p, fk_s, fv_p, fv_s, fl_p, fl_s = [], [], [], [], [], []
    dk_p, dk_s, dv_p, dv_s = [], [], [], []
    for l in range(DEPTH):
        i = l // 2
        if l % 2 == 0:
            pe = (w_in_even[i], w_out_even[i], conv_w[i], gdn_a_log[i], gdn_dt_bias[i], gdn_out_norm[i],
                  ssm_lambda_re[i], ssm_lambda_im[i], ssm_log_dt[i], ssm_b_re[i], ssm_b_im[i],
                  ssm_c_re[i], ssm_c_im[i], ssm_d[i], ssm_glu_w[i], ssm_glu_b[i])
            yp, cb, sd, sr, si = even_mixer(
                rms_norm(hp, norm_mix[l]),
                jnp.zeros((bp, CONV_W - 1, GDN_QKV), dtp),
                jnp.zeros((bp, GDN_HEADS, GDN_DK, GDN_DV), jnp.float32),
                jnp.zeros((bp, SSM_GROUPS, SSM_STATE), jnp.float32),
                jnp.zeros((bp, SSM_GROUPS, SSM_STATE), jnp.float32), *pe)
            ys, cb2, sd2, sr2, si2 = even_mixer(
                rms_norm(hs, norm_mix[l]), state_conv[i], state_delta[i],
                state_ssm_re[i], state_ssm_im[i], *pe)
            conv_p.append(cb); conv_s.append(cb2)
            delta_p.append(sd); delta_s.append(sd2)
            re_p.append(sr); re_s.append(sr2)
            im_p.append(si); im_s.append(si2)
        else:
            lam_init = 0.8 - 0.6 * math.exp(-0.3 * l)
            po = (w_in_odd[i], w_out_odd[i], fox_f_bias[i], diff_lambda[i], diff_out_norm[i], lam_init)
            yp, a1, a2, a3, a4, a5 = odd_mixer(
                rms_norm(hp, norm_mix[l]),
                jnp.zeros((bp, 0, FOX_HEADS, HEAD_DIM), dtp), jnp.zeros((bp, 0, FOX_HEADS, HEAD_DIM), dtp),
                jnp.zeros((bp, 0, FOX_HEADS), jnp.float32),
                jnp.zeros((bp, 0, 2 * DIFF_HEADS, HEAD_DIM), dtp), jnp.zeros((bp, 0, DIFF_HEADS, DIFF_V), dtp),
                pos_p, cid_p, no_pos, no_pos, *po)
            ys, b1, b2, b3, b4, b5 = odd_mixer(
                rms_norm(hs, norm_mix[l]), cache_fox_k[i], cache_fox_v[i], cache_fox_logf[i],
                cache_diff_k[i], cache_diff_v[i], pos_s, cid_s, past_pos, past_cid, *po)
            fk_p.append(a1); fk_s.append(b1)
            fv_p.append(a2); fv_s.append(b2)
            fl_p.append(a3); fl_s.append(b3)
            dk_p.append(a4); dk_s.append(b4)
            dv_p.append(a5); dv_s.append(b5)
        hp = hp + yp
        hs = hs + ys
        hp = hp + swiglu(rms_norm(hp, norm_ffn[l]), ffn_w1[l], ffn_w3[l], ffn_w2[l])
        hs = hs + swiglu(rms_norm(hs, norm_ffn[l]), ffn_w1[l], ffn_w3[l], ffn_w2[l])

    y_prompt = rms_norm(hp, norm_final)[:, N_META:]
    y_sample = rms_norm(hs, norm_final)
    return (y_prompt, y_sample,
            jnp.stack(conv_p), jnp.stack(conv_s), jnp.stack(delta_p), jnp.stack(delta_s),
            jnp.stack(re_p), jnp.stack(re_s), jnp.stack(im_p), jnp.stack(im_s),
            jnp.stack(fk_p), jnp.stack(fk_s), jnp.stack(fv_p), jnp.stack(fv_s),
            jnp.stack(fl_p), jnp.stack(fl_s), jnp.stack(dk_p), jnp.stack(dk_s),
            jnp.stack(dv_p), jnp.stack(dv_s))
```

```python
import contextlib
import math
import os
import numpy as np
import concourse.bass as bass
import concourse.mybir as mybir
from concourse.bass_utils import run_bass_kernel_spmd

F32 = mybir.dt.float32
BF16 = mybir.dt.bfloat16
AF = mybir.ActivationFunctionType
ALU = mybir.AluOpType
AX = mybir.AxisListType

NCORES = 8
D = 1024
NE = 2
NO = 2
DEPTH = 4
LP = 4112
LS = 32
NROWS = 2 * LP + LS
SEQROW = [0, LP, 2 * LP]
SEQLEN = [LP, LP, LS]
PCOL = [0, 4160, 8320, 8384]
TOTP = 8448
EVEN_PROJ = 4624
ODD_PROJ = 3080
FFN = 2816
RMS_EPS = 1e-6
AK = [0, LP, 2 * LP]
NK = 2 * LP + 4096 + LS
CB = [AK[0], AK[1] + 1, AK[2] + 2]
DTSIZE = {F32: 4, BF16: 2, mybir.dt.int32: 4}

SEGS = []
for _s in range(3):
    if _s < 2:
        SEGS.append((SEQROW[_s], 16, _s, 0, PCOL[_s]))
        for _k in range(8):
            SEGS.append((SEQROW[_s] + 16 + 512 * _k, 512, _s, 16 + 512 * _k, PCOL[_s] + 64 + 512 * _k))
    else:
        SEGS.append((SEQROW[_s], 32, _s, 0, PCOL[_s]))


class Dep:
    __slots__ = ("w", "r")

    def __init__(self):
        self.w = None
        self.r = []


class Tl:
    __slots__ = ("ap", "dep")

    def __init__(self, ap, dep=None):
        self.ap = ap
        self.dep = dep if dep is not None else Dep()

    def __getitem__(self, k):
        return self.ap[k]


class Prog:
    ENGS = ("pe", "act", "dve", "pool", "sp")
    EPOCH = 8000
    NDMA = 20

    def __init__(self, nc):
        self.nc = nc
        self.ops = []
        self.streams = {e: [] for e in self.ENGS}
        self.pending = {}

    def op(self, eng, fn, reads=(), writes=(), dma=False):
        idx = len(self.ops)
        cls = eng + ("_dma" if dma else "")
        raw = set()
        oth = set()
        for d in reads:
            if d.w is not None:
                raw.add(d.w)
        for d in writes:
            if d.w is not None:
                oth.add(d.w)
            oth.update(d.r)
        deps = set(raw)
        for j in oth:
            if j in raw:
                continue
            if self.ops[j][5] == cls and not dma:
                continue
            deps.add(j)
        if cls == "pe":
            deps = {j for j in deps if self.ops[j][5] != "pe"}
        pend = self.pending.pop(eng, None)
        if pend:
            deps |= pend
        deps.discard(idx)
        for d in reads:
            d.r.append(idx)
        for d in writes:
            d.w = idx
            d.r = []
        for j in deps:
            self.ops[j][4] = True
        self.ops.append([eng, fn, deps, dma, False, cls])
        self.streams[eng].append(idx)
        return idx

    def barrier(self):
        pend = set()
        for e in self.ENGS:
            comp = None
            nd = 0
            for i in reversed(self.streams[e]):
                if self.ops[i][3]:
                    if nd < self.NDMA:
                        pend.add(i)
                        nd += 1
                elif comp is None:
                    comp = i
                    pend.add(i)
                if comp is not None and nd >= self.NDMA:
                    break
        for e in self.ENGS:
            cur = self.pending.get(e, set())
            self.pending[e] = cur | pend

    def emit(self, stack):
        nc = self.nc
        ticket = {}
        comp_count = {e: 0 for e in self.ENGS}
        dma_count = {e: 0 for e in self.ENGS}
        dma_prev = {}
        for e in self.ENGS:
            for idx in self.streams[e]:
                o = self.ops[idx]
                if o[3]:
                    k = dma_count[e]
                    dma_count[e] += 1
                    key = (e, "d", k % self.NDMA)
                    val = 16 * (k // self.NDMA + 1)
                    ticket[idx] = (key, val)
                    dma_prev[idx] = (key, val - 16)
                elif o[4]:
                    comp_count[e] += 1
                    c = comp_count[e]
                    ep = (c - 1) // self.EPOCH
                    ticket[idx] = ((e, "c", ep), c - ep * self.EPOCH)
        sems = {}
        for k in sorted(set(t[0] for t in ticket.values())):
            sems[k] = stack.enter_context(nc.semaphore("s_%s_%s_%d" % k))
        block = stack.enter_context(nc.Block())
        ops = self.ops
        streams = self.streams

        def build(e, engine):
            waited = {}
            last = {}
            for idx in streams[e]:
                o = ops[idx]
                need = {}
                for j in o[2]:
                    if j not in ticket:
                        continue
                    k, v = ticket[j]
                    if need.get(k, 0) < v:
                        need[k] = v
                if o[3]:
                    k, v = dma_prev[idx]
                    if v > 0 and need.get(k, 0) < v:
                        need[k] = v
                for k, v in need.items():
                    if waited.get(k, 0) >= v:
                        continue
                    engine.wait_ge(sems[k], v)
                    waited[k] = v
                inst = o[1](engine)
                if o[3]:
                    k, v = ticket[idx]
                    inst.then_inc(sems[k], 16)
                    last[k] = v
                elif o[4]:
                    inst.then_inc(sems[ticket[idx][0]], 1)
            for k, v in last.items():
                if waited.get(k, 0) < v:
                    engine.wait_ge(sems[k], v)

        @block.tensor
        def _(eng):
            build("pe", eng)

        @block.scalar
        def _(eng):
            build("act", eng)

        @block.vector
        def _(eng):
            build("dve", eng)

        @block.gpsimd
        def _(eng):
            build("pool", eng)

        @block.sync
        def _(eng):
            build("sp", eng)


class Builder:
    ARENA_F32 = 49152

    def __init__(self, stages):
        self.stages = stages
        self.nc = bass.Bass("TRN2", target_bir_lowering=False)
        self.P = Prog(self.nc)
        self.stack = contextlib.ExitStack()
        self.din = {}
        self.dout = {}
        self.scr = {}

    def inp(self, name, shape):
        self.din[name] = self.nc.dram_tensor(name, list(shape), F32, kind="ExternalInput").ap()
        return self.din[name]

    def outp(self, name, shape):
        self.dout[name] = self.nc.dram_tensor(name, list(shape), F32, kind="ExternalOutput").ap()
        return self.dout[name]

    def scratch(self, name, shape, dt):
        self.scr[name] = self.nc.dram_tensor(name, list(shape), dt, kind="Internal").ap()
        return self.scr[name]

    def reset_arena(self):
        self.aoff = self.aperm

    def sb(self, free, dt, parts=128):
        n = 1
        for f in free:
            n *= f
        nbytes = n * DTSIZE[dt]
        nb = (nbytes + 31) // 32 * 32
        assert self.aoff + nb <= self.ARENA_F32 * 4, "arena overflow %d" % (self.aoff + nb)
        v = self.arena[:, self.aoff // 4:(self.aoff + nbytes + 3) // 4]
        self.aoff += nb
        if dt != F32:
            v = v.bitcast(dt)
        v = v[:, 0:n]
        if len(free) == 2:
            v = v.rearrange("p (a b) -> p a b", a=free[0])
        elif len(free) == 3:
            v = v.rearrange("p (a b c) -> p a b c", a=free[0], b=free[1])
        if parts != 128:
            v = v[0:parts]
        return Tl(v)

    def psum(self, bank, dt=F32):
        t = self.banks[bank]
        ap = t.ap if dt == F32 else t.ap.bitcast(dt)
        return Tl(ap, t.dep)

    def op(self, eng, fn, r=(), w=()):
        self.P.op(eng, fn, [t.dep for t in r], [t.dep for t in w])

    def dma(self, q, out, in_, r=(), w=(), **kw):
        self.P.op(q, lambda e: e.dma_start(out=out, in_=in_, **kw), [t.dep for t in r], [t.dep for t in w], dma=True)

    def mm(self, out, lhsT, rhs, start, stop, r, w):
        self.op("pe", lambda e: e.matmul(out, lhsT=lhsT, rhs=rhs, start=start, stop=stop), r, w)

    def tr(self, out, in_, ident, r, w):
        self.op("pe", lambda e: e.transpose(out, in_, ident), r, w)

    def act(self, out, in_, func, r, w, bias=None, scale=None, accum=None):
        kw = {}
        if bias is not None:
            kw["bias"] = bias
        if scale is not None:
            kw["scale"] = scale
        if accum is not None:
            kw["accum_out"] = accum
        self.op("act", lambda e: e.activation(out=out, in_=in_, func=func, **kw), r, w)

    def tt(self, eng, out, in0, in1, op, r, w):
        self.op(eng, lambda e: e.tensor_tensor(out=out, in0=in0, in1=in1, op=op), r, w)

    def ts(self, eng, out, in0, s1, op0, r, w, s2=None, op1=None):
        if op1 is None:
            self.op(eng, lambda e: e.tensor_scalar(out=out, in0=in0, scalar1=s1, scalar2=None, op0=op0), r, w)
        else:
            self.op(eng, lambda e: e.tensor_scalar(out=out, in0=in0, scalar1=s1, scalar2=s2, op0=op0, op1=op1), r, w)

    def stt(self, out, in0, scalar, in1, op0, op1, r, w):
        self.op("dve", lambda e: e.scalar_tensor_tensor(out=out, in0=in0, scalar=scalar, in1=in1, op0=op0, op1=op1), r, w)

    def cp(self, eng, out, in_, r, w):
        if eng == "act":
            self.op("act", lambda e: e.copy(out=out, in_=in_), r, w)
        else:
            self.op(eng, lambda e: e.tensor_copy(out=out, in_=in_), r, w)

    def memset(self, eng, ap, val, w):
        self.op(eng, lambda e: e.memset(ap, val), (), w)

    def declare(self):
        i = self.inp
        i("xp", [2, 4096, D]); i("xs", [LS, D]); i("meta", [16, D])
        i("st_conv", [NE, 3, 3072]); i("st_delta", [NE, 8, 128, 128])
        i("st_re", [NE, 32, 64]); i("st_im", [NE, 32, 64])
        i("c_fk", [NO, 4096, 8, 64]); i("c_fv", [NO, 4096, 8, 64]); i("c_fl", [NO, 4096, 8])
        i("c_dk", [NO, 4096, 8, 64]); i("c_dv", [NO, 4096, 4, 128])
        i("norm_mix", [DEPTH, D]); i("norm_ffn", [DEPTH, D]); i("norm_final", [D])
        i("w_in_even", [NE, D, EVEN_PROJ]); i("w_out_even", [NE, 1536, D]); i("conv_w", [NE, 4, 3072])
        i("gdn_a_log", [NE, 8]); i("gdn_dt_bias", [NE, 8]); i("gdn_out_norm", [NE, 128])
        i("ssm_lambda_re", [NE, 32, 64]); i("ssm_lambda_im", [NE, 32, 64]); i("ssm_log_dt", [NE, 32])
        i("ssm_b_re", [NE, 32, 64, 16]); i("ssm_b_im", [NE, 32, 64, 16])
        i("ssm_c_re", [NE, 32, 16, 64]); i("ssm_c_im", [NE, 32, 16, 64])
        i("ssm_d", [NE, 512]); i("ssm_glu_w", [NE, 512, 512]); i("ssm_glu_b", [NE, 512])
        i("w_in_odd", [NO, D, ODD_PROJ]); i("w_out_odd", [NO, D, D])
        i("fox_f_bias", [NO, 8]); i("diff_lambda", [NO, 4, 64]); i("diff_out_norm", [NO, 128])
        i("ffn_w1", [DEPTH, D, FFN]); i("ffn_w3", [DEPTH, D, FFN]); i("ffn_w2", [DEPTH, FFN, D])
        o = self.outp
        o("y_p", [2, 4096, D]); o("y_s", [LS, D])
        o("conv_p", [NE, 2, 3, 3072]); o("conv_s", [NE, 3, 3072])
        o("delta_p", [NE, 2, 8, 128, 128]); o("delta_s", [NE, 8, 128, 128])
        o("re_p", [NE, 2, 32, 64]); o("re_s", [NE, 32, 64]); o("im_p", [NE, 2, 32, 64]); o("im_s", [NE, 32, 64])
        o("fk_p", [NO, 2, LP, 8, 64]); o("fk_s", [NO, LS, 8, 64])
        o("fv_p", [NO, 2, LP, 8, 64]); o("fv_s", [NO, LS, 8, 64])
        o("fl_p", [NO, 2, LP, 8]); o("fl_s", [NO, LS, 8])
        o("dk_p", [NO, 2, LP, 8, 64]); o("dk_s", [NO, LS, 8, 64])
        o("dv_p", [NO, 2, LP, 4, 128]); o("dv_s", [NO, LS, 4, 128])
        s = self.scratch
        s("Xres", [NROWS, D], F32)
        s("QT", [1024, TOTP], BF16); s("KT", [1024, TOTP], BF16); s("VT", [1024, TOTP], BF16)
        s("ZT", [1024, TOTP], BF16); s("UT", [512, TOTP], BF16)
        s("Btok", [TOTP, 8], F32); s("GCtok", [TOTP, 8], F32); s("GCT", [8, TOTP], F32)
        s("MIXT", [1536, TOTP], BF16)
        s("KFT", [512, NK], BF16); s("KDT", [512, NK], BF16)
        s("QFT", [512, NROWS], BF16); s("QDT", [512, NROWS], BF16)
        s("VF", [NK, 512], BF16); s("VD", [NK, 512], BF16)
        s("CUM", [NK + 3, 8], F32)

    def build(self):
        nc = self.nc
        self.declare()
        st = self.stack
        self.arena = st.enter_context(nc.sbuf_tensor("arena", [128, self.ARENA_F32], F32))
        self.banks = [Tl(st.enter_context(nc.psum_tensor("bank%d" % b, [128, 512], F32))[:]) for b in range(8)]
        self.aoff = 0
        self.aperm = 0
        self.ident = self.sb([128], BF16)
        self.identf = self.sb([128], F32)
        self.ones_bf = self.sb([128], BF16)
        self.tri = self.sb([128], F32)
        self.zeros = self.sb([512], F32)
        self.onec = self.sb([1], F32)
        self.trif = self.sb([128], F32)
        self.onesf = self.sb([128], F32)
        self.aperm = self.aoff
        self.consts()
        self.init_x()
        for l in range(DEPTH):
            if ("L%d" % l) not in self.stages:
                continue
            i_ = l // 2
            if l % 2 == 0:
                self.phaseA_even(l)
                self.phaseB_gdn(l)
                self.phaseB_s5(l)
                self.phaseC_outproj(l, self.din["w_out_even"][i_], 12)
            else:
                self.phaseA_odd(l)
                self.phaseB_attn(l)
                self.phaseC_outproj(l, self.din["w_out_odd"][i_], 8)
            self.phaseD_ffn(l)
        self.P.emit(st)
        return nc

    def consts(self):
        idb, idf, tri = self.ident, self.identf, self.tri
        self.memset("pool", idb[:], 1.0, [idb])
        self.op("pool", lambda e: e.affine_select(out=idb[:], in_=idb[:], pattern=[[-1, 128]], compare_op=ALU.is_equal,
                                                  fill=0.0, base=0, channel_multiplier=1), [idb], [idb])
        self.memset("pool", idf[:], 1.0, [idf])
        self.op("pool", lambda e: e.affine_select(out=idf[:], in_=idf[:], pattern=[[-1, 128]], compare_op=ALU.is_equal,
                                                  fill=0.0, base=0, channel_multiplier=1), [idf], [idf])
        self.memset("dve", self.ones_bf[:], 1.0, [self.ones_bf])
        self.memset("dve", self.zeros[:], 0.0, [self.zeros])
        self.memset("dve", self.onec[:], 1.0, [self.onec])
        self.memset("pool", tri[:], 1.0, [tri])
        self.op("pool", lambda e: e.affine_select(out=tri[:], in_=tri[:], pattern=[[1, 128]], compare_op=ALU.is_ge,
                                                  fill=0.0, base=0, channel_multiplier=-1), [tri], [tri])
        trif = self.trif
        self.memset("pool", trif[:], 1.0, [trif])
        self.op("pool", lambda e: e.affine_select(out=trif[:], in_=trif[:], pattern=[[1, 128]], compare_op=ALU.is_ge,
                                                  fill=0.0, base=0, channel_multiplier=-1), [trif], [trif])
        self.memset("pool", self.onesf[:], 1.0, [self.onesf])
        self.memset("pool", tri[64:128, 0:64], 0.0, [tri])
        self.memset("pool", tri[0:64, 64:128], 0.0, [tri])

    def init_x(self):
        X = self.scr["Xres"]
        for s in range(2):
            self.dma("sp", X[SEQROW[s]:SEQROW[s] + 16, :], self.din["meta"])
            for q in range(4):
                self.dma("sp", X[SEQROW[s] + 16 + 1024 * q:SEQROW[s] + 16 + 1024 * (q + 1), :],
                         self.din["xp"][s, 1024 * q:1024 * (q + 1), :])
        self.dma("sp", X[SEQROW[2]:SEQROW[2] + LS, :], self.din["xs"])
        z = self.zeros
        for name in ("QT", "KT", "VT"):
            A = self.scr[name].rearrange("(h p) t -> p h t", p=128)
            zb = z.ap.bitcast(BF16)
            for s, (c0, c1) in ((0, (16, 64)), (1, (16, 64)), (2, (32, 64)), (3, (0, 64))):
                w = c1 - c0
                src = zb[:, 0:8 * w].rearrange("p (h t) -> p h t", h=8)
                self.dma("sp", A[:, :, PCOL[s] + c0:PCOL[s] + c1], src, r=[z])
        for name in ("Btok", "GCtok"):
            A = self.scr[name]
            for s, (c0, c1) in ((0, (16, 64)), (1, (16, 64)), (2, (32, 64)), (3, (0, 64))):
                self.dma("sp", A[PCOL[s] + c0:PCOL[s] + c1, :], z[0:c1 - c0, 0:8], r=[z])
        A = self.scr["GCT"]
        for s, (c0, c1) in ((0, (16, 64)), (1, (16, 64)), (2, (32, 64)), (3, (0, 64))):
            self.dma("sp", A[:, PCOL[s] + c0:PCOL[s] + c1], z[0:8, 0:c1 - c0], r=[z])
        for s_ in range(3):
            self.dma("sp", self.scr["CUM"][CB[s_]:CB[s_] + 1, :], z[0:1, 0:8], r=[z])
        self.P.barrier()

    def load_weight(self, wdram, K, N, dst, gain=None, stage_cols=2312):
        kc = K // 128
        stg = [self.sb([stage_cols], F32), self.sb([stage_cols], F32)]
        i = 0
        for c in range(kc):
            for n0 in range(0, N, stage_cols):
                n1 = min(N, n0 + stage_cols)
                s = stg[i % 2]
                i += 1
                self.dma("pool" if i % 2 else "sp", s[:, 0:n1 - n0], wdram[c * 128:(c + 1) * 128, n0:n1], w=[s])
                if gain is not None:
                    self.act(dst[:, c, n0:n1], s[:, 0:n1 - n0], AF.Copy, [s, gain], [dst], scale=gain[:, c:c + 1])
                else:
                    self.cp("act", dst[:, c, n0:n1], s[:, 0:n1 - n0], [s], [dst])

    def load_gain(self, vec_ap):
        g = self.sb([8], F32)
        self.dma("sp", g[:], vec_ap.rearrange("(c p) -> p c", p=128), w=[g], allow_slow_non_contiguous=True)
        return g

    def load_x(self, r0, n, xt):
        X = self.scr["Xres"]
        nsub = (n + 127) // 128
        if n >= 128:
            self.dma("sp", xt[:, 0:nsub, :], X[r0:r0 + n, :].rearrange("(j p) d -> p j d", p=128), w=[xt])
        else:
            self.dma("sp", xt[0:n, 0, :], X[r0:r0 + n, :], w=[xt])

    def store_x(self, r0, n, xt):
        X = self.scr["Xres"]
        nsub = (n + 127) // 128
        if n >= 128:
            self.dma("pool", X[r0:r0 + n, :].rearrange("(j p) d -> p j d", p=128), xt[:, 0:nsub, :], r=[xt])
        else:
            self.dma("pool", X[r0:r0 + n, :], xt[0:n, 0, :], r=[xt])

    def norm_transpose(self, seg, xt, h, junk, ss, hT, bank, load=True):
        r0, n = seg[0], seg[1]
        X = self.scr["Xres"]
        nsub = (n + 127) // 128
        if not load:
            pass
        elif n >= 128:
            self.dma("sp", xt[:, 0:nsub, :], X[r0:r0 + n, :].rearrange("(j p) d -> p j d", p=128), w=[xt])
        else:
            self.dma("sp", xt[0:n, 0, :], X[r0:r0 + n, :], w=[xt])
        pj = min(128, n)
        for j in range(nsub):
            self.act(junk[0:pj, :], xt[0:pj, j, :], AF.Square, [xt], [junk, ss], accum=ss[0:pj, j:j + 1])
        self.act(ss[0:pj, 0:nsub], ss[0:pj, 0:nsub], AF.Sqrt, [ss], [ss], bias=self.epsc[0:pj, :], scale=1.0 / D)
        self.op("dve", lambda e: e.reciprocal(out=ss[0:pj, 0:nsub], in_=ss[0:pj, 0:nsub]), [ss], [ss])
        for j in range(nsub):
            self.ts("dve", h[0:pj, j, :], xt[0:pj, j, :], ss[0:pj, j:j + 1], ALU.mult, [xt, ss], [h])
        for j in range(nsub):
            pT = self.psum(bank[j % len(bank)], BF16)
            pv = pT.ap.rearrange("p (c t) -> p c t", c=8)
            for c in range(8):
                self.tr(pv[:, c, 0:pj], h[0:pj, j, c * 128:(c + 1) * 128], self.ident[0:pj, 0:pj], [h, self.ident], [pT])
            self.cp("act" if j % 2 else "dve", hT[:, :, j * 128:j * 128 + pj], pv[:, :, 0:pj], [pT], [hT])

    def phaseA_even(self, l):
        i = l // 2
        self.reset_arena()
        S = self.scr
        self.epsc = self.sb([1], F32)
        self.memset("dve", self.epsc[:], RMS_EPS, [self.epsc])
        eps6 = self.sb([1], F32)
        self.memset("dve", eps6[:], 1e-6, [eps6])
        wbf = self.sb([8, EVEN_PROJ], BF16)
        gain = self.load_gain(self.din["norm_mix"][l])
        save = self.aoff
        self.load_weight(self.din["w_in_even"][i], D, EVEN_PROJ, wbf, gain)
        cw = self.sb([24, 4], F32)
        for j in range(4):
            self.dma("sp", cw[:, :, j], self.din["conv_w"][i, j].rearrange("(m p) -> p m", p=128), w=[cw],
                     allow_slow_non_contiguous=True)
        dtb = self.sb([8], F32)
        self.dma("sp", dtb[:], self.din["gdn_dt_bias"][i:i + 1, :].to_broadcast([128, 8]), w=[dtb])
        nea = self.sb([8], F32)
        self.dma("sp", nea[:], self.din["gdn_a_log"][i:i + 1, :].to_broadcast([128, 8]), w=[nea])
        self.act(nea[:], nea[:], AF.Exp, [nea], [nea])
        self.ts("dve", nea[:], nea[:], -1.0, ALU.mult, [nea], [nea])
        halo = self.sb([24, 3], F32)
        xts = [self.sb([4, D], F32), self.sb([4, D], F32)]
        h = self.sb([4, D], BF16)
        junk = self.sb([D], BF16)
        sss = [self.sb([4], F32), self.sb([4], F32)]
        hTs = [self.sb([8, 512], BF16), self.sb([8, 512], BF16)]
        pcs = [self.sb([516], F32) for _ in range(2)]
        accs = [self.sb([512], F32) for _ in range(2)]
        svs = [self.sb([512], F32) for _ in range(2)]
        sqs = [self.sb([512], BF16) for _ in range(2)]
        rss = [self.sb([512], F32) for _ in range(2)]
        obs = [self.sb([512], BF16) for _ in range(3)]
        tks = [self.sb([16], F32) for _ in range(2)]
        gcs = [self.sb([8], F32) for _ in range(2)]
        gct = self.sb([512], F32)
        ob_i = 0
        for si, seg in enumerate(SEGS):
            r0, n, s, t0, pc0 = seg
            xt, ss, hT = xts[si % 2], sss[si % 2], hTs[si % 2]
            self.norm_transpose(seg, xt, h, junk, ss, hT, bank=[0, 1])
            nsub = (n + 127) // 128
            pj = min(128, n)
            if t0 == 0:
                if s < 2:
                    self.memset("pool", halo[:], 0.0, [halo])
                else:
                    for tt_ in range(3):
                        self.dma("sp", halo[:, :, tt_], self.din["st_conv"][i, tt_].rearrange("(m p) -> p m", p=128), w=[halo],
                                 allow_slow_non_contiguous=True)
            last = (t0 + n == SEQLEN[s])
            chunks = [("qkv", m, m * 128) for m in range(24)] + [("z", m, 3072 + m * 128) for m in range(8)] + \
                     [("u", m, 4112 + m * 128) for m in range(4)]
            for ci, (kind, m, col) in enumerate(chunks):
                bk = 2 + ci % 4
                ps = self.psum(bk)
                for c in range(8):
                    self.mm(ps[:, 0:n], wbf[:, c, col:col + 128], hT[:, c, 0:n], c == 0, c == 7, [wbf, hT], [ps])
                ob = obs[ob_i % 3]
                ob_i += 1
                if kind == "qkv":
                    pc, acc, sv, sq, rs = pcs[ci % 2], accs[ci % 2], svs[ci % 2], sqs[ci % 2], rss[ci % 2]
                    self.cp("act", pc[:, 3:3 + n], ps[:, 0:n], [ps], [pc])
                    self.cp("pool", pc[:, 0:3], halo[:, m, :], [halo], [pc])
                    if last:
                        if s < 2:
                            dst = self.dout["conv_p"][i, s, :, m * 128:(m + 1) * 128]
                        else:
                            dst = self.dout["conv_s"][i, :, m * 128:(m + 1) * 128]
                        self.dma("pool", dst.rearrange("t c -> c t"), pc[:, n:n + 3], r=[pc], allow_slow_non_contiguous=True)
                    else:
                        self.cp("pool", halo[:, m, :], pc[:, n:n + 3], [pc], [halo])
                    self.ts("dve", acc[:, 0:n], pc[:, 3:3 + n], cw[:, m, 3:4], ALU.mult, [pc, cw], [acc])
                    for j in (2, 1, 0):
                        self.stt(acc[:, 0:n], pc[:, j:j + n], cw[:, m, j:j + 1], acc[:, 0:n], ALU.mult, ALU.add, [pc, cw, acc], [acc])
                    if m >= 16:
                        self.act(ob[:, 0:n], acc[:, 0:n], AF.Silu, [acc], [ob])
                        dstA = S["VT"]
                    else:
                        self.act(sv[:, 0:n], acc[:, 0:n], AF.Silu, [acc], [sv])
                        self.tt("pool", sq[:, 0:n], sv[:, 0:n], sv[:, 0:n], ALU.mult, [sv], [sq])
                        p2 = self.psum(6 + ci % 2)
                        self.mm(p2[:, 0:n], self.ones_bf[:], sq[:, 0:n], True, True, [self.ones_bf, sq], [p2])
                        self.act(rs[:, 0:n], p2[:, 0:n], AF.Sqrt, [p2, eps6], [rs], bias=eps6[:])
                        self.op("dve", lambda e, rs=rs, n=n: e.reciprocal(out=rs[:, 0:n], in_=rs[:, 0:n]), [rs], [rs])
                        sc = (128.0 ** -0.5) if m < 8 else 1.0
                        self.stt(ob[:, 0:n], sv[:, 0:n], sc, rs[:, 0:n], ALU.mult, ALU.mult, [sv, rs], [ob])
                        dstA = S["QT"] if m < 8 else S["KT"]
                    mm_ = m % 8
                elif kind == "z":
                    self.cp("act", ob[:, 0:n], ps[:, 0:n], [ps], [ob])
                    dstA = S["ZT"]
                    mm_ = m
                else:
                    self.cp("act", ob[:, 0:n], ps[:, 0:n], [ps], [ob])
                    dstA = S["UT"]
                    mm_ = m
                self.dma("sp", dstA[mm_ * 128:(mm_ + 1) * 128, pc0:pc0 + n], ob[:, 0:n], r=[ob])
            for j in range(nsub):
                tk, gc = tks[j % 2], gcs[j % 2]
                ps = self.psum(6 + j % 2)
                for c in range(8):
                    self.mm(ps[0:pj, 0:16], hT[:, c, j * 128:j * 128 + pj], wbf[:, c, 4096:4112], c == 0, c == 7, [hT, wbf], [ps])
                self.act(tk[0:pj, 0:8], ps[0:pj, 0:8], AF.Sigmoid, [ps], [tk])
                self.dma("pool", S["Btok"][pc0 + j * 128:pc0 + j * 128 + pj, :], tk[0:pj, 0:8], r=[tk])
                self.tt("dve", tk[0:pj, 8:16], ps[0:pj, 8:16], dtb[0:pj, :], ALU.add, [ps, dtb], [tk])
                self.act(tk[0:pj, 8:16], tk[0:pj, 8:16], AF.Exp, [tk], [tk])
                self.act(tk[0:pj, 8:16], tk[0:pj, 8:16], AF.Ln, [tk, self.onec], [tk], bias=self.onec[0:pj, :])
                self.tt("dve", tk[0:pj, 8:16], tk[0:pj, 8:16], nea[0:pj, :], ALU.mult, [tk, nea], [tk])
                pm = max(pj, 64)
                ps3 = self.psum(2 + j % 2)
                self.mm(ps3[0:pm, 0:8], self.tri[0:pj, 0:pm], tk[0:pj, 8:16], True, True, [self.tri, tk], [ps3])
                self.cp("dve", gc[0:pm, :], ps3[0:pm, 0:8], [ps3], [gc])
                self.dma("pool", S["GCtok"][pc0 + j * 128:pc0 + j * 128 + pm, :], gc[0:pm, :], r=[gc])
                ps4 = self.psum(4 + j % 2)
                self.tr(ps4[0:8, 0:pm], gc[0:pm, :], self.identf[0:pm, 0:pm], [gc, self.identf], [ps4])
                self.cp("act", gct[0:8, j * 128:j * 128 + pm], ps4[0:8, 0:pm], [ps4], [gct])
            self.dma("pool", S["GCT"][:, pc0:pc0 + max(n, 64)], gct[0:8, 0:max(n, 64)], r=[gct])
        self.P.barrier()


    def phaseB_gdn(self, l):
        i = l // 2
        self.reset_arena()
        S = self.scr
        NEG = -1.0e5
        sb = self.sb
        epsc = sb([1], F32)
        self.memset("dve", epsc[:], RMS_EPS, [epsc])
        mSL = sb([128], F32)
        mUI = sb([128], F32)
        self.memset("pool", mSL[:], 0.0, [mSL])
        self.op("pool", lambda e: e.affine_select(out=mSL[:], in_=mSL[:], pattern=[[-1, 128]], compare_op=ALU.is_gt,
                                                  fill=NEG, base=0, channel_multiplier=1), [mSL], [mSL])
        self.memset("pool", mUI[:], 0.0, [mUI])
        self.op("pool", lambda e: e.affine_select(out=mUI[:], in_=mUI[:], pattern=[[1, 128]], compare_op=ALU.is_ge,
                                                  fill=NEG, base=0, channel_multiplier=-1), [mUI], [mUI])
        for m_ in (mSL, mUI):
            self.memset("pool", m_[64:128, 0:64], NEG, [m_])
            self.memset("pool", m_[0:64, 64:128], NEG, [m_])
        gain = sb([1], F32)
        self.dma("sp", gain[:], self.din["gdn_out_norm"][i].rearrange("(p o) -> p o", o=1), w=[gain])

        def T3(dt):
            return sb([8, 128], dt)
        ld = [dict(kT=T3(BF16), qT=T3(BF16), vT=T3(BF16), zT=T3(BF16), gcrow=T3(F32), btok=sb([8], F32), gctok=sb([8], F32))
              for _ in range(2)]
        diff, e1, e2 = T3(F32), T3(F32), T3(F32)
        A, B = T3(F32), T3(F32)
        Xs, Ys, Ps, Pts = [T3(F32), T3(F32)], [T3(F32), T3(F32)], [T3(F32), T3(F32)], [T3(F32), T3(F32)]
        Tt, Kb, Kd, Vb, QKm = T3(BF16), T3(BF16), T3(BF16), T3(BF16), T3(BF16)
        nWTa, nWTb, vn, qsa, qsb = T3(BF16), T3(BF16), T3(BF16), T3(BF16), T3(BF16)
        EG, o, sq = T3(F32), T3(F32), T3(F32)
        on, gsz, mixed = T3(BF16), T3(BF16), T3(BF16)
        Sa, Sb, Sabf, Sbbf = T3(F32), T3(F32), T3(BF16), T3(BF16)
        glast, sc1, sc2, eglA, eglB, ssq = (sb([8], F32) for _ in range(6))
        for t_ in (nWTa, nWTb, qsa, qsb):
            self.memset("pool", t_[:], 0.0, [t_])

        def bcl(ap):
            return ap.unsqueeze(2).to_broadcast([128, 8, 128])

        def bcm(ap):
            return ap.unsqueeze(1).to_broadcast([128, 8, 128])

        def v4(t, g):
            return t.ap.rearrange("p (a b) -> p a b", a=4)

        def v8(t):
            return t.ap.rearrange("p (a b) -> p a b", a=8)

        KTv = S["KT"].rearrange("(h p) t -> p h t", p=128)
        QTv = S["QT"].rearrange("(h p) t -> p h t", p=128)
        VTv = S["VT"].rearrange("(h p) t -> p h t", p=128)
        ZTv = S["ZT"].rearrange("(h p) t -> p h t", p=128)
        MXv = S["MIXT"].rearrange("(h p) t -> p h t", p=128)

        def init_state(sample):
            for (F, Fb) in ((Sa, Sabf), (Sb, Sbbf)):
                self.memset("pool", F[:], 0.0, [F])
                self.memset("pool", Fb[:], 0.0, [Fb])
            if sample:
                self.dma("sp", Sa[:], self.din["st_delta"][i].rearrange("h k v -> k h v"), w=[Sa])
                self.cp("act", Sabf[:], Sa[:], [Sa], [Sabf])

        packs = [(False, PCOL[0] + 64 * c, PCOL[1] + 64 * c) for c in range(65)] + [(True, PCOL[2], PCOL[3])]
        for pi, (sample, ca, cb) in enumerate(packs):
            if pi == 0 or sample:
                init_state(sample)
            L = ld[pi % 2]
            kT, qT, vT, zT, gcrow, btok, gctok = (L[k] for k in ("kT", "qT", "vT", "zT", "gcrow", "btok", "gctok"))
            for slot, col in ((0, ca), (1, cb)):
                fs = slice(64 * slot, 64 * slot + 64)
                self.dma("sp", kT[:, :, fs], KTv[:, :, col:col + 64], w=[kT])
                self.dma("sp", qT[:, :, fs], QTv[:, :, col:col + 64], w=[qT])
                self.dma("sp", vT[:, :, fs], VTv[:, :, col:col + 64], w=[vT])
                self.dma("pool", zT[:, :, fs], ZTv[:, :, col:col + 64], w=[zT])
                self.dma("pool", gcrow[:, :, fs], S["GCT"][:, col:col + 64].partition_broadcast(128), w=[gcrow])
                self.dma("pool", btok[fs, :], S["Btok"][col:col + 64, :], w=[btok])
                self.dma("pool", gctok[fs, :], S["GCtok"][col:col + 64, :], w=[gctok])
            self.tt("dve", diff[:], bcl(gctok[:]), gcrow[:], ALU.subtract, [gctok, gcrow], [diff])
            self.tt("pool", e1[:], diff[:], bcm(mSL[:]), ALU.add, [diff, mSL], [e1])
            self.tt("dve", e2[:], bcm(mUI[:]), diff[:], ALU.subtract, [diff, mUI], [e2])
            self.act(e1[:], e1[:], AF.Exp, [e1], [e1])
            self.act(e2[:], e2[:], AF.Exp, [e2], [e2])
            self.tt("pool", e1[:], e1[:], bcl(btok[:]), ALU.mult, [e1, btok], [e1])
            self.cp("pool", glast[0:64, :], gcrow[0:64, :, 63], [gcrow], [glast])
            self.cp("pool", glast[64:128, :], gcrow[64:128, :, 127], [gcrow], [glast])
            self.act(sc1[:], gctok[:], AF.Exp, [gctok], [sc1])
            self.tt("dve", sc1[:], sc1[:], btok[:], ALU.mult, [sc1, btok], [sc1])
            self.tt("dve", sc2[:], glast[:], gctok[:], ALU.subtract, [glast, gctok], [sc2])
            self.act(sc2[:], sc2[:], AF.Exp, [sc2], [sc2])
            self.act(eglA[:], gcrow[:, :, 63], AF.Exp, [gcrow], [eglA])
            self.act(eglB[:], gcrow[:, :, 127], AF.Exp, [gcrow], [eglB])
            self.act(EG[:], gcrow[:], AF.Exp, [gcrow], [EG])
            for g in range(2):
                pk = self.psum(g)
                pq = self.psum(2 + g)
                for hh in range(4):
                    h = 4 * g + hh
                    self.mm(v4(pk, 0)[:, hh, :], kT[:, h, :], kT[:, h, :], True, True, [kT], [pk])
                    self.mm(v4(pq, 0)[:, hh, :], kT[:, h, :], qT[:, h, :], True, True, [kT, qT], [pq])
                hs = slice(4 * g, 4 * g + 4)
                self.tt("dve", A[:, hs, :], v4(pk, 0), e1[:, hs, :], ALU.mult, [pk, e1], [A])
                self.tt("dve", QKm[:, hs, :], v4(pq, 0), e2[:, hs, :], ALU.mult, [pq, e2], [QKm])
            for g in range(2):
                pb = self.psum(4 + g)
                for hh in range(4):
                    self.tr(v4(pb, 0)[:, hh, :], A[:, 4 * g + hh, :], self.identf[:], [A, self.identf], [pb])
                self.cp("act", B[:, 4 * g:4 * g + 4, :], v4(pb, 0), [pb], [B])
            X, Y = A, B
            P0, Pt0 = Ps[0], Pts[0]
            self.tt("pool", P0[:], bcm(self.identf[:]), A[:], ALU.subtract, [A, self.identf], [P0])
            self.tt("pool", Pt0[:], bcm(self.identf[:]), B[:], ALU.subtract, [B, self.identf], [Pt0])
            Pc, Ptc = P0, Pt0
            for k in range(1, 6):
                lastk = (k == 5)
                Xn, Yn = Xs[k % 2], Ys[k % 2]
                Pn, Ptn = Ps[k % 2], Pts[k % 2]
                for g in range(2):
                    hs = slice(4 * g, 4 * g + 4)
                    if not lastk:
                        px = self.psum(g)
                        for hh in range(4):
                            h = 4 * g + hh
                            self.mm(v4(px, 0)[:, hh, :], Y[:, h, :], X[:, h, :], True, True, [X, Y], [px])
                        self.cp("act", Xn[:, hs, :], v4(px, 0), [px], [Xn])
                    py = self.psum(2 + g)
                    for hh in range(4):
                        h = 4 * g + hh
                        self.mm(v4(py, 0)[:, hh, :], X[:, h, :], Y[:, h, :], True, True, [X, Y], [py])
                    self.cp("dve", Yn[:, hs, :], v4(py, 0), [py], [Yn])
                for g in range(2):
                    hs = slice(4 * g, 4 * g + 4)
                    pt = self.psum(4 + g)
                    for hh in range(4):
                        h = 4 * g + hh
                        self.mm(v4(pt, 0)[:, hh, :], Pc[:, h, :], self.identf[:], True, False, [Pc, self.identf], [pt])
                        self.mm(v4(pt, 0)[:, hh, :], Pc[:, h, :], Yn[:, h, :], False, True, [Pc, Yn], [pt])
                    if lastk:
                        self.cp("act", Tt[:, hs, :], v4(pt, 0), [pt], [Tt])
                    else:
                        self.cp("act", Ptn[:, hs, :], v4(pt, 0), [pt], [Ptn])
                        pp = self.psum(6 + g)
                        for hh in range(4):
                            h = 4 * g + hh
                            self.mm(v4(pp, 0)[:, hh, :], Ptc[:, h, :], self.identf[:], True, False, [Ptc, self.identf], [pp])
                            self.mm(v4(pp, 0)[:, hh, :], Ptc[:, h, :], Xn[:, h, :], False, True, [Ptc, Xn], [pp])
                        self.cp("dve", Pn[:, hs, :], v4(pp, 0), [pp], [Pn])
                X, Y, Pc, Ptc = Xn, Yn, Pn, Ptn
            pK = self.psum(0, BF16)
            pV = self.psum(1, BF16)
            for h in range(8):
                self.tr(v8(pK)[:, h, :], kT[:, h, :], self.ident[:], [kT, self.ident], [pK])
                self.tr(v8(pV)[:, h, :], vT[:, h, :], self.ident[:], [vT, self.ident], [pV])
            self.tt("dve", Kb[:], v8(pK), bcl(sc1[:]), ALU.mult, [pK, sc1], [Kb])
            self.tt("dve", Kd[:], v8(pK), bcl(sc2[:]), ALU.mult, [pK, sc2], [Kd])
            self.tt("dve", Vb[:], v8(pV), bcl(btok[:]), ALU.mult, [pV, btok], [Vb])
            for g in range(2):
                pw = self.psum(2 + g)
                for hh in range(4):
                    h = 4 * g + hh
                    self.mm(v4(pw, 0)[:, hh, :], Kb[:, h, :], Tt[:, h, :], True, True, [Kb, Tt], [pw])
                hs = slice(4 * g, 4 * g + 4)
                self.ts("dve", nWTa[:, hs, 0:64], v4(pw, 0)[:, :, 0:64], -1.0, ALU.mult, [pw], [nWTa])
                self.op("act", lambda e, o_=nWTb[:, hs, 64:128], i_=v4(pw, 0)[:, :, 64:128]: e.mul(out=o_, in_=i_, mul=-1.0)
                        if False else e.activation(out=o_, in_=i_, func=AF.Copy, scale=-1.0), [pw], [nWTb])
            for g in range(2):
                pv = self.psum(4 + g)
                for hh in range(4):
                    h = 4 * g + hh
                    self.mm(v4(pv, 0)[:, hh, :], Tt[:, h, :], Vb[:, h, :], True, False, [Tt, Vb], [pv])
                    self.mm(v4(pv, 0)[:, hh, :], nWTa[:, h, :], Sabf[:, h, :], False, False, [nWTa, Sabf], [pv])
                    self.mm(v4(pv, 0)[:, hh, :], nWTb[:, h, :], Sbbf[:, h, :], False, True, [nWTb, Sbbf], [pv])
                self.cp("act", vn[:, 4 * g:4 * g + 4, :], v4(pv, 0), [pv], [vn])
            self.tt("dve", qsa[:, :, 0:64], qT[:, :, 0:64], EG[:, :, 0:64], ALU.mult, [qT, EG], [qsa])
            self.tt("pool", qsb[:, :, 64:128], qT[:, :, 64:128], EG[:, :, 64:128], ALU.mult, [qT, EG], [qsb])
            for g in range(2):
                po = self.psum(6 + g)
                for hh in range(4):
                    h = 4 * g + hh
                    self.mm(v4(po, 0)[:, hh, :], QKm[:, h, :], vn[:, h, :], True, False, [QKm, vn], [po])
                    self.mm(v4(po, 0)[:, hh, :], qsa[:, h, :], Sabf[:, h, :], False, False, [qsa, Sabf], [po])
                    self.mm(v4(po, 0)[:, hh, :], qsb[:, h, :], Sbbf[:, h, :], False, True, [qsb, Sbbf], [po])
                self.cp("act", o[:, 4 * g:4 * g + 4, :], v4(po, 0), [po], [o])
            for (F, Fb, egl, rows, bk) in ((Sa, Sabf, eglA, slice(0, 64), 0), (Sb, Sbbf, eglB, slice(64, 128), 2)):
                self.tt("pool", F[:], F[:], bcl(egl[:]), ALU.mult, [F, egl], [F])
                for g in range(2):
                    pss = self.psum(bk + g)
                    for hh in range(4):
                        h = 4 * g + hh
                        self.mm(v4(pss, 0)[:, hh, :], Kd[rows, h, :], vn[rows, h, :], True, True, [Kd, vn], [pss])
                    hs = slice(4 * g, 4 * g + 4)
                    self.tt("dve", F[:, hs, :], F[:, hs, :], v4(pss, 0), ALU.add, [F, pss], [F])
                self.cp("act", Fb[:], F[:], [F], [Fb])
            self.tt("pool", sq[:], o[:], o[:], ALU.mult, [o], [sq])
            self.op("dve", lambda e: e.tensor_reduce(out=ssq[:], in_=sq[:], axis=AX.X, op=ALU.add), [sq], [ssq])
            self.act(ssq[:], ssq[:], AF.Sqrt, [ssq, epsc], [ssq], bias=epsc[:], scale=1.0 / 128)
            self.op("dve", lambda e: e.reciprocal(out=ssq[:], in_=ssq[:]), [ssq], [ssq])
            self.tt("dve", on[:], o[:], bcl(ssq[:]), ALU.mult, [o, ssq], [on])
            pT = self.psum(4, BF16)
            for h in range(8):
                self.tr(v8(pT)[:, h, :], on[:, h, :], self.ident[:], [on, self.ident], [pT])
            self.act(gsz[:], zT[:], AF.Silu, [zT], [gsz])
            self.stt(mixed[:], v8(pT), gain[:, 0:1], gsz[:], ALU.mult, ALU.mult, [pT, gain, gsz], [mixed])
            for slot, col in ((0, ca), (1, cb)):
                if sample and slot == 1:
                    continue
                self.dma("sp", MXv[:, 0:8, col:col + 64], mixed[:, :, 64 * slot:64 * slot + 64], r=[mixed])
            if pi == 64:
                self.dma("sp", self.dout["delta_p"][i, 0].rearrange("h k v -> k h v"), Sa[:], r=[Sa])
                self.dma("sp", self.dout["delta_p"][i, 1].rearrange("h k v -> k h v"), Sb[:], r=[Sb])
            if sample:
                self.dma("sp", self.dout["delta_s"][i].rearrange("h k v -> k h v"), Sa[:], r=[Sa])
        self.P.barrier()


    def phaseB_s5(self, l):
        i = l // 2
        self.reset_arena()
        S = self.scr
        sb = self.sb
        TW = 516
        cosT = sb([16, TW], F32)
        sinT = sb([16, TW], F32)

        def t16():
            return sb([16], F32)
        lr, li, dt, mag, th, kk, tmp, sn, ch, cs, ar, ai, den, cr, ci, t2_ = (t16() for _ in range(16))
        D_ = self.din
        self.dma("sp", lr[:], D_["ssm_lambda_re"][i].rearrange("(sc gl) p -> (gl p) sc", gl=2), w=[lr], allow_slow_non_contiguous=True)
        self.dma("sp", li[:], D_["ssm_lambda_im"][i].rearrange("(sc gl) p -> (gl p) sc", gl=2), w=[li], allow_slow_non_contiguous=True)
        ldv = D_["ssm_log_dt"][i].rearrange("(sc gl) -> gl sc", gl=2)
        for gl in range(2):
            self.dma("sp", dt[64 * gl:64 * gl + 64, :], ldv[gl:gl + 1, :].to_broadcast([64, 16]), w=[dt], allow_slow_non_contiguous=True)
        self.act(dt[:], dt[:], AF.Exp, [dt], [dt])
        self.ts("dve", lr[:], lr[:], -1e-4, ALU.min, [lr], [lr])
        self.tt("dve", mag[:], lr[:], dt[:], ALU.mult, [lr, dt], [mag])
        self.act(mag[:], mag[:], AF.Exp, [mag], [mag])
        self.tt("dve", th[:], li[:], dt[:], ALU.mult, [li, dt], [th])
        self.memset("dve", kk[:], 0.0, [kk])
        for n_ in range(10):
            self.ts("dve", tmp[:], th[:], (n_ + 0.5) * 2 * math.pi, ALU.is_gt, [th], [tmp])
            self.tt("dve", kk[:], kk[:], tmp[:], ALU.add, [kk, tmp], [kk])
        self.stt(th[:], kk[:], -2 * math.pi, th[:], ALU.mult, ALU.add, [kk, th], [th])
        self.act(sn[:], th[:], AF.Sin, [th], [sn])
        self.act(ch[:], th[:], AF.Sin, [th], [ch], scale=0.5)
        self.tt("dve", cs[:], ch[:], ch[:], ALU.mult, [ch], [cs])
        self.ts("dve", cs[:], cs[:], -2.0, ALU.mult, [cs], [cs], s2=1.0, op1=ALU.add)
        self.tt("dve", ar[:], mag[:], cs[:], ALU.mult, [mag, cs], [ar])
        self.tt("dve", ai[:], mag[:], sn[:], ALU.mult, [mag, sn], [ai])
        self.tt("dve", den[:], lr[:], lr[:], ALU.mult, [lr], [den])
        self.tt("dve", tmp[:], li[:], li[:], ALU.mult, [li], [tmp])
        self.tt("dve", den[:], den[:], tmp[:], ALU.add, [den, tmp], [den])
        self.op("dve", lambda e: e.reciprocal(out=den[:], in_=den[:]), [den], [den])
        nr = t2_
        self.ts("dve", nr[:], ar[:], -1.0, ALU.add, [ar], [nr])
        self.tt("dve", cr[:], nr[:], lr[:], ALU.mult, [nr, lr], [cr])
        self.tt("dve", tmp[:], ai[:], li[:], ALU.mult, [ai, li], [tmp])
        self.tt("dve", cr[:], cr[:], tmp[:], ALU.add, [cr, tmp], [cr])
        self.tt("dve", cr[:], cr[:], den[:], ALU.mult, [cr, den], [cr])
        self.tt("dve", ci[:], ai[:], lr[:], ALU.mult, [ai, lr], [ci])
        self.tt("dve", tmp[:], nr[:], li[:], ALU.mult, [nr, li], [tmp])
        self.tt("dve", ci[:], ci[:], tmp[:], ALU.subtract, [ci, tmp], [ci])
        self.tt("dve", ci[:], ci[:], den[:], ALU.mult, [ci, den], [ci])
        cm, sm, c2, s2 = t16(), t16(), t16(), t16()
        self.cp("dve", cm[:], cs[:], [cs], [cm])
        self.cp("dve", sm[:], sn[:], [sn], [sm])
        self.memset("dve", cosT[:, :, 0:1], 1.0, [cosT])
        self.memset("dve", sinT[:, :, 0:1], 0.0, [sinT])
        mark_ = self.aoff
        tA = sb([16, 512], F32)
        tB = sb([16, 512], F32)
        m = 1
        while m <= 512:
            w_ = min(m, 513 - m)

            def bc(ap, w_=w_):
                return ap.unsqueeze(2).to_broadcast([128, 16, w_])
            self.tt("dve", tA[:, :, 0:w_], cosT[:, :, 0:w_], bc(cm[:]), ALU.mult, [cosT, cm], [tA])
            self.tt("pool", tB[:, :, 0:w_], sinT[:, :, 0:w_], bc(sm[:]), ALU.mult, [sinT, sm], [tB])
            self.tt("dve", cosT[:, :, m:m + w_], tA[:, :, 0:w_], tB[:, :, 0:w_], ALU.subtract, [tA, tB], [cosT])
            self.tt("dve", tA[:, :, 0:w_], sinT[:, :, 0:w_], bc(cm[:]), ALU.mult, [sinT, cm], [tA])
            self.tt("pool", tB[:, :, 0:w_], cosT[:, :, 0:w_], bc(sm[:]), ALU.mult, [cosT, sm], [tB])
            self.tt("dve", sinT[:, :, m:m + w_], tA[:, :, 0:w_], tB[:, :, 0:w_], ALU.add, [tA, tB], [sinT])
            self.tt("dve", c2[:], cm[:], cm[:], ALU.mult, [cm], [c2])
            self.tt("dve", s2[:], sm[:], sm[:], ALU.mult, [sm], [s2])
            self.tt("dve", s2[:], c2[:], s2[:], ALU.subtract, [c2, s2], [s2])
            self.tt("dve", c2[:], cm[:], sm[:], ALU.mult, [cm, sm], [c2])
            self.ts("dve", sm[:], c2[:], 2.0, ALU.mult, [c2], [sm])
            self.cp("dve", cm[:], s2[:], [s2], [cm])
            m *= 2
        self.P.barrier()
        self.aoff = mark_
        bre = sb([16, 16], F32)
        bim = sb([16, 16], F32)
        self.dma("sp", bre[:], D_["ssm_b_re"][i].rearrange("(sc gl) p m -> (gl p) sc m", gl=2), w=[bre])
        self.dma("sp", bim[:], D_["ssm_b_im"][i].rearrange("(sc gl) p m -> (gl p) sc m", gl=2), w=[bim])
        bbr = sb([16, 16], F32)
        bbi = sb([16, 16], F32)
        tC = sb([16, 16], F32)

        def bcB(ap):
            return ap.unsqueeze(2).to_broadcast([128, 16, 16])
        self.tt("dve", bbr[:], bre[:], bcB(cr[:]), ALU.mult, [bre, cr], [bbr])
        self.tt("dve", tC[:], bim[:], bcB(ci[:]), ALU.mult, [bim, ci], [tC])
        self.tt("dve", bbr[:], bbr[:], tC[:], ALU.subtract, [bbr, tC], [bbr])
        self.tt("dve", bbi[:], bim[:], bcB(cr[:]), ALU.mult, [bim, cr], [bbi])
        self.tt("dve", tC[:], bre[:], bcB(ci[:]), ALU.mult, [bre, ci], [tC])
        self.tt("dve", bbi[:], bbi[:], tC[:], ALU.add, [bbi, tC], [bbi])
        blk = sb([16, 128], BF16)
        BTr = sb([16, 128], BF16)
        BTi = sb([16, 128], BF16)
        for (bb, BT) in ((bbr, BTr), (bbi, BTi)):
            self.memset("pool", blk[:], 0.0, [blk])
            for r in range(4):
                self.cp("dve", blk[0:64, r::4, r * 32:r * 32 + 16], bb[0:64, r::4, :], [bb], [blk])
                self.cp("dve", blk[64:128, r::4, r * 32 + 16:r * 32 + 32], bb[64:128, r::4, :], [bb], [blk])
            for g in range(2):
                pt = self.psum(g, BF16)
                pv = pt.ap.rearrange("p (a b) -> p a b", a=8)
                for j in range(8):
                    self.tr(pv[:, j, :], blk[:, 8 * g + j, :], self.ident[:], [blk, self.ident], [pt])
                self.cp("act", BT[:, 8 * g:8 * g + 8, :], pv, [pt], [BT])
        evm = sb([1], F32)
        odm = sb([1], F32)
        self.memset("pool", evm[:], 0.0, [evm])
        self.memset("pool", odm[:], 1.0, [odm])
        for p0 in (0, 32, 64, 96):
            self.memset("pool", evm[p0:p0 + 16, :], 1.0, [evm])
            self.memset("pool", odm[p0:p0 + 16, :], 0.0, [odm])
        Cl = sb([4, 64], F32)
        Cin = sb([4, 128], BF16)
        CT = sb([4, 128], BF16)
        CTr = sb([16, 128], BF16)
        CTi = sb([16, 128], BF16)
        for (name, CTm, sgn) in (("ssm_c_re", CTr, 1.0), ("ssm_c_im", CTi, -1.0)):
            self.dma("sp", Cl[:], D_[name][i].rearrange("(q g) m p -> (g m) q p", q=4), w=[Cl])
            self.ts("dve", Cin[:, :, 0:64], Cl[:], evm[:, 0:1], ALU.mult, [Cl, evm], [Cin], s2=sgn, op1=ALU.mult)
            self.ts("dve", Cin[:, :, 64:128], Cl[:], odm[:, 0:1], ALU.mult, [Cl, odm], [Cin], s2=sgn, op1=ALU.mult)
            pt = self.psum(2, BF16)
            pv = pt.ap.rearrange("p (a b) -> p a b", a=8)
            for q in range(4):
                self.tr(pv[:, q, :], Cin[:, q, :], self.ident[:], [Cin, self.ident], [pt])
            self.cp("act", CT[:], pv[:, 0:4, :], [pt], [CT])
            self.memset("pool", CTm[:], 0.0, [CTm])
            for r in range(4):
                self.cp("dve", CTm[:, r::4, r * 32:r * 32 + 32], CT[:, :, r * 32:r * 32 + 32], [CT], [CTm])
        dsk = sb([4], F32)
        self.dma("sp", dsk[:], D_["ssm_d"][i].rearrange("(q p) -> p q", p=128), w=[dsk], allow_slow_non_contiguous=True)
        glb = sb([4], F32)
        self.dma("sp", glb[:], D_["ssm_glu_b"][i].rearrange("(q p) -> p q", p=128), w=[glb], allow_slow_non_contiguous=True)
        glw = sb([4, 512], BF16)
        self.load_weight(D_["ssm_glu_w"][i], 512, 512, glw, None, stage_cols=512)
        uTs = [sb([4, 512], BF16), sb([4, 512], BF16)]
        t1s = [sb([512], F32) for _ in range(2)]
        t2s = [sb([512], F32) for _ in range(2)]
        t3s = [sb([512], F32) for _ in range(2)]
        t4s = [sb([512], F32) for _ in range(2)]
        wrs = [sb([512], F32) for _ in range(2)]
        wis = [sb([512], F32) for _ in range(2)]
        xrs = [sb([512], BF16) for _ in range(2)]
        xis = [sb([512], BF16) for _ in range(2)]
        yv = sb([512], F32)
        g1 = sb([512], F32)
        hg = sb([4, 512], BF16)
        sg = sb([512], F32)
        obs = [sb([512], BF16) for _ in range(2)]
        wlr, wli, inr, ini, t16a, t16b = (t16() for _ in range(6))
        UTv = S["UT"].rearrange("(q p) t -> p q t", p=128)
        it = 0
        for si, seg in enumerate(SEGS):
            r0, n, s, t0, pc0 = seg
            first = (t0 == 0)
            lastseg = (t0 + n == SEQLEN[s])
            uT = uTs[si % 2]
            self.dma("sp", uT[:, :, 0:n], UTv[:, :, pc0:pc0 + n], w=[uT])
            if first:
                if s < 2:
                    self.memset("dve", inr[:], 0.0, [inr])
                    self.memset("dve", ini[:], 0.0, [ini])
                else:
                    self.dma("sp", wlr[:], D_["st_re"][i].rearrange("(sc gl) p -> (gl p) sc", gl=2), w=[wlr], allow_slow_non_contiguous=True)
                    self.dma("sp", wli[:], D_["st_im"][i].rearrange("(sc gl) p -> (gl p) sc", gl=2), w=[wli], allow_slow_non_contiguous=True)
                    jprev = 1
            if not first or s == 2:
                jp = jprev if (first and s == 2) else nprev
                ec, es = cosT[:, :, jp], sinT[:, :, jp]
                self.tt("dve", t16a[:], wlr[:], ec, ALU.mult, [wlr, cosT], [t16a])
                self.tt("dve", t16b[:], wli[:], es, ALU.mult, [wli, sinT], [t16b])
                self.tt("dve", inr[:], t16a[:], t16b[:], ALU.subtract, [t16a, t16b], [inr])
                self.tt("dve", t16a[:], wli[:], ec, ALU.mult, [wli, cosT], [t16a])
                self.tt("dve", t16b[:], wlr[:], es, ALU.mult, [wlr, sinT], [t16b])
                self.tt("dve", ini[:], t16a[:], t16b[:], ALU.add, [t16a, t16b], [ini])
            nprev = n
            for q in range(4):
                py = self.psum(4 + q % 2)
                for sl in range(4):
                    sc = 4 * q + sl
                    k2 = it % 2
                    it += 1
                    t1, t2, t3, t4, wr, wi, xr, xi = t1s[k2], t2s[k2], t3s[k2], t4s[k2], wrs[k2], wis[k2], xrs[k2], xis[k2]
                    pbr = self.psum(0 + k2)
                    pbi = self.psum(2 + k2)
                    self.mm(pbr[:, 0:n], BTr[:, sc, :], uT[:, q, 0:n], True, True, [BTr, uT], [pbr])
                    self.mm(pbi[:, 0:n], BTi[:, sc, :], uT[:, q, 0:n], True, True, [BTi, uT], [pbi])
                    cT, sT = cosT[:, sc, 0:n], sinT[:, sc, 0:n]
                    self.tt("dve", t1[:, 0:n], pbr[:, 0:n], cT, ALU.mult, [pbr, cosT], [t1])
                    self.tt("dve", t2[:, 0:n], pbi[:, 0:n], sT, ALU.mult, [pbi, sinT], [t2])
                    self.tt("dve", t3[:, 0:n], pbi[:, 0:n], cT, ALU.mult, [pbi, cosT], [t3])
                    self.tt("dve", t4[:, 0:n], pbr[:, 0:n], sT, ALU.mult, [pbr, sinT], [t4])
                    self.tt("pool", t1[:, 0:n], t1[:, 0:n], t2[:, 0:n], ALU.add, [t1, t2], [t1])
                    self.tt("pool", t3[:, 0:n], t3[:, 0:n], t4[:, 0:n], ALU.subtract, [t3, t4], [t3])
                    mg = mag[:, sc:sc + 1].to_broadcast([128, n])
                    self.op("dve", lambda e, o_=wr[:, 0:n], d0=mg, d1=t1[:, 0:n], in_=inr[:, sc:sc + 1]:
                            e.tensor_tensor_scan(out=o_, data0=d0, data1=d1, initial=in_, op0=ALU.mult, op1=ALU.add),
                            [t1, mag, inr], [wr])
                    self.op("dve", lambda e, o_=wi[:, 0:n], d0=mg, d1=t3[:, 0:n], in_=ini[:, sc:sc + 1]:
                            e.tensor_tensor_scan(out=o_, data0=d0, data1=d1, initial=in_, op0=ALU.mult, op1=ALU.add),
                            [t3, mag, ini], [wi])
                    self.cp("pool", wlr[:, sc:sc + 1], wr[:, n - 1:n], [wr], [wlr])
                    self.cp("pool", wli[:, sc:sc + 1], wi[:, n - 1:n], [wi], [wli])
                    self.tt("pool", t2[:, 0:n], wr[:, 0:n], cT, ALU.mult, [wr, cosT], [t2])
                    self.tt("pool", t4[:, 0:n], wi[:, 0:n], sT, ALU.mult, [wi, sinT], [t4])
                    self.tt("dve", xr[:, 0:n], t2[:, 0:n], t4[:, 0:n], ALU.subtract, [t2, t4], [xr])
                    self.tt("pool", t2[:, 0:n], wi[:, 0:n], cT, ALU.mult, [wi, cosT], [t2])
                    self.tt("pool", t4[:, 0:n], wr[:, 0:n], sT, ALU.mult, [wr, sinT], [t4])
                    self.tt("dve", xi[:, 0:n], t2[:, 0:n], t4[:, 0:n], ALU.add, [t2, t4], [xi])
                    self.mm(py[:, 0:n], CTr[:, sc, :], xr[:, 0:n], sl == 0, False, [CTr, xr], [py])
                    self.mm(py[:, 0:n], CTi[:, sc, :], xi[:, 0:n], False, sl == 3, [CTi, xi], [py])
                self.stt(yv[:, 0:n], uT[:, q, 0:n], dsk[:, q:q + 1], py[:, 0:n], ALU.mult, ALU.add, [uT, dsk, py], [yv])
                self.tt("pool", g1[:, 0:n], yv[:, 0:n], yv[:, 0:n], ALU.mult, [yv], [g1])
                self.ts("dve", g1[:, 0:n], g1[:, 0:n], 0.044715, ALU.mult, [g1], [g1], s2=1.0, op1=ALU.add)
                self.tt("pool", g1[:, 0:n], g1[:, 0:n], yv[:, 0:n], ALU.mult, [g1, yv], [g1])
                self.act(g1[:, 0:n], g1[:, 0:n], AF.Sigmoid, [g1], [g1], scale=2.0 * math.sqrt(2.0 / math.pi))
                self.tt("dve", hg[:, q, 0:n], yv[:, 0:n], g1[:, 0:n], ALU.mult, [yv, g1], [hg])
            for qo in range(4):
                pg = self.psum(6 + qo % 2)
                for qi in range(4):
                    self.mm(pg[:, 0:n], glw[:, qi, qo * 128:(qo + 1) * 128], hg[:, qi, 0:n], qi == 0, qi == 3, [glw, hg], [pg])
                self.act(sg[:, 0:n], pg[:, 0:n], AF.Sigmoid, [pg, glb], [sg], bias=glb[:, qo:qo + 1])
                ob = obs[qo % 2]
                self.tt("dve", ob[:, 0:n], hg[:, qo, 0:n], sg[:, 0:n], ALU.mult, [hg, sg], [ob])
                self.dma("sp", S["MIXT"][1024 + qo * 128:1024 + (qo + 1) * 128, pc0:pc0 + n], ob[:, 0:n], r=[ob])
            if lastseg:
                ec, es = cosT[:, :, n - 1], sinT[:, :, n - 1]
                fr, fi = t16(), t16()
                self.tt("dve", t16a[:], wlr[:], ec, ALU.mult, [wlr, cosT], [t16a])
                self.tt("dve", t16b[:], wli[:], es, ALU.mult, [wli, sinT], [t16b])
                self.tt("dve", fr[:], t16a[:], t16b[:], ALU.subtract, [t16a, t16b], [fr])
                self.tt("dve", t16a[:], wli[:], ec, ALU.mult, [wli, cosT], [t16a])
                self.tt("dve", t16b[:], wlr[:], es, ALU.mult, [wlr, sinT], [t16b])
                self.tt("dve", fi[:], t16a[:], t16b[:], ALU.add, [t16a, t16b], [fi])
                if s < 2:
                    dr, di = self.dout["re_p"][i, s], self.dout["im_p"][i, s]
                else:
                    dr, di = self.dout["re_s"][i], self.dout["im_s"][i]
                self.dma("sp", dr.rearrange("(sc gl) p -> (gl p) sc", gl=2), fr[:], r=[fr], allow_slow_non_contiguous=True)
                self.dma("sp", di.rearrange("(sc gl) p -> (gl p) sc", gl=2), fi[:], r=[fi], allow_slow_non_contiguous=True)
        self.P.barrier()


    def phaseC_outproj(self, l, wdram, KC):
        self.reset_arena()
        sb = self.sb
        wo = sb([KC, D], BF16)
        self.load_weight(wdram, KC * 128, D, wo, None, stage_cols=1024)
        xts = [sb([4, D], F32), sb([4, D], F32)]
        mts = [sb([KC, 512], BF16), sb([KC, 512], BF16)]
        MXv = self.scr["MIXT"].rearrange("(c p) t -> p c t", p=128)
        k = 0
        for si, seg in enumerate(SEGS):
            r0, n, s_, t0, pc0 = seg
            xt, mt = xts[si % 2], mts[si % 2]
            self.load_x(r0, n, xt)
            self.dma("sp", mt[:, :, 0:n], MXv[:, 0:KC, pc0:pc0 + n], w=[mt])
            nsub = (n + 127) // 128
            pj = min(128, n)
            for j in range(nsub):
                for dh in range(2):
                    ps = self.psum(k % 8)
                    k += 1
                    for c in range(KC):
                        self.mm(ps[0:pj, :], mt[:, c, j * 128:j * 128 + pj], wo[:, c, dh * 512:(dh + 1) * 512], c == 0, c == KC - 1,
                                [mt, wo], [ps])
                    self.tt("dve", xt[0:pj, j, dh * 512:(dh + 1) * 512], xt[0:pj, j, dh * 512:(dh + 1) * 512], ps[0:pj, :], ALU.add,
                            [xt, ps], [xt])
            self.store_x(r0, n, xt)
        self.P.barrier()

    def phaseD_ffn(self, l):
        self.reset_arena()
        sb = self.sb
        self.epsc = sb([1], F32)
        self.memset("dve", self.epsc[:], RMS_EPS, [self.epsc])
        w1 = sb([8, FFN], BF16)
        w3 = sb([8, FFN], BF16)
        w2 = sb([22, D], BF16)
        gain = self.load_gain(self.din["norm_ffn"][l])
        mark_ = self.aoff
        self.load_weight(self.din["ffn_w1"][l], D, FFN, w1, gain, stage_cols=1408)
        self.load_weight(self.din["ffn_w3"][l], D, FFN, w3, gain, stage_cols=1408)
        self.load_weight(self.din["ffn_w2"][l], FFN, D, w2, None, stage_cols=1024)
        self.P.barrier()
        self.aoff = mark_
        final = (l == DEPTH - 1)
        if final:
            gfin = sb([D], F32)
            self.dma("sp", gfin[:], self.din["norm_final"].rearrange("(o d) -> o d", o=1).to_broadcast([128, D]), w=[gfin])
        xt = sb([2, D], F32)
        h = sb([2, D], BF16)
        junk = sb([D], BF16)
        ss = sb([4], F32)
        hT = sb([8, 256], BF16)
        aT = sb([22, 256], BF16)
        sgs = [sb([256], F32), sb([256], F32)]
        segs = []
        for (r0, n, s_, t0, pc0) in SEGS:
            for o_ in range(0, n, 256):
                segs.append((r0 + o_, min(256, n - o_), s_, t0 + o_))
        k = 0
        for si, seg in enumerate(segs):
            r0, n, s_, t0 = seg
            nsub = (n + 127) // 128
            pj = min(128, n)
            self.norm_transpose(seg, xt, h, junk, ss, hT, bank=[0, 1])
            for jh in range(22):
                pg = self.psum(2 + (jh % 2))
                pu = self.psum(4 + (jh % 2))
                for c in range(8):
                    self.mm(pg[:, 0:n], w1[:, c, jh * 128:(jh + 1) * 128], hT[:, c, 0:n], c == 0, c == 7, [w1, hT], [pg])
                for c in range(8):
                    self.mm(pu[:, 0:n], w3[:, c, jh * 128:(jh + 1) * 128], hT[:, c, 0:n], c == 0, c == 7, [w3, hT], [pu])
                sg = sgs[jh % 2]
                self.act(sg[:, 0:n], pg[:, 0:n], AF.Silu, [pg], [sg])
                self.tt("dve", aT[:, jh, 0:n], sg[:, 0:n], pu[:, 0:n], ALU.mult, [sg, pu], [aT])
            for j in range(nsub):
                for dh in range(2):
                    ps = self.psum(6 + k % 2)
                    k += 1
                    for jh in range(22):
                        self.mm(ps[0:pj, :], aT[:, jh, j * 128:j * 128 + pj], w2[:, jh, dh * 512:(dh + 1) * 512], jh == 0, jh == 21,
                                [aT, w2], [ps])
                    self.tt("dve", xt[0:pj, j, dh * 512:(dh + 1) * 512], xt[0:pj, j, dh * 512:(dh + 1) * 512], ps[0:pj, :], ALU.add,
                            [xt, ps], [xt])
            if not final:
                self.store_x(r0, n, xt)
            else:
                for j in range(nsub):
                    self.act(junk[0:pj, :], xt[0:pj, j, :], AF.Square, [xt], [junk, ss], accum=ss[0:pj, j:j + 1])
                self.act(ss[0:pj, 0:nsub], ss[0:pj, 0:nsub], AF.Sqrt, [ss], [ss], bias=self.epsc[0:pj, :], scale=1.0 / D)
                self.op("dve", lambda e, pj=pj, nsub=nsub: e.reciprocal(out=ss[0:pj, 0:nsub], in_=ss[0:pj, 0:nsub]), [ss], [ss])
                for j in range(nsub):
                    self.stt(xt[0:pj, j, :], xt[0:pj, j, :], ss[0:pj, j:j + 1], gfin[0:pj, :], ALU.mult, ALU.mult, [xt, ss, gfin], [xt])
                if s_ == 2:
                    self.dma("pool", self.dout["y_s"][t0:t0 + n, :], xt[0:n, 0, :], r=[xt])
                elif t0 >= 16:
                    self.dma("pool", self.dout["y_p"][s_, t0 - 16:t0 - 16 + n, :].rearrange("(j p) d -> p j d", p=128),
                             xt[:, 0:nsub, :], r=[xt])
        self.P.barrier()

    def phaseA_odd(self, l):
        i = l // 2
        self.reset_arena()
        sb = self.sb
        D_ = self.din
        S = self.scr
        self.epsc = sb([1], F32)
        self.memset("dve", self.epsc[:], RMS_EPS, [self.epsc])
        wbf = sb([8, ODD_PROJ], BF16)
        gain = self.load_gain(D_["norm_mix"][l])
        mark_ = self.aoff
        self.load_weight(D_["w_in_odd"][i], D, ODD_PROJ, wbf, gain, stage_cols=1540)
        self.P.barrier()
        self.aoff = mark_
        fb = sb([8], F32)
        self.dma("sp", fb[:], D_["fox_f_bias"][i:i + 1, :].to_broadcast([128, 8]), w=[fb])
        posf = sb([34], F32)
        self.op("pool", lambda e: e.iota(posf[:, 0:32], pattern=[[128, 32]], base=16, channel_multiplier=1,
                                         allow_small_or_imprecise_dtypes=True), (), [posf])
        self.op("pool", lambda e: e.iota(posf[:, 32:33], pattern=[[1, 1]], base=0, channel_multiplier=1,
                                         allow_small_or_imprecise_dtypes=True), (), [posf])
        self.op("pool", lambda e: e.iota(posf[:, 33:34], pattern=[[1, 1]], base=4096, channel_multiplier=1,
                                         allow_small_or_imprecise_dtypes=True), (), [posf])
        invf = sb([8], F32)
        for f_ in range(8):
            self.memset("dve", invf[:, f_:f_ + 1], 500000.0 ** (-f_ / 8.0), [invf])
        ang = sb([34, 8], F32)
        kq = sb([34, 8], F32)
        ki = sb([34, 8], mybir.dt.int32)
        msk = sb([34, 8], F32)
        cosR = sb([34, 8], F32)
        sinR = sb([34, 8], F32)
        self.tt("dve", ang[:], posf[:].unsqueeze(2).to_broadcast([128, 34, 8]), invf[:].unsqueeze(1).to_broadcast([128, 34, 8]),
                ALU.mult, [posf, invf], [ang])
        self.ts("dve", kq[:], ang[:], 1.0 / (2 * math.pi), ALU.mult, [ang], [kq])
        self.cp("dve", ki[:], kq[:], [kq], [ki])
        self.cp("dve", kq[:], ki[:], [ki], [kq])
        self.stt(ang[:], kq[:], -2 * math.pi, ang[:], ALU.mult, ALU.add, [kq, ang], [ang])
        self.ts("dve", msk[:], ang[:], math.pi, ALU.is_gt, [ang], [msk])
        self.stt(ang[:], msk[:], -2 * math.pi, ang[:], ALU.mult, ALU.add, [msk, ang], [ang])
        self.ts("dve", msk[:], ang[:], -math.pi, ALU.is_lt, [ang], [msk])
        self.stt(ang[:], msk[:], 2 * math.pi, ang[:], ALU.mult, ALU.add, [msk, ang], [ang])
        self.act(sinR[:], ang[:], AF.Sin, [ang], [sinR])
        self.act(cosR[:], ang[:], AF.Sin, [ang], [cosR], scale=0.5)
        self.tt("dve", cosR[:], cosR[:], cosR[:], ALU.mult, [cosR], [cosR])
        self.ts("dve", cosR[:], cosR[:], -2.0, ALU.mult, [cosR], [cosR], s2=1.0, op1=ALU.add)
        xts = [sb([4, D], F32), sb([4, D], F32)]
        h = sb([4, D], BF16)
        junk = sb([D], BF16)
        sss = [sb([4], F32), sb([4], F32)]
        hTs = [sb([8, 512], BF16), sb([8, 512], BF16)]
        oks = [sb([512], F32) for _ in range(4)]
        okbs = [sb([512], BF16) for _ in range(4)]
        kts = [sb([4, 128], BF16) for _ in range(4)]
        rts = [sb([8, 8], F32) for _ in range(3)]
        lfs = [sb([8], F32) for _ in range(2)]
        cms = [sb([8], F32) for _ in range(2)]
        carry = sb([8], F32)
        cst = [sb([512], F32) for _ in range(2)]
        cache_names = (("c_fk", "KFT", True), ("c_dk", "KDT", True), ("c_fv", "VF", False), ("c_dv", "VD", False))
        st = dict(ko=0, kb=0, kt=0, cc=0)

        def cum_step(lf, pj, row0):
            cm = cms[st["cc"] % 2]
            ps = self.psum(6 + st["cc"] % 2)
            st["cc"] += 1
            self.mm(ps[0:pj, 0:8], self.trif[0:pj, 0:pj], lf[0:pj, :], True, False, [self.trif, lf], [ps])
            self.mm(ps[0:pj, 0:8], self.identf[0:pj, 0:pj], carry[0:pj, :], False, True, [self.identf, carry], [ps])
            self.mm(ps[:, 8:16], self.onesf[0:pj, :], lf[0:pj, :], True, True, [self.onesf, lf], [ps])
            self.cp("dve", cm[0:pj, :], ps[0:pj, 0:8], [ps], [cm])
            self.dma("pool", S["CUM"][row0:row0 + pj, :], cm[0:pj, :], r=[cm])
            self.tt("dve", carry[:], carry[:], ps[:, 8:16], ALU.add, [carry, ps], [carry])

        def to_featmajor(okb, pj, dst, col0):
            pT = self.psum(st["kt"] % 2, BF16)
            kt = kts[st["kt"] % 4]
            st["kt"] += 1
            pv = pT.ap.rearrange("p (c t) -> p c t", c=8)
            for c in range(4):
                self.tr(pv[:, c, 0:pj], okb[0:pj, c * 128:(c + 1) * 128], self.ident[0:pj, 0:pj], [okb, self.ident], [pT])
            self.cp("dve", kt[:, :, 0:pj], pv[:, 0:4, 0:pj], [pT], [kt])
            self.dma("sp", dst.rearrange("(c p) t -> p c t", p=128)[:, :, col0:col0 + pj], kt[:, :, 0:pj], r=[kt])

        for si, seg in enumerate(SEGS):
            r0, n, s_, t0, pc0 = seg
            if t0 == 0:
                self.memset("dve", carry[:], 0.0, [carry])
            if s_ == 2:
                for m in range(32):
                    for (cn, dn, isk) in cache_names:
                        ok = oks[st["ko"] % 4]
                        okb = okbs[st["ko"] % 4]
                        st["ko"] += 1
                        self.dma("sp", ok[:], D_[cn][i, 128 * m:128 * (m + 1)].rearrange("t a b -> t (a b)"), w=[ok])
                        self.cp("pool", okb[:], ok[:], [ok], [okb])
                        if isk:
                            to_featmajor(okb, 128, S[dn], AK[2] + 128 * m)
                        else:
                            self.dma("pool", S[dn][AK[2] + 128 * m:AK[2] + 128 * (m + 1), :], okb[:], r=[okb])
                    lf = lfs[m % 2]
                    self.dma("sp", lf[:], D_["c_fl"][i, 128 * m:128 * (m + 1), :], w=[lf])
                    cum_step(lf, 128, CB[2] + 1 + 128 * m)
            koff = 4096 if s_ == 2 else 0
            xt, ss, hT = xts[si % 2], sss[si % 2], hTs[si % 2]
            self.norm_transpose(seg, xt, h, junk, ss, hT, bank=[0, 1])
            nsub = (n + 127) // 128
            pj = min(128, n)
            for j in range(nsub):
                tcol = 33 if s_ == 2 else (32 if t0 == 0 else (t0 - 16) // 128 + j)
                tok0 = t0 + 128 * j
                for (name, col, rope, kind) in (("qf", 0, False, "q"), ("fk", 512, False, "k"), ("fv", 1024, False, "v"),
                                                ("qd", 1544, True, "q"), ("dk", 2056, True, "k"), ("dv", 2568, False, "v")):
                    ps = self.psum(2 + st["ko"] % 4)
                    ok = oks[st["ko"] % 4]
                    okb = okbs[st["ko"] % 4]
                    st["ko"] += 1
                    for c in range(8):
                        self.mm(ps[0:pj, :], hT[:, c, j * 128:j * 128 + pj], wbf[:, c, col:col + 512], c == 0, c == 7, [hT, wbf], [ps])
                    self.cp("act", ok[0:pj, :], ps[0:pj, :], [ps], [ok])
                    if rope:
                        okv = ok.ap.rearrange("p (h d) -> p h d", h=8)
                        cb = cosR[0:pj, tcol, :].unsqueeze(1).to_broadcast([pj, 8, 8])
                        sbb = sinR[0:pj, tcol, :].unsqueeze(1).to_broadcast([pj, 8, 8])
                        x1, x2 = okv[0:pj, :, 0:8], okv[0:pj, :, 8:16]
                        rt, r1, r2 = rts
                        self.tt("dve", rt[0:pj], x1, cb, ALU.mult, [ok, cosR], [rt])
                        self.tt("dve", r1[0:pj], x2, sbb, ALU.mult, [ok, sinR], [r1])
                        self.tt("dve", r2[0:pj], x2, cb, ALU.mult, [ok, cosR], [r2])
                        self.tt("dve", x2, x1, sbb, ALU.mult, [ok, sinR], [ok])
                        self.tt("dve", x2, x2, r2[0:pj], ALU.add, [ok, r2], [ok])
                        self.tt("dve", x1, rt[0:pj], r1[0:pj], ALU.subtract, [rt, r1], [ok])
                    if kind != "q":
                        if s_ < 2:
                            dst = self.dout[name + "_p"][i, s_, tok0:tok0 + pj]
                        else:
                            dst = self.dout[name + "_s"][i, tok0:tok0 + pj]
                        self.dma("pool", dst.rearrange("t a b -> t (a b)"), ok[0:pj, :], r=[ok])
                    if kind == "q":
                        self.ts("pool", okb[0:pj, :], ok[0:pj, :], 0.125, ALU.mult, [ok], [okb])
                        to_featmajor(okb, pj, S["QFT" if name == "qf" else "QDT"], SEQROW[s_] + tok0)
                    elif kind == "k":
                        self.cp("pool", okb[0:pj, :], ok[0:pj, :], [ok], [okb])
                        to_featmajor(okb, pj, S["KFT" if name == "fk" else "KDT"], AK[s_] + koff + tok0)
                    else:
                        self.cp("pool", okb[0:pj, :], ok[0:pj, :], [ok], [okb])
                        dn = "VF" if name == "fv" else "VD"
                        self.dma("pool", S[dn][AK[s_] + koff + tok0:AK[s_] + koff + tok0 + pj, :], okb[0:pj, :], r=[okb])
                ps = self.psum(6 + j % 2)
                lf = lfs[j % 2]
                for c in range(8):
                    self.mm(ps[0:pj, 0:8], hT[:, c, j * 128:j * 128 + pj], wbf[:, c, 1536:1544], c == 0, c == 7, [hT, wbf], [ps])
                self.tt("dve", lf[0:pj, :], ps[0:pj, 0:8], fb[0:pj, :], ALU.add, [ps, fb], [lf])
                self.act(lf[0:pj, :], lf[0:pj, :], AF.Exp, [lf], [lf], scale=-1.0)
                self.act(lf[0:pj, :], lf[0:pj, :], AF.Ln, [lf, self.onec], [lf], bias=self.onec[0:pj, :])
                self.ts("dve", lf[0:pj, :], lf[0:pj, :], -1.0, ALU.mult, [lf], [lf])
                if s_ < 2:
                    dst = self.dout["fl_p"][i, s_, tok0:tok0 + pj, :]
                else:
                    dst = self.dout["fl_s"][i, tok0:tok0 + pj, :]
                self.dma("pool", dst, lf[0:pj, :], r=[lf])
                cum_step(lf, pj, CB[s_] + 1 + koff + tok0)
        self.P.barrier()

    def phaseB_attn(self, l):
        i = l // 2
        self.reset_arena()
        sb = self.sb
        D_ = self.din
        S = self.scr
        lam_init = 0.8 - 0.6 * math.exp(-0.3 * l)
        epsc = sb([1], F32)
        self.memset("dve", epsc[:], RMS_EPS, [epsc])
        dl = sb([4, 64], F32)
        self.dma("sp", dl[:], D_["diff_lambda"][i:i + 1].to_broadcast([128, 4, 64]), w=[dl])
        pr = sb([2, 64], F32)
        lsum = sb([2], F32)
        nlam = sb([1], F32)
        self.tt("dve", pr[:, 0, :], dl[:, 0, :], dl[:, 1, :], ALU.mult, [dl], [pr])
        self.tt("dve", pr[:, 1, :], dl[:, 2, :], dl[:, 3, :], ALU.mult, [dl], [pr])
        self.op("dve", lambda e: e.tensor_reduce(out=lsum[:], in_=pr[:], axis=AX.X, op=ALU.add), [pr], [lsum])
        self.act(lsum[:], lsum[:], AF.Exp, [lsum], [lsum])
        self.tt("dve", nlam[:], lsum[:, 1:2], lsum[:, 0:1], ALU.subtract, [lsum], [nlam])
        self.ts("dve", nlam[:], nlam[:], -lam_init, ALU.add, [nlam], [nlam])
        dg = sb([1], F32)
        self.dma("sp", dg[:], D_["diff_out_norm"][i].rearrange("(p o) -> p o", o=1), w=[dg])
        self.ts("dve", dg[:], dg[:], 1.0 - lam_init, ALU.mult, [dg], [dg])
        mfox = [sb([512], BF16) for _ in range(4)]
        mdif = [sb([512], BF16) for _ in range(4)]
        for kt in range(4):
            m_ = mfox[kt]
            self.memset("pool", m_[:], 1.0, [m_])
            self.op("pool", lambda e, m_=m_, kt=kt: e.affine_select(out=m_[:], in_=m_[:], pattern=[[1, 512]], compare_op=ALU.is_ge,
                                                                    fill=0.0, base=-128 * kt, channel_multiplier=-1), [m_], [m_])
            d_ = mdif[kt]
            self.memset("pool", d_[:], 1.0, [d_])
            if kt > 0:
                self.memset("pool", d_[:, 0:128 * kt], 0.0, [d_])
            self.memset("pool", d_[64:128, 128 * kt:128 * kt + 64], 0.0, [d_])
        KTs = [sb([4128], BF16) for _ in range(2)]
        QTs = [sb([4112], BF16) for _ in range(2)]
        Vs = [sb([33, 128], BF16) for _ in range(2)]
        cumT = sb([33, 8], F32)
        crefb = sb([8], F32)
        biasTs = [sb([33, 8], F32) for _ in range(9)]
        pTs = [sb([512], BF16) for _ in range(4)]
        rls = [sb([512], F32) for _ in range(2)]
        o1 = sb([512], F32)
        o2 = sb([512], F32)
        sqb = sb([512], BF16)
        rstd = sb([512], F32)
        mixs = [sb([512], BF16) for _ in range(2)]
        cnt = dict(ld=0, pt=0, mx=0)
        for s_ in range(3):
            nkeys = LP if s_ < 2 else 4096 + LS
            if s_ < 2:
                ktiles = [(0, 16)] + [(16 + 128 * m, 128) for m in range(32)]
            else:
                ktiles = [(128 * m, 128) for m in range(32)] + [(4096, 32)]
            ntile = len(ktiles)
            qsegs = [sg for sg in SEGS if sg[2] == s_]
            for m, (k0, nk) in enumerate(ktiles):
                if nk == 128 and (m == 0 or ktiles[m - 1][1] != 128):
                    m_end = m
                    while m_end < ntile and ktiles[m_end][1] == 128:
                        m_end += 1
                    self.dma("sp", cumT[:, m:m_end, :],
                             S["CUM"][CB[s_] + 1 + k0:CB[s_] + 1 + k0 + 128 * (m_end - m), :].rearrange("(m p) h -> p m h", p=128), w=[cumT])
                elif nk != 128:
                    self.dma("sp", cumT[0:nk, m, :], S["CUM"][CB[s_] + 1 + k0:CB[s_] + 1 + k0 + nk, :], w=[cumT])
            for qi, (r0, nq, _s, t0, pc0) in enumerate(qsegs):
                qkey0 = t0 + (4096 if s_ == 2 else 0)
                qref = qkey0 + nq // 2
                self.dma("sp", crefb[:], S["CUM"][CB[s_] + qref:CB[s_] + qref + 1, :].to_broadcast([128, 8]), w=[crefb])
                self.tt("dve", biasTs[qi][:], crefb[:].unsqueeze(1).to_broadcast([128, 33, 8]), cumT[:], ALU.subtract, [crefb, cumT], [biasTs[qi]])
                self.ts("dve", biasTs[qi][:], biasTs[qi][:], 70.0, ALU.min, [biasTs[qi]], [biasTs[qi]])
            for kind in ("fox", "dif"):
                for c in range(4):
                    KT, QT, V = KTs[cnt["ld"] % 2], QTs[cnt["ld"] % 2], Vs[cnt["ld"] % 2]
                    cnt["ld"] += 1
                    ksrc = S["KFT" if kind == "fox" else "KDT"]
                    qsrc = S["QFT" if kind == "fox" else "QDT"]
                    vsrc = S["VF" if kind == "fox" else "VD"]
                    self.dma("sp", KT[:, 0:nkeys], ksrc[128 * c:128 * (c + 1), AK[s_]:AK[s_] + nkeys], w=[KT])
                    self.dma("sp", QT[:, 0:SEQLEN[s_]], qsrc[128 * c:128 * (c + 1), SEQROW[s_]:SEQROW[s_] + SEQLEN[s_]], w=[QT])
                    for m, (k0, nk) in enumerate(ktiles):
                        if nk == 128 and (m == 0 or ktiles[m - 1][1] != 128):
                            m_end = m
                            while m_end < ntile and ktiles[m_end][1] == 128:
                                m_end += 1
                            self.dma("pool", V[:, m:m_end, :],
                                     vsrc[AK[s_] + k0:AK[s_] + k0 + 128 * (m_end - m), 128 * c:128 * (c + 1)].rearrange("(m p) d -> p m d", p=128), w=[V])
                        elif nk != 128:
                            self.dma("pool", V[0:nk, m, :], vsrc[AK[s_] + k0:AK[s_] + k0 + nk, 128 * c:128 * (c + 1)], w=[V])
                    for qi, (r0, nq, _s, t0, pc0) in enumerate(qsegs):
                        qkey0 = t0 + (4096 if s_ == 2 else 0)
                        tiles = [(m, k0, nk) for m, (k0, nk) in enumerate(ktiles) if k0 < qkey0 + nq]
                        if kind == "fox":
                            accO, accL = self.psum(4), self.psum(5)
                        else:
                            accO, accL, accO2, accL2 = self.psum(4), self.psum(5), self.psum(6), self.psum(7)
                        for ti, (m, k0, nk) in enumerate(tiles):
                            first, lastt = (ti == 0), (ti == len(tiles) - 1)
                            diag = (k0 >= qkey0)
                            pts = []
                            for hl in range(2):
                                pss = self.psum(cnt["pt"] % 4)
                                pT = pTs[cnt["pt"] % 4]
                                cnt["pt"] += 1
                                hp = slice(64 * hl, 64 * hl + 64)
                                self.mm(pss[0:nk, 0:nq], KT[hp, k0:k0 + nk], QT[hp, t0:t0 + nq], True, True, [KT, QT], [pss])
                                if kind == "fox":
                                    hh = 2 * c + hl
                                    self.act(pT[0:nk, 0:nq], pss[0:nk, 0:nq], AF.Exp, [pss, biasTs[qi]], [pT], bias=biasTs[qi][0:nk, m, hh:hh + 1])
                                else:
                                    self.act(pT[0:nk, 0:nq], pss[0:nk, 0:nq], AF.Exp, [pss], [pT])
                                if diag:
                                    kt_ = (k0 - qkey0) // 128
                                    if kind == "fox":
                                        mk = mfox[kt_]
                                    elif nq == 512:
                                        mk = mdif[kt_]
                                    else:
                                        mk = None
                                    if mk is not None:
                                        self.tt("pool", pT[0:nk, 0:nq], pT[0:nk, 0:nq], mk[0:nk, 0:nq], ALU.mult, [pT, mk], [pT])
                                pts.append(pT)
                            if kind == "fox":
                                for hl in range(2):
                                    hp = slice(64 * hl, 64 * hl + 64)
                                    self.mm(accO[hp, 0:nq], V[0:nk, m, hp], pts[hl][0:nk, 0:nq], first, lastt, [V, pts[hl]], [accO])
                                    self.mm(accL[hp, 0:nq], self.ones_bf[0:nk, 0:64], pts[hl][0:nk, 0:nq], first, lastt, [self.ones_bf, pts[hl]], [accL])
                            else:
                                self.mm(accO[:, 0:nq], V[0:nk, m, :], pts[0][0:nk, 0:nq], first, lastt, [V, pts[0]], [accO])
                                self.mm(accL[:, 0:nq], self.ones_bf[0:nk, :], pts[0][0:nk, 0:nq], first, lastt, [self.ones_bf, pts[0]], [accL])
                                self.mm(accO2[:, 0:nq], V[0:nk, m, :], pts[1][0:nk, 0:nq], first, lastt, [V, pts[1]], [accO2])
                                self.mm(accL2[:, 0:nq], self.ones_bf[0:nk, :], pts[1][0:nk, 0:nq], first, lastt, [self.ones_bf, pts[1]], [accL2])
                        mix = mixs[cnt["mx"] % 2]
                        cnt["mx"] += 1
                        rl = rls[0]
                        self.op("dve", lambda e, rl=rl, accL=accL, nq=nq: e.reciprocal(out=rl[:, 0:nq], in_=accL[:, 0:nq]), [accL], [rl])
                        if kind == "fox":
                            self.tt("dve", mix[:, 0:nq], accO[:, 0:nq], rl[:, 0:nq], ALU.mult, [accO, rl], [mix])
                            row0 = 128 * c
                        else:
                            rl2 = rls[1]
                            self.op("dve", lambda e, rl2=rl2, accL2=accL2, nq=nq: e.reciprocal(out=rl2[:, 0:nq], in_=accL2[:, 0:nq]), [accL2], [rl2])
                            self.tt("dve", o1[:, 0:nq], accO[:, 0:nq], rl[:, 0:nq], ALU.mult, [accO, rl], [o1])
                            self.tt("dve", o2[:, 0:nq], accO2[:, 0:nq], rl2[:, 0:nq], ALU.mult, [accO2, rl2], [o2])
                            self.stt(o1[:, 0:nq], o2[:, 0:nq], nlam[:, 0:1], o1[:, 0:nq], ALU.mult, ALU.add, [o2, nlam, o1], [o1])
                            self.tt("pool", sqb[:, 0:nq], o1[:, 0:nq], o1[:, 0:nq], ALU.mult, [o1], [sqb])
                            pq = self.psum(cnt["pt"] % 4)
                            cnt["pt"] += 1
                            self.mm(pq[:, 0:nq], self.ones_bf[:], sqb[:, 0:nq], True, True, [self.ones_bf, sqb], [pq])
                            self.act(rstd[:, 0:nq], pq[:, 0:nq], AF.Sqrt, [pq, epsc], [rstd], bias=epsc[:], scale=1.0 / 128)
                            self.op("dve", lambda e, nq=nq: e.reciprocal(out=rstd[:, 0:nq], in_=rstd[:, 0:nq]), [rstd], [rstd])
                            self.stt(mix[:, 0:nq], o1[:, 0:nq], dg[:, 0:1], rstd[:, 0:nq], ALU.mult, ALU.mult, [o1, dg, rstd], [mix])
                            row0 = 512 + 128 * c
                        self.dma("sp", S["MIXT"][row0:row0 + 128, pc0:pc0 + nq], mix[:, 0:nq], r=[mix])
        self.P.barrier()

_CACHE = {}


def get_program(stages):
    key = tuple(sorted(stages))
    if key not in _CACHE:
        b = Builder(set(stages))
        nc = b.build()
        _CACHE[key] = (b, nc)
    return _CACHE[key]


STAGES = ("L0", "L1", "L2", "L3")

W_NAMES = ["norm_mix", "norm_ffn", "norm_final", "w_in_even", "w_out_even", "conv_w", "gdn_a_log", "gdn_dt_bias",
           "gdn_out_norm", "ssm_lambda_re", "ssm_lambda_im", "ssm_log_dt", "ssm_b_re", "ssm_b_im", "ssm_c_re",
           "ssm_c_im", "ssm_d", "ssm_glu_w", "ssm_glu_b", "w_in_odd", "w_out_odd", "fox_f_bias", "diff_lambda",
           "diff_out_norm", "ffn_w1", "ffn_w3", "ffn_w2"]


def kernel(**inp):
    b, nc = get_program(STAGES)
    f = lambda a: np.ascontiguousarray(np.asarray(a, dtype=np.float32))
    shared = {k: f(inp[k]) for k in W_NAMES}
    meta = f(inp["meta_tokens"])
    in_maps = []
    ncore = int(os.environ.get("MK_DEV_CORES", NCORES))
    for c in range(ncore):
        m = dict(shared)
        m["xp"] = f(inp["x_prompt"][2 * c:2 * c + 2])
        m["xs"] = f(inp["x_sample"][c])
        m["meta"] = meta
        m["st_conv"] = f(inp["state_conv"][:, c])
        m["st_delta"] = f(inp["state_delta"][:, c])
        m["st_re"] = f(inp["state_ssm_re"][:, c])
        m["st_im"] = f(inp["state_ssm_im"][:, c])
        m["c_fk"] = f(inp["cache_fox_k"][:, c])
        m["c_fv"] = f(inp["cache_fox_v"][:, c])
        m["c_fl"] = f(inp["cache_fox_logf"][:, c])
        m["c_dk"] = f(inp["cache_diff_k"][:, c])
        m["c_dv"] = f(inp["cache_diff_v"][:, c])
        in_maps.append(m)
    res = run_bass_kernel_spmd(nc, in_maps, core_ids=list(range(ncore)))
    R = list(res.results)
    while len(R) < NCORES:
        R.append({k: np.zeros_like(np.asarray(v)) for k, v in R[0].items()})
    cat = lambda name, ax: np.concatenate([np.asarray(r[name]) for r in R], axis=ax)
    stk = lambda name, ax: np.stack([np.asarray(r[name]) for r in R], axis=ax)
    outs = (
        cat("y_p", 0), stk("y_s", 0),
        cat("conv_p", 1), stk("conv_s", 1),
        cat("delta_p", 1), stk("delta_s", 1),
        cat("re_p", 1), stk("re_s", 1), cat("im_p", 1), stk("im_s", 1),
        cat("fk_p", 1), stk("fk_s", 1), cat("fv_p", 1), stk("fv_s", 1),
        cat("fl_p", 1), stk("fl_s", 1), cat("dk_p", 1), stk("dk_s", 1),
        cat("dv_p", 1), stk("dv_s", 1),
    )
    return tuple(np.ascontiguousarray(o, dtype=np.float32) for o in outs)
```

```python
import contextlib
import math
import os
import numpy as np
import concourse.bass as bass
import concourse.mybir as mybir
from concourse.bass_utils import run_bass_kernel_spmd

F32 = mybir.dt.float32
BF16 = mybir.dt.bfloat16
AF = mybir.ActivationFunctionType
ALU = mybir.AluOpType
AX = mybir.AxisListType

NCORES = 8
D = 1024
NE = 2
NO = 2
DEPTH = 4
LP = 4112
LS = 32
NROWS = 2 * LP + LS
SEQROW = [0, LP, 2 * LP]
SEQLEN = [LP, LP, LS]
PCOL = [0, 4160, 8320, 8384]
TOTP = 8448
EVEN_PROJ = 4624
ODD_PROJ = 3080
FFN = 2816
RMS_EPS = 1e-6
AK = [0, LP, 2 * LP]
NK = 2 * LP + 4096 + LS
CB = [AK[0], AK[1] + 1, AK[2] + 2]
DTSIZE = {F32: 4, BF16: 2, mybir.dt.int32: 4}

SEGS = []
for _s in range(3):
    if _s < 2:
        SEGS.append((SEQROW[_s], 16, _s, 0, PCOL[_s]))
        for _k in range(8):
            SEGS.append((SEQROW[_s] + 16 + 512 * _k, 512, _s, 16 + 512 * _k, PCOL[_s] + 64 + 512 * _k))
    else:
        SEGS.append((SEQROW[_s], 32, _s, 0, PCOL[_s]))


class Dep:
    __slots__ = ("w", "r")

    def __init__(self):
        self.w = None
        self.r = []


class Tl:
    __slots__ = ("ap", "dep")

    def __init__(self, ap, dep=None):
        self.ap = ap
        self.dep = dep if dep is not None else Dep()

    def __getitem__(self, k):
        return self.ap[k]


class Prog:
    ENGS = ("pe", "act", "dve", "pool", "sp")
    EPOCH = 8000
    NDMA = 20
    LOOK = 24

    def __init__(self, nc):
        self.nc = nc
        self.ops = []
        self.barriers = []

    def op(self, eng, fn, reads=(), writes=(), dma=False, busy=0.3, lat=None):
        idx = len(self.ops)
        cls = eng + ("_dma" if dma else "")
        raw = set()
        oth = set()
        for d in reads:
            if d.w is not None:
                raw.add(d.w)
        for d in writes:
            if d.w is not None:
                oth.add(d.w)
            oth.update(d.r)
        order = raw | oth
        order.discard(idx)
        deps = set(raw)
        for j in oth:
            if j in raw:
                continue
            if self.ops[j][5] == cls and not dma:
                continue
            deps.add(j)
        if cls == "pe":
            deps = {j for j in deps if self.ops[j][5] != "pe"}
        deps.discard(idx)
        for d in reads:
            d.r.append(idx)
        for d in writes:
            d.w = idx
            d.r = []
        self.ops.append([eng, fn, deps, dma, False, cls, order, busy, busy if lat is None else lat])
        return idx

    def barrier(self):
        if not self.barriers or self.barriers[-1] != len(self.ops):
            self.barriers.append(len(self.ops))

    def schedule_block(self, lo, hi, streams):
        import bisect
        ops = self.ops
        nosched = bool(os.environ.get("MK_NOSCHED"))
        if nosched:
            for i in range(lo, hi):
                streams[ops[i][0]].append(i)
            return
        npred = {}
        succs = {}
        pin = set((os.environ.get("MK_PIN", "")).split(","))
        prev_on = {}
        for i in range(lo, hi):
            e_ = ops[i][0]
            if e_ in pin:
                if e_ in prev_on:
                    ops[i][6].add(prev_on[e_])
                prev_on[e_] = i
        for i in range(lo, hi):
            c = 0
            for j in ops[i][6]:
                if j >= lo:
                    c += 1
                    succs.setdefault(j, []).append(i)
            npred[i] = c
        ready = {}
        avail = {e: [] for e in self.ENGS}
        for i in range(lo, hi):
            if npred[i] == 0:
                avail[ops[i][0]].append(i)
        free_at = {e: 0.0 for e in self.ENGS}
        left = hi - lo
        LOOK = self.LOOK
        while left:
            best = None
            for e in self.ENGS:
                av = avail[e]
                if not av:
                    continue
                fa = free_at[e]
                for c in av[:LOOK]:
                    st = ready.get(c, 0.0)
                    if st < fa:
                        st = fa
                    key = (int(st * 20), c)
                    if best is None or key < best[0]:
                        best = (key, e, c, st)
            _, e, c, st = best
            avail[e].remove(c)
            o = ops[c]
            free_at[e] = st + o[7]
            fin = st + o[8]
            streams[e].append(c)
            left -= 1
            for sidx in succs.get(c, ()):
                if ready.get(sidx, 0.0) < fin:
                    ready[sidx] = fin
                npred[sidx] -= 1
                if npred[sidx] == 0:
                    bisect.insort(avail[ops[sidx][0]], sidx)

    def emit(self, stack):
        nc = self.nc
        ops = self.ops
        streams = {e: [] for e in self.ENGS}
        bounds = [0] + [b for b in self.barriers if 0 < b < len(ops)] + [len(ops)]
        extra = {}
        for bi in range(len(bounds) - 1):
            lo, hi = bounds[bi], bounds[bi + 1]
            if lo == hi:
                continue
            pos = {e: len(streams[e]) for e in self.ENGS}
            pend = set()
            if bi > 0:
                for e in self.ENGS:
                    comp = None
                    nd = 0
                    for i in reversed(streams[e]):
                        if ops[i][3]:
                            if nd < self.NDMA:
                                pend.add(i)
                                nd += 1
                        elif comp is None:
                            comp = i
                            pend.add(i)
                        if comp is not None and nd >= self.NDMA:
                            break
            self.schedule_block(lo, hi, streams)
            if pend:
                for e in self.ENGS:
                    if len(streams[e]) > pos[e]:
                        extra[streams[e][pos[e]]] = pend
        for i, o in enumerate(ops):
            d = o[2]
            if i in extra:
                d = d | extra[i]
                d.discard(i)
                o[2] = d
            for j in d:
                ops[j][4] = True
        self.streams = streams
        ticket = {}
        comp_count = {e: 0 for e in self.ENGS}
        dma_count = {e: 0 for e in self.ENGS}
        dma_prev = {}
        for e in self.ENGS:
            for idx in streams[e]:
                o = ops[idx]
                if o[3]:
                    k = dma_count[e]
                    dma_count[e] += 1
                    key = (e, "d", k % self.NDMA)
                    val = 16 * (k // self.NDMA + 1)
                    ticket[idx] = (key, val)
                    dma_prev[idx] = (key, val - 16)
                elif o[4]:
                    comp_count[e] += 1
                    c = comp_count[e]
                    ep = (c - 1) // self.EPOCH
                    ticket[idx] = ((e, "c", ep), c - ep * self.EPOCH)
        sems = {}
        for k in sorted(set(t[0] for t in ticket.values())):
            sems[k] = stack.enter_context(nc.semaphore("s_%s_%s_%d" % k))
        self.ticket, self.dma_prev, self.nsems = ticket, dma_prev, len(sems)
        block = stack.enter_context(nc.Block())

        def build(e, engine):
            waited = {}
            last = {}
            for idx in streams[e]:
                o = ops[idx]
                need = {}
                for j in o[2]:
                    if j not in ticket:
                        continue
                    k, v = ticket[j]
                    if need.get(k, 0) < v:
                        need[k] = v
                if o[3]:
                    k, v = dma_prev[idx]
                    if v > 0 and need.get(k, 0) < v:
                        need[k] = v
                for k, v in need.items():
                    if waited.get(k, 0) >= v:
                        continue
                    engine.wait_ge(sems[k], v)
                    waited[k] = v
                inst = o[1](engine)
                if o[3]:
                    k, v = ticket[idx]
                    inst.then_inc(sems[k], 16)
                    last[k] = v
                elif o[4]:
                    inst.then_inc(sems[ticket[idx][0]], 1)
            for k, v in last.items():
                if waited.get(k, 0) < v:
                    engine.wait_ge(sems[k], v)

        @block.tensor
        def _(eng):
            build("pe", eng)

        @block.scalar
        def _(eng):
            build("act", eng)

        @block.vector
        def _(eng):
            build("dve", eng)

        @block.gpsimd
        def _(eng):
            build("pool", eng)

        @block.sync
        def _(eng):
            build("sp", eng)


class Builder:
    ARENA_F32 = 49152

    def __init__(self, stages):
        self.stages = stages
        self.nc = bass.Bass("TRN2", target_bir_lowering=False)
        self.P = Prog(self.nc)
        self.stack = contextlib.ExitStack()
        self.din = {}
        self.dout = {}
        self.scr = {}

    def inp(self, name, shape):
        self.din[name] = self.nc.dram_tensor(name, list(shape), F32, kind="ExternalInput").ap()
        return self.din[name]

    def outp(self, name, shape):
        self.dout[name] = self.nc.dram_tensor(name, list(shape), F32, kind="ExternalOutput").ap()
        return self.dout[name]

    def scratch(self, name, shape, dt):
        self.scr[name] = self.nc.dram_tensor(name, list(shape), dt, kind="Internal").ap()
        return self.scr[name]

    def reset_arena(self):
        self.aoff = self.aperm

    def sb(self, free, dt, parts=128):
        n = 1
        for f in free:
            n *= f
        nbytes = n * DTSIZE[dt]
        nb = (nbytes + 31) // 32 * 32
        assert self.aoff + nb <= self.ARENA_F32 * 4, "arena overflow %d" % (self.aoff + nb)
        v = self.arena[:, self.aoff // 4:(self.aoff + nbytes + 3) // 4]
        self.aoff += nb
        if dt != F32:
            v = v.bitcast(dt)
        v = v[:, 0:n]
        if len(free) == 2:
            v = v.rearrange("p (a b) -> p a b", a=free[0])
        elif len(free) == 3:
            v = v.rearrange("p (a b c) -> p a b c", a=free[0], b=free[1])
        if parts != 128:
            v = v[0:parts]
        return Tl(v)

    def psum(self, bank, dt=F32):
        t = self.banks[bank]
        ap = t.ap if dt == F32 else t.ap.bitcast(dt)
        return Tl(ap, t.dep)

    @staticmethod
    def _fe(ap):
        n = 1
        for d in tuple(ap.shape)[1:]:
            n *= int(d)
        return n

    def op(self, eng, fn, r=(), w=(), cost=None):
        if cost is None:
            cost = 0.3
        self.P.op(eng, fn, [t.dep for t in r], [t.dep for t in w], busy=cost)

    def dma(self, q, out, in_, r=(), w=(), **kw):
        nb = 1
        for d in tuple(out.shape):
            nb *= int(d)
        nb *= DTSIZE.get(out.dtype, 4)
        busy = 0.08 if q == "sp" else 0.7
        self.P.op(q, lambda e: e.dma_start(out=out, in_=in_, **kw), [t.dep for t in r], [t.dep for t in w], dma=True,
                  busy=busy, lat=2.0 + nb / 120000.0)

    def mm(self, out, lhsT, rhs, start, stop, r, w):
        c = 0.035 + self._fe(rhs) * (4 if rhs.dtype == F32 else 1) / 2400.0
        self.op("pe", lambda e: e.matmul(out, lhsT=lhsT, rhs=rhs, start=start, stop=stop), r, w, cost=c)

    def tr(self, out, in_, ident, r, w):
        c = 0.06 + self._fe(in_) * (4 if in_.dtype == F32 else 1) / 2400.0
        self.op("pe", lambda e: e.transpose(out, in_, ident), r, w, cost=c)

    def act(self, out, in_, func, r, w, bias=None, scale=None, accum=None):
        kw = {}
        if bias is not None:
            kw["bias"] = bias
        if scale is not None:
            kw["scale"] = scale
        if accum is not None:
            kw["accum_out"] = accum
        c = 0.2 + self._fe(in_) / 1400.0
        self.op("act", lambda e: e.activation(out=out, in_=in_, func=func, **kw), r, w, cost=c)

    def _vc(self, eng, n, two=False):
        if eng == "pool":
            return 0.2 + n / 600.0
        return 0.12 + n * (2 if two else 1) / 960.0

    def tt(self, eng, out, in0, in1, op, r, w):
        self.op(eng, lambda e: e.tensor_tensor(out=out, in0=in0, in1=in1, op=op), r, w, cost=self._vc(eng, self._fe(out), True))

    def ts(self, eng, out, in0, s1, op0, r, w, s2=None, op1=None):
        c = self._vc(eng, self._fe(out))
        if op1 is None:
            self.op(eng, lambda e: e.tensor_scalar(out=out, in0=in0, scalar1=s1, scalar2=None, op0=op0), r, w, cost=c)
        else:
            self.op(eng, lambda e: e.tensor_scalar(out=out, in0=in0, scalar1=s1, scalar2=s2, op0=op0, op1=op1), r, w, cost=c)

    def stt(self, out, in0, scalar, in1, op0, op1, r, w):
        self.op("dve", lambda e: e.scalar_tensor_tensor(out=out, in0=in0, scalar=scalar, in1=in1, op0=op0, op1=op1), r, w,
                cost=self._vc("dve", self._fe(out), True))

    def cp(self, eng, out, in_, r, w):
        if eng == "act":
            self.op("act", lambda e: e.copy(out=out, in_=in_), r, w, cost=0.2 + self._fe(out) / 1400.0)
        else:
            self.op(eng, lambda e: e.tensor_copy(out=out, in_=in_), r, w, cost=self._vc(eng, self._fe(out)))

    def memset(self, eng, ap, val, w):
        self.op(eng, lambda e: e.memset(ap, val), (), w, cost=self._vc(eng, self._fe(ap)))

    def declare(self):
        i = self.inp
        i("xp", [2, 4096, D]); i("xs", [LS, D]); i("meta", [16, D])
        i("st_conv", [NE, 3, 3072]); i("st_delta", [NE, 8, 128, 128])
        i("st_re", [NE, 32, 64]); i("st_im", [NE, 32, 64])
        i("c_fk", [NO, 4096, 8, 64]); i("c_fv", [NO, 4096, 8, 64]); i("c_fl", [NO, 4096, 8])
        i("c_dk", [NO, 4096, 8, 64]); i("c_dv", [NO, 4096, 4, 128])
        i("norm_mix", [DEPTH, D]); i("norm_ffn", [DEPTH, D]); i("norm_final", [D])
        i("w_in_even", [NE, D, EVEN_PROJ]); i("w_out_even", [NE, 1536, D]); i("conv_w", [NE, 4, 3072])
        i("gdn_a_log", [NE, 8]); i("gdn_dt_bias", [NE, 8]); i("gdn_out_norm", [NE, 128])
        i("ssm_lambda_re", [NE, 32, 64]); i("ssm_lambda_im", [NE, 32, 64]); i("ssm_log_dt", [NE, 32])
        i("ssm_b_re", [NE, 32, 64, 16]); i("ssm_b_im", [NE, 32, 64, 16])
        i("ssm_c_re", [NE, 32, 16, 64]); i("ssm_c_im", [NE, 32, 16, 64])
        i("ssm_d", [NE, 512]); i("ssm_glu_w", [NE, 512, 512]); i("ssm_glu_b", [NE, 512])
        i("w_in_odd", [NO, D, ODD_PROJ]); i("w_out_odd", [NO, D, D])
        i("fox_f_bias", [NO, 8]); i("diff_lambda", [NO, 4, 64]); i("diff_out_norm", [NO, 128])
        i("ffn_w1", [DEPTH, D, FFN]); i("ffn_w3", [DEPTH, D, FFN]); i("ffn_w2", [DEPTH, FFN, D])
        o = self.outp
        o("y_p", [2, 4096, D]); o("y_s", [LS, D])
        o("conv_p", [NE, 2, 3, 3072]); o("conv_s", [NE, 3, 3072])
        o("delta_p", [NE, 2, 8, 128, 128]); o("delta_s", [NE, 8, 128, 128])
        o("re_p", [NE, 2, 32, 64]); o("re_s", [NE, 32, 64]); o("im_p", [NE, 2, 32, 64]); o("im_s", [NE, 32, 64])
        o("fk_p", [NO, 2, LP, 8, 64]); o("fk_s", [NO, LS, 8, 64])
        o("fv_p", [NO, 2, LP, 8, 64]); o("fv_s", [NO, LS, 8, 64])
        o("fl_p", [NO, 2, LP, 8]); o("fl_s", [NO, LS, 8])
        o("dk_p", [NO, 2, LP, 8, 64]); o("dk_s", [NO, LS, 8, 64])
        o("dv_p", [NO, 2, LP, 4, 128]); o("dv_s", [NO, LS, 4, 128])
        s = self.scratch
        s("Xres", [NROWS, D], F32)
        s("QT", [1024, TOTP], BF16); s("KT", [1024, TOTP], BF16); s("VT", [1024, TOTP], BF16)
        s("ZT", [1024, TOTP], BF16); s("UT", [512, TOTP], BF16)
        s("Btok", [TOTP, 8], F32); s("GCtok", [TOTP, 8], F32); s("GCT", [8, TOTP], F32)
        s("MIXT", [1536, TOTP], BF16)
        s("KFT", [512, NK], BF16); s("KDT", [512, NK], BF16)
        s("QFT", [512, NROWS], BF16); s("QDT", [512, NROWS], BF16)
        s("VF", [NK, 512], BF16); s("VD", [NK, 512], BF16)
        s("CUM", [NK + 3, 8], F32)

    def build(self):
        nc = self.nc
        self.declare()
        st = self.stack
        self.arena = st.enter_context(nc.sbuf_tensor("arena", [128, self.ARENA_F32], F32))
        self.banks = [Tl(st.enter_context(nc.psum_tensor("bank%d" % b, [128, 512], F32))[:]) for b in range(8)]
        self.aoff = 0
        self.aperm = 0
        self.ident = self.sb([128], BF16)
        self.identf = self.sb([128], F32)
        self.ones_bf = self.sb([128], BF16)
        self.tri = self.sb([128], F32)
        self.zeros = self.sb([512], F32)
        self.onec = self.sb([1], F32)
        self.trif = self.sb([128], F32)
        self.onesf = self.sb([128], F32)
        self.aperm = self.aoff
        self.consts()
        self.init_x()
        for l in range(DEPTH):
            if ("L%d" % l) not in self.stages:
                continue
            i_ = l // 2
            if l % 2 == 0:
                self.phaseA_even(l)
                self.phaseB_gdn(l)
                self.phaseB_s5(l)
                self.phaseC_outproj(l, self.din["w_out_even"][i_], 12)
            else:
                self.phaseA_odd(l)
                self.phaseB_attn(l)
                self.phaseC_outproj(l, self.din["w_out_odd"][i_], 8)
            self.phaseD_ffn(l)
        self.P.emit(st)
        return nc

    def consts(self):
        idb, idf, tri = self.ident, self.identf, self.tri
        self.memset("pool", idb[:], 1.0, [idb])
        self.op("pool", lambda e: e.affine_select(out=idb[:], in_=idb[:], pattern=[[-1, 128]], compare_op=ALU.is_equal,
                                                  fill=0.0, base=0, channel_multiplier=1), [idb], [idb])
        self.memset("pool", idf[:], 1.0, [idf])
        self.op("pool", lambda e: e.affine_select(out=idf[:], in_=idf[:], pattern=[[-1, 128]], compare_op=ALU.is_equal,
                                                  fill=0.0, base=0, channel_multiplier=1), [idf], [idf])
        self.memset("dve", self.ones_bf[:], 1.0, [self.ones_bf])
        self.memset("dve", self.zeros[:], 0.0, [self.zeros])
        self.memset("dve", self.onec[:], 1.0, [self.onec])
        self.memset("pool", tri[:], 1.0, [tri])
        self.op("pool", lambda e: e.affine_select(out=tri[:], in_=tri[:], pattern=[[1, 128]], compare_op=ALU.is_ge,
                                                  fill=0.0, base=0, channel_multiplier=-1), [tri], [tri])
        trif = self.trif
        self.memset("pool", trif[:], 1.0, [trif])
        self.op("pool", lambda e: e.affine_select(out=trif[:], in_=trif[:], pattern=[[1, 128]], compare_op=ALU.is_ge,
                                                  fill=0.0, base=0, channel_multiplier=-1), [trif], [trif])
        self.memset("pool", self.onesf[:], 1.0, [self.onesf])
        self.memset("pool", tri[64:128, 0:64], 0.0, [tri])
        self.memset("pool", tri[0:64, 64:128], 0.0, [tri])

    def init_x(self):
        X = self.scr["Xres"]
        for s in range(2):
            self.dma("sp", X[SEQROW[s]:SEQROW[s] + 16, :], self.din["meta"])
            for q in range(4):
                self.dma("sp", X[SEQROW[s] + 16 + 1024 * q:SEQROW[s] + 16 + 1024 * (q + 1), :],
                         self.din["xp"][s, 1024 * q:1024 * (q + 1), :])
        self.dma("sp", X[SEQROW[2]:SEQROW[2] + LS, :], self.din["xs"])
        z = self.zeros
        for name in ("QT", "KT", "VT"):
            A = self.scr[name].rearrange("(h p) t -> p h t", p=128)
            zb = z.ap.bitcast(BF16)
            for s, (c0, c1) in ((0, (16, 64)), (1, (16, 64)), (2, (32, 64)), (3, (0, 64))):
                w = c1 - c0
                src = zb[:, 0:8 * w].rearrange("p (h t) -> p h t", h=8)
                self.dma("sp", A[:, :, PCOL[s] + c0:PCOL[s] + c1], src, r=[z])
        for name in ("Btok", "GCtok"):
            A = self.scr[name]
            for s, (c0, c1) in ((0, (16, 64)), (1, (16, 64)), (2, (32, 64)), (3, (0, 64))):
                self.dma("sp", A[PCOL[s] + c0:PCOL[s] + c1, :], z[0:c1 - c0, 0:8], r=[z])
        A = self.scr["GCT"]
        for s, (c0, c1) in ((0, (16, 64)), (1, (16, 64)), (2, (32, 64)), (3, (0, 64))):
            self.dma("sp", A[:, PCOL[s] + c0:PCOL[s] + c1], z[0:8, 0:c1 - c0], r=[z])
        for s_ in range(3):
            self.dma("sp", self.scr["CUM"][CB[s_]:CB[s_] + 1, :], z[0:1, 0:8], r=[z])
        self.P.barrier()

    def load_weight(self, wdram, K, N, dst, gain=None, stage_cols=2312):
        kc = K // 128
        stg = [self.sb([stage_cols], F32), self.sb([stage_cols], F32)]
        i = 0
        for c in range(kc):
            for n0 in range(0, N, stage_cols):
                n1 = min(N, n0 + stage_cols)
                s = stg[i % 2]
                i += 1
                self.dma("pool" if i % 2 else "sp", s[:, 0:n1 - n0], wdram[c * 128:(c + 1) * 128, n0:n1], w=[s])
                if gain is not None:
                    self.act(dst[:, c, n0:n1], s[:, 0:n1 - n0], AF.Copy, [s, gain], [dst], scale=gain[:, c:c + 1])
                else:
                    self.cp("act", dst[:, c, n0:n1], s[:, 0:n1 - n0], [s], [dst])

    def load_gain(self, vec_ap):
        g = self.sb([8], F32)
        self.dma("sp", g[:], vec_ap.rearrange("(c p) -> p c", p=128), w=[g], allow_slow_non_contiguous=True)
        return g

    def load_x(self, r0, n, xt):
        X = self.scr["Xres"]
        nsub = (n + 127) // 128
        if n >= 128:
            self.dma("sp", xt[:, 0:nsub, :], X[r0:r0 + n, :].rearrange("(j p) d -> p j d", p=128), w=[xt])
        else:
            self.dma("sp", xt[0:n, 0, :], X[r0:r0 + n, :], w=[xt])

    def store_x(self, r0, n, xt):
        X = self.scr["Xres"]
        nsub = (n + 127) // 128
        if n >= 128:
            self.dma("pool", X[r0:r0 + n, :].rearrange("(j p) d -> p j d", p=128), xt[:, 0:nsub, :], r=[xt])
        else:
            self.dma("pool", X[r0:r0 + n, :], xt[0:n, 0, :], r=[xt])

    def norm_transpose(self, seg, xt, h, junk, ss, hT, bank, load=True):
        r0, n = seg[0], seg[1]
        X = self.scr["Xres"]
        nsub = (n + 127) // 128
        if not load:
            pass
        elif n >= 128:
            self.dma("sp", xt[:, 0:nsub, :], X[r0:r0 + n, :].rearrange("(j p) d -> p j d", p=128), w=[xt])
        else:
            self.dma("sp", xt[0:n, 0, :], X[r0:r0 + n, :], w=[xt])
        pj = min(128, n)
        for j in range(nsub):
            self.act(junk[0:pj, :], xt[0:pj, j, :], AF.Square, [xt], [junk, ss], accum=ss[0:pj, j:j + 1])
        self.act(ss[0:pj, 0:nsub], ss[0:pj, 0:nsub], AF.Sqrt, [ss], [ss], bias=self.epsc[0:pj, :], scale=1.0 / D)
        self.op("dve", lambda e: e.reciprocal(out=ss[0:pj, 0:nsub], in_=ss[0:pj, 0:nsub]), [ss], [ss])
        for j in range(nsub):
            self.ts("dve", h[0:pj, j, :], xt[0:pj, j, :], ss[0:pj, j:j + 1], ALU.mult, [xt, ss], [h])
        for j in range(nsub):
            pT = self.psum(bank[j % len(bank)], BF16)
            pv = pT.ap.rearrange("p (c t) -> p c t", c=8)
            for c in range(8):
                self.tr(pv[:, c, 0:pj], h[0:pj, j, c * 128:(c + 1) * 128], self.ident[0:pj, 0:pj], [h, self.ident], [pT])
            self.cp("act" if j % 2 else "dve", hT[:, :, j * 128:j * 128 + pj], pv[:, :, 0:pj], [pT], [hT])

    def phaseA_even(self, l):
        i = l // 2
        self.reset_arena()
        S = self.scr
        self.epsc = self.sb([1], F32)
        self.memset("dve", self.epsc[:], RMS_EPS, [self.epsc])
        eps6 = self.sb([1], F32)
        self.memset("dve", eps6[:], 1e-6, [eps6])
        wbf = self.sb([8, EVEN_PROJ], BF16)
        gain = self.load_gain(self.din["norm_mix"][l])
        save = self.aoff
        self.load_weight(self.din["w_in_even"][i], D, EVEN_PROJ, wbf, gain)
        cw = self.sb([24, 4], F32)
        for j in range(4):
            self.dma("sp", cw[:, :, j], self.din["conv_w"][i, j].rearrange("(m p) -> p m", p=128), w=[cw],
                     allow_slow_non_contiguous=True)
        dtb = self.sb([8], F32)
        self.dma("sp", dtb[:], self.din["gdn_dt_bias"][i:i + 1, :].to_broadcast([128, 8]), w=[dtb])
        nea = self.sb([8], F32)
        self.dma("sp", nea[:], self.din["gdn_a_log"][i:i + 1, :].to_broadcast([128, 8]), w=[nea])
        self.act(nea[:], nea[:], AF.Exp, [nea], [nea])
        self.ts("dve", nea[:], nea[:], -1.0, ALU.mult, [nea], [nea])
        halo = self.sb([24, 3], F32)
        xts = [self.sb([4, D], F32), self.sb([4, D], F32)]
        h = self.sb([4, D], BF16)
        junk = self.sb([D], BF16)
        sss = [self.sb([4], F32), self.sb([4], F32)]
        hTs = [self.sb([8, 512], BF16), self.sb([8, 512], BF16)]
        pcs = [self.sb([516], F32) for _ in range(2)]
        accs = [self.sb([512], F32) for _ in range(2)]
        svs = [self.sb([512], F32) for _ in range(2)]
        sqs = [self.sb([512], BF16) for _ in range(2)]
        rss = [self.sb([512], F32) for _ in range(2)]
        obs = [self.sb([512], BF16) for _ in range(3)]
        tks = [self.sb([16], F32) for _ in range(2)]
        gcs = [self.sb([8], F32) for _ in range(2)]
        gct = self.sb([512], F32)
        ob_i = 0
        for si, seg in enumerate(SEGS):
            r0, n, s, t0, pc0 = seg
            xt, ss, hT = xts[si % 2], sss[si % 2], hTs[si % 2]
            self.norm_transpose(seg, xt, h, junk, ss, hT, bank=[0, 1])
            nsub = (n + 127) // 128
            pj = min(128, n)
            if t0 == 0:
                if s < 2:
                    self.memset("pool", halo[:], 0.0, [halo])
                else:
                    for tt_ in range(3):
                        self.dma("sp", halo[:, :, tt_], self.din["st_conv"][i, tt_].rearrange("(m p) -> p m", p=128), w=[halo],
                                 allow_slow_non_contiguous=True)
            last = (t0 + n == SEQLEN[s])
            chunks = [("qkv", m, m * 128) for m in range(24)] + [("z", m, 3072 + m * 128) for m in range(8)] + \
                     [("u", m, 4112 + m * 128) for m in range(4)]
            for ci, (kind, m, col) in enumerate(chunks):
                bk = 2 + ci % 4
                ps = self.psum(bk)
                for c in range(8):
                    self.mm(ps[:, 0:n], wbf[:, c, col:col + 128], hT[:, c, 0:n], c == 0, c == 7, [wbf, hT], [ps])
                ob = obs[ob_i % 3]
                ob_i += 1
                if kind == "qkv":
                    pc, acc, sv, sq, rs = pcs[ci % 2], accs[ci % 2], svs[ci % 2], sqs[ci % 2], rss[ci % 2]
                    self.cp("act", pc[:, 3:3 + n], ps[:, 0:n], [ps], [pc])
                    self.cp("pool", pc[:, 0:3], halo[:, m, :], [halo], [pc])
                    if last:
                        if s < 2:
                            dst = self.dout["conv_p"][i, s, :, m * 128:(m + 1) * 128]
                        else:
                            dst = self.dout["conv_s"][i, :, m * 128:(m + 1) * 128]
                        self.dma("pool", dst.rearrange("t c -> c t"), pc[:, n:n + 3], r=[pc], allow_slow_non_contiguous=True)
                    else:
                        self.cp("pool", halo[:, m, :], pc[:, n:n + 3], [pc], [halo])
                    self.ts("dve", acc[:, 0:n], pc[:, 3:3 + n], cw[:, m, 3:4], ALU.mult, [pc, cw], [acc])
                    for j in (2, 1, 0):
                        self.stt(acc[:, 0:n], pc[:, j:j + n], cw[:, m, j:j + 1], acc[:, 0:n], ALU.mult, ALU.add, [pc, cw, acc], [acc])
                    if m >= 16:
                        self.act(ob[:, 0:n], acc[:, 0:n], AF.Silu, [acc], [ob])
                        dstA = S["VT"]
                    else:
                        self.act(sv[:, 0:n], acc[:, 0:n], AF.Silu, [acc], [sv])
                        self.tt("pool", sq[:, 0:n], sv[:, 0:n], sv[:, 0:n], ALU.mult, [sv], [sq])
                        p2 = self.psum(6 + ci % 2)
                        self.mm(p2[:, 0:n], self.ones_bf[:], sq[:, 0:n], True, True, [self.ones_bf, sq], [p2])
                        self.act(rs[:, 0:n], p2[:, 0:n], AF.Sqrt, [p2, eps6], [rs], bias=eps6[:])
                        self.op("dve", lambda e, rs=rs, n=n: e.reciprocal(out=rs[:, 0:n], in_=rs[:, 0:n]), [rs], [rs], cost=0.12 + n / 960.0)
                        sc = (128.0 ** -0.5) if m < 8 else 1.0
                        self.stt(ob[:, 0:n], sv[:, 0:n], sc, rs[:, 0:n], ALU.mult, ALU.mult, [sv, rs], [ob])
                        dstA = S["QT"] if m < 8 else S["KT"]
                    mm_ = m % 8
                elif kind == "z":
                    self.cp("act", ob[:, 0:n], ps[:, 0:n], [ps], [ob])
                    dstA = S["ZT"]
                    mm_ = m
                else:
                    self.cp("act", ob[:, 0:n], ps[:, 0:n], [ps], [ob])
                    dstA = S["UT"]
                    mm_ = m
                self.dma("sp", dstA[mm_ * 128:(mm_ + 1) * 128, pc0:pc0 + n], ob[:, 0:n], r=[ob])
            for j in range(nsub):
                tk, gc = tks[j % 2], gcs[j % 2]
                ps = self.psum(6 + j % 2)
                for c in range(8):
                    self.mm(ps[0:pj, 0:16], hT[:, c, j * 128:j * 128 + pj], wbf[:, c, 4096:4112], c == 0, c == 7, [hT, wbf], [ps])
                self.act(tk[0:pj, 0:8], ps[0:pj, 0:8], AF.Sigmoid, [ps], [tk])
                self.dma("pool", S["Btok"][pc0 + j * 128:pc0 + j * 128 + pj, :], tk[0:pj, 0:8], r=[tk])
                self.tt("dve", tk[0:pj, 8:16], ps[0:pj, 8:16], dtb[0:pj, :], ALU.add, [ps, dtb], [tk])
                self.act(tk[0:pj, 8:16], tk[0:pj, 8:16], AF.Exp, [tk], [tk])
                self.act(tk[0:pj, 8:16], tk[0:pj, 8:16], AF.Ln, [tk, self.onec], [tk], bias=self.onec[0:pj, :])
                self.tt("dve", tk[0:pj, 8:16], tk[0:pj, 8:16], nea[0:pj, :], ALU.mult, [tk, nea], [tk])
                pm = max(pj, 64)
                ps3 = self.psum(2 + j % 2)
                self.mm(ps3[0:pm, 0:8], self.tri[0:pj, 0:pm], tk[0:pj, 8:16], True, True, [self.tri, tk], [ps3])
                self.cp("dve", gc[0:pm, :], ps3[0:pm, 0:8], [ps3], [gc])
                self.dma("pool", S["GCtok"][pc0 + j * 128:pc0 + j * 128 + pm, :], gc[0:pm, :], r=[gc])
                ps4 = self.psum(4 + j % 2)
                self.tr(ps4[0:8, 0:pm], gc[0:pm, :], self.identf[0:pm, 0:pm], [gc, self.identf], [ps4])
                self.cp("act", gct[0:8, j * 128:j * 128 + pm], ps4[0:8, 0:pm], [ps4], [gct])
            self.dma("pool", S["GCT"][:, pc0:pc0 + max(n, 64)], gct[0:8, 0:max(n, 64)], r=[gct])
        self.P.barrier()


    def phaseB_gdn(self, l):
        i = l // 2
        self.reset_arena()
        S = self.scr
        NEG = -1.0e5
        sb = self.sb
        epsc = sb([1], F32)
        self.memset("dve", epsc[:], RMS_EPS, [epsc])
        mSL = sb([128], F32)
        mUI = sb([128], F32)
        self.memset("pool", mSL[:], 0.0, [mSL])
        self.op("pool", lambda e: e.affine_select(out=mSL[:], in_=mSL[:], pattern=[[-1, 128]], compare_op=ALU.is_gt,
                                                  fill=NEG, base=0, channel_multiplier=1), [mSL], [mSL])
        self.memset("pool", mUI[:], 0.0, [mUI])
        self.op("pool", lambda e: e.affine_select(out=mUI[:], in_=mUI[:], pattern=[[1, 128]], compare_op=ALU.is_ge,
                                                  fill=NEG, base=0, channel_multiplier=-1), [mUI], [mUI])
        for m_ in (mSL, mUI):
            self.memset("pool", m_[64:128, 0:64], NEG, [m_])
            self.memset("pool", m_[0:64, 64:128], NEG, [m_])
        gain = sb([1], F32)
        self.dma("sp", gain[:], self.din["gdn_out_norm"][i].rearrange("(p o) -> p o", o=1), w=[gain])

        def T3(dt):
            return sb([8, 128], dt)
        ld = [dict(kT=T3(BF16), qT=T3(BF16), vT=T3(BF16), zT=T3(BF16), gcrow=T3(F32), btok=sb([8], F32), gctok=sb([8], F32))
              for _ in range(2)]
        diff, e1, e2 = T3(F32), T3(F32), T3(F32)
        A, B = T3(F32), T3(F32)
        Xs, Ys, Ps, Pts = [T3(F32), T3(F32)], [T3(F32), T3(F32)], [T3(F32), T3(F32)], [T3(F32), T3(F32)]
        IXs, IYs = [T3(F32), T3(F32)], [T3(F32), T3(F32)]
        Tt, Kb, Kd, Vb, QKm = T3(BF16), T3(BF16), T3(BF16), T3(BF16), T3(BF16)
        nWTa, nWTb, vn, qsa, qsb = T3(BF16), T3(BF16), T3(BF16), T3(BF16), T3(BF16)
        EG, o, sq = T3(F32), T3(F32), T3(F32)
        on, gsz, mixed = T3(BF16), T3(BF16), T3(BF16)
        Sa, Sb, Sabf, Sbbf = T3(F32), T3(F32), T3(BF16), T3(BF16)
        glast, sc1, sc2, eglA, eglB, ssq = (sb([8], F32) for _ in range(6))
        for t_ in (nWTa, nWTb, qsa, qsb):
            self.memset("pool", t_[:], 0.0, [t_])

        def bcl(ap):
            return ap.unsqueeze(2).to_broadcast([128, 8, 128])

        def bcm(ap):
            return ap.unsqueeze(1).to_broadcast([128, 8, 128])

        def v4(t, g):
            return t.ap.rearrange("p (a b) -> p a b", a=4)

        def v8(t):
            return t.ap.rearrange("p (a b) -> p a b", a=8)

        KTv = S["KT"].rearrange("(h p) t -> p h t", p=128)
        QTv = S["QT"].rearrange("(h p) t -> p h t", p=128)
        VTv = S["VT"].rearrange("(h p) t -> p h t", p=128)
        ZTv = S["ZT"].rearrange("(h p) t -> p h t", p=128)
        MXv = S["MIXT"].rearrange("(h p) t -> p h t", p=128)

        def init_state(sample):
            for (F, Fb) in ((Sa, Sabf), (Sb, Sbbf)):
                self.memset("pool", F[:], 0.0, [F])
                self.memset("pool", Fb[:], 0.0, [Fb])
            if sample:
                self.dma("sp", Sa[:], self.din["st_delta"][i].rearrange("h k v -> k h v"), w=[Sa])
                self.cp("act", Sabf[:], Sa[:], [Sa], [Sabf])

        packs = [(False, PCOL[0] + 64 * c, PCOL[1] + 64 * c) for c in range(65)] + [(True, PCOL[2], PCOL[3])]
        for pi, (sample, ca, cb) in enumerate(packs):
            if pi == 0 or sample:
                init_state(sample)
            L = ld[pi % 2]
            kT, qT, vT, zT, gcrow, btok, gctok = (L[k] for k in ("kT", "qT", "vT", "zT", "gcrow", "btok", "gctok"))
            for slot, col in ((0, ca), (1, cb)):
                fs = slice(64 * slot, 64 * slot + 64)
                self.dma("sp", kT[:, :, fs], KTv[:, :, col:col + 64], w=[kT])
                self.dma("sp", qT[:, :, fs], QTv[:, :, col:col + 64], w=[qT])
                self.dma("sp", vT[:, :, fs], VTv[:, :, col:col + 64], w=[vT])
                self.dma("pool", zT[:, :, fs], ZTv[:, :, col:col + 64], w=[zT])
                self.dma("pool", gcrow[:, :, fs], S["GCT"][:, col:col + 64].partition_broadcast(128), w=[gcrow])
                self.dma("pool", btok[fs, :], S["Btok"][col:col + 64, :], w=[btok])
                self.dma("pool", gctok[fs, :], S["GCtok"][col:col + 64, :], w=[gctok])
            self.tt("dve", diff[:], bcl(gctok[:]), gcrow[:], ALU.subtract, [gctok, gcrow], [diff])
            self.tt("pool", e1[:], diff[:], bcm(mSL[:]), ALU.add, [diff, mSL], [e1])
            self.tt("dve", e2[:], bcm(mUI[:]), diff[:], ALU.subtract, [diff, mUI], [e2])
            self.act(e1[:], e1[:], AF.Exp, [e1], [e1])
            self.act(e2[:], e2[:], AF.Exp, [e2], [e2])
            self.tt("pool", e1[:], e1[:], bcl(btok[:]), ALU.mult, [e1, btok], [e1])
            self.cp("pool", glast[0:64, :], gcrow[0:64, :, 63], [gcrow], [glast])
            self.cp("pool", glast[64:128, :], gcrow[64:128, :, 127], [gcrow], [glast])
            self.act(sc1[:], gctok[:], AF.Exp, [gctok], [sc1])
            self.tt("dve", sc1[:], sc1[:], btok[:], ALU.mult, [sc1, btok], [sc1])
            self.tt("dve", sc2[:], glast[:], gctok[:], ALU.subtract, [glast, gctok], [sc2])
            self.act(sc2[:], sc2[:], AF.Exp, [sc2], [sc2])
            self.act(eglA[:], gcrow[:, :, 63], AF.Exp, [gcrow], [eglA])
            self.act(eglB[:], gcrow[:, :, 127], AF.Exp, [gcrow], [eglB])
            self.act(EG[:], gcrow[:], AF.Exp, [gcrow], [EG])
            for g in range(2):
                pk = self.psum(g)
                pq = self.psum(2 + g)
                for hh in range(4):
                    h = 4 * g + hh
                    self.mm(v4(pk, 0)[:, hh, :], kT[:, h, :], kT[:, h, :], True, True, [kT], [pk])
                    self.mm(v4(pq, 0)[:, hh, :], kT[:, h, :], qT[:, h, :], True, True, [kT, qT], [pq])
                hs = slice(4 * g, 4 * g + 4)
                self.tt("dve", A[:, hs, :], v4(pk, 0), e1[:, hs, :], ALU.mult, [pk, e1], [A])
                self.tt("dve", QKm[:, hs, :], v4(pq, 0), e2[:, hs, :], ALU.mult, [pq, e2], [QKm])
            for g in range(2):
                pb = self.psum(4 + g)
                for hh in range(4):
                    self.tr(v4(pb, 0)[:, hh, :], A[:, 4 * g + hh, :], self.identf[:], [A, self.identf], [pb])
                self.cp("act", B[:, 4 * g:4 * g + 4, :], v4(pb, 0), [pb], [B])
            X, Y = A, B
            P0, Pt0 = Ps[0], Pts[0]
            self.tt("pool", P0[:], bcm(self.identf[:]), A[:], ALU.subtract, [A, self.identf], [P0])
            self.tt("pool", Pt0[:], bcm(self.identf[:]), B[:], ALU.subtract, [B, self.identf], [Pt0])
            Pc, Ptc = P0, Pt0
            for k in range(1, 6):
                lastk = (k == 5)
                Xn, Yn = Xs[k % 2], Ys[k % 2]
                Pn, Ptn = Ps[k % 2], Pts[k % 2]
                for g in range(2):
                    hs = slice(4 * g, 4 * g + 4)
                    if not lastk:
                        px = self.psum(g)
                        for hh in range(4):
                            h = 4 * g + hh
                            self.mm(v4(px, 0)[:, hh, :], Y[:, h, :], X[:, h, :], True, True, [X, Y], [px])
                        self.cp("act", Xn[:, hs, :], v4(px, 0), [px], [Xn])
                    py = self.psum(2 + g)
                    for hh in range(4):
                        h = 4 * g + hh
                        self.mm(v4(py, 0)[:, hh, :], X[:, h, :], Y[:, h, :], True, True, [X, Y], [py])
                    self.cp("dve", Yn[:, hs, :], v4(py, 0), [py], [Yn])
                for g in range(2):
                    hs = slice(4 * g, 4 * g + 4)
                    pt = self.psum(4 + g)
                    for hh in range(4):
                        h = 4 * g + hh
                        self.mm(v4(pt, 0)[:, hh, :], Pc[:, h, :], self.identf[:], True, False, [Pc, self.identf], [pt])
                        self.mm(v4(pt, 0)[:, hh, :], Pc[:, h, :], Yn[:, h, :], False, True, [Pc, Yn], [pt])
                    if lastk:
                        self.cp("act", Tt[:, hs, :], v4(pt, 0), [pt], [Tt])
                    else:
                        self.cp("act", Ptn[:, hs, :], v4(pt, 0), [pt], [Ptn])
                        pp = self.psum(6 + g)
                        for hh in range(4):
                            h = 4 * g + hh
                            self.mm(v4(pp, 0)[:, hh, :], Ptc[:, h, :], self.identf[:], True, False, [Ptc, self.identf], [pp])
                            self.mm(v4(pp, 0)[:, hh, :], Ptc[:, h, :], Xn[:, h, :], False, True, [Ptc, Xn], [pp])
                        self.cp("dve", Pn[:, hs, :], v4(pp, 0), [pp], [Pn])
                X, Y, Pc, Ptc = Xn, Yn, Pn, Ptn
            pK = self.psum(0, BF16)
            pV = self.psum(1, BF16)
            for h in range(8):
                self.tr(v8(pK)[:, h, :], kT[:, h, :], self.ident[:], [kT, self.ident], [pK])
                self.tr(v8(pV)[:, h, :], vT[:, h, :], self.ident[:], [vT, self.ident], [pV])
            self.tt("dve", Kb[:], v8(pK), bcl(sc1[:]), ALU.mult, [pK, sc1], [Kb])
            self.tt("dve", Kd[:], v8(pK), bcl(sc2[:]), ALU.mult, [pK, sc2], [Kd])
            self.tt("dve", Vb[:], v8(pV), bcl(btok[:]), ALU.mult, [pV, btok], [Vb])
            for g in range(2):
                pw = self.psum(2 + g)
                for hh in range(4):
                    h = 4 * g + hh
                    self.mm(v4(pw, 0)[:, hh, :], Kb[:, h, :], Tt[:, h, :], True, True, [Kb, Tt], [pw])
                hs = slice(4 * g, 4 * g + 4)
                self.ts("dve", nWTa[:, hs, 0:64], v4(pw, 0)[:, :, 0:64], -1.0, ALU.mult, [pw], [nWTa])
                self.op("act", lambda e, o_=nWTb[:, hs, 64:128], i_=v4(pw, 0)[:, :, 64:128]: e.mul(out=o_, in_=i_, mul=-1.0)
                        if False else e.activation(out=o_, in_=i_, func=AF.Copy, scale=-1.0), [pw], [nWTb])
            for g in range(2):
                pv = self.psum(4 + g)
                for hh in range(4):
                    h = 4 * g + hh
                    self.mm(v4(pv, 0)[:, hh, :], Tt[:, h, :], Vb[:, h, :], True, False, [Tt, Vb], [pv])
                    self.mm(v4(pv, 0)[:, hh, :], nWTa[:, h, :], Sabf[:, h, :], False, False, [nWTa, Sabf], [pv])
                    self.mm(v4(pv, 0)[:, hh, :], nWTb[:, h, :], Sbbf[:, h, :], False, True, [nWTb, Sbbf], [pv])
                self.cp("act", vn[:, 4 * g:4 * g + 4, :], v4(pv, 0), [pv], [vn])
            self.tt("dve", qsa[:, :, 0:64], qT[:, :, 0:64], EG[:, :, 0:64], ALU.mult, [qT, EG], [qsa])
            self.tt("pool", qsb[:, :, 64:128], qT[:, :, 64:128], EG[:, :, 64:128], ALU.mult, [qT, EG], [qsb])
            for g in range(2):
                po = self.psum(6 + g)
                for hh in range(4):
                    h = 4 * g + hh
                    self.mm(v4(po, 0)[:, hh, :], QKm[:, h, :], vn[:, h, :], True, False, [QKm, vn], [po])
                    self.mm(v4(po, 0)[:, hh, :], qsa[:, h, :], Sabf[:, h, :], False, False, [qsa, Sabf], [po])
                    self.mm(v4(po, 0)[:, hh, :], qsb[:, h, :], Sbbf[:, h, :], False, True, [qsb, Sbbf], [po])
                self.cp("act", o[:, 4 * g:4 * g + 4, :], v4(po, 0), [po], [o])
            for (F, Fb, egl, rows, bk) in ((Sa, Sabf, eglA, slice(0, 64), 0), (Sb, Sbbf, eglB, slice(64, 128), 2)):
                self.tt("pool", F[:], F[:], bcl(egl[:]), ALU.mult, [F, egl], [F])
                for g in range(2):
                    pss = self.psum(bk + g)
                    for hh in range(4):
                        h = 4 * g + hh
                        self.mm(v4(pss, 0)[:, hh, :], Kd[rows, h, :], vn[rows, h, :], True, True, [Kd, vn], [pss])
                    hs = slice(4 * g, 4 * g + 4)
                    self.tt("dve", F[:, hs, :], F[:, hs, :], v4(pss, 0), ALU.add, [F, pss], [F])
                self.cp("act", Fb[:], F[:], [F], [Fb])
            self.tt("pool", sq[:], o[:], o[:], ALU.mult, [o], [sq])
            self.op("dve", lambda e: e.tensor_reduce(out=ssq[:], in_=sq[:], axis=AX.X, op=ALU.add), [sq], [ssq])
            self.act(ssq[:], ssq[:], AF.Sqrt, [ssq, epsc], [ssq], bias=epsc[:], scale=1.0 / 128)
            self.op("dve", lambda e: e.reciprocal(out=ssq[:], in_=ssq[:]), [ssq], [ssq])
            self.tt("dve", on[:], o[:], bcl(ssq[:]), ALU.mult, [o, ssq], [on])
            pT = self.psum(4, BF16)
            for h in range(8):
                self.tr(v8(pT)[:, h, :], on[:, h, :], self.ident[:], [on, self.ident], [pT])
            self.act(gsz[:], zT[:], AF.Silu, [zT], [gsz])
            self.stt(mixed[:], v8(pT), gain[:, 0:1], gsz[:], ALU.mult, ALU.mult, [pT, gain, gsz], [mixed])
            for slot, col in ((0, ca), (1, cb)):
                if sample and slot == 1:
                    continue
                self.dma("sp", MXv[:, 0:8, col:col + 64], mixed[:, :, 64 * slot:64 * slot + 64], r=[mixed])
            if pi == 64:
                self.dma("sp", self.dout["delta_p"][i, 0].rearrange("h k v -> k h v"), Sa[:], r=[Sa])
                self.dma("sp", self.dout["delta_p"][i, 1].rearrange("h k v -> k h v"), Sb[:], r=[Sb])
            if sample:
                self.dma("sp", self.dout["delta_s"][i].rearrange("h k v -> k h v"), Sa[:], r=[Sa])
        self.P.barrier()


    def phaseB_s5(self, l):
        i = l // 2
        self.reset_arena()
        S = self.scr
        sb = self.sb
        TW = 516
        cosT = sb([16, TW], F32)
        sinT = sb([16, TW], F32)

        def t16():
            return sb([16], F32)
        lr, li, dt, mag, th, kk, tmp, sn, ch, cs, ar, ai, den, cr, ci, t2_ = (t16() for _ in range(16))
        D_ = self.din
        self.dma("sp", lr[:], D_["ssm_lambda_re"][i].rearrange("(sc gl) p -> (gl p) sc", gl=2), w=[lr], allow_slow_non_contiguous=True)
        self.dma("sp", li[:], D_["ssm_lambda_im"][i].rearrange("(sc gl) p -> (gl p) sc", gl=2), w=[li], allow_slow_non_contiguous=True)
        ldv = D_["ssm_log_dt"][i].rearrange("(sc gl) -> gl sc", gl=2)
        for gl in range(2):
            self.dma("sp", dt[64 * gl:64 * gl + 64, :], ldv[gl:gl + 1, :].to_broadcast([64, 16]), w=[dt], allow_slow_non_contiguous=True)
        self.act(dt[:], dt[:], AF.Exp, [dt], [dt])
        self.ts("dve", lr[:], lr[:], -1e-4, ALU.min, [lr], [lr])
        self.tt("dve", mag[:], lr[:], dt[:], ALU.mult, [lr, dt], [mag])
        self.act(mag[:], mag[:], AF.Exp, [mag], [mag])
        self.tt("dve", th[:], li[:], dt[:], ALU.mult, [li, dt], [th])
        self.memset("dve", kk[:], 0.0, [kk])
        for n_ in range(10):
            self.ts("dve", tmp[:], th[:], (n_ + 0.5) * 2 * math.pi, ALU.is_gt, [th], [tmp])
            self.tt("dve", kk[:], kk[:], tmp[:], ALU.add, [kk, tmp], [kk])
        self.stt(th[:], kk[:], -2 * math.pi, th[:], ALU.mult, ALU.add, [kk, th], [th])
        self.act(sn[:], th[:], AF.Sin, [th], [sn])
        self.act(ch[:], th[:], AF.Sin, [th], [ch], scale=0.5)
        self.tt("dve", cs[:], ch[:], ch[:], ALU.mult, [ch], [cs])
        self.ts("dve", cs[:], cs[:], -2.0, ALU.mult, [cs], [cs], s2=1.0, op1=ALU.add)
        self.tt("dve", ar[:], mag[:], cs[:], ALU.mult, [mag, cs], [ar])
        self.tt("dve", ai[:], mag[:], sn[:], ALU.mult, [mag, sn], [ai])
        self.tt("dve", den[:], lr[:], lr[:], ALU.mult, [lr], [den])
        self.tt("dve", tmp[:], li[:], li[:], ALU.mult, [li], [tmp])
        self.tt("dve", den[:], den[:], tmp[:], ALU.add, [den, tmp], [den])
        self.op("dve", lambda e: e.reciprocal(out=den[:], in_=den[:]), [den], [den])
        nr = t2_
        self.ts("dve", nr[:], ar[:], -1.0, ALU.add, [ar], [nr])
        self.tt("dve", cr[:], nr[:], lr[:], ALU.mult, [nr, lr], [cr])
        self.tt("dve", tmp[:], ai[:], li[:], ALU.mult, [ai, li], [tmp])
        self.tt("dve", cr[:], cr[:], tmp[:], ALU.add, [cr, tmp], [cr])
        self.tt("dve", cr[:], cr[:], den[:], ALU.mult, [cr, den], [cr])
        self.tt("dve", ci[:], ai[:], lr[:], ALU.mult, [ai, lr], [ci])
        self.tt("dve", tmp[:], nr[:], li[:], ALU.mult, [nr, li], [tmp])
        self.tt("dve", ci[:], ci[:], tmp[:], ALU.subtract, [ci, tmp], [ci])
        self.tt("dve", ci[:], ci[:], den[:], ALU.mult, [ci, den], [ci])
        cm, sm, c2, s2 = t16(), t16(), t16(), t16()
        self.cp("dve", cm[:], cs[:], [cs], [cm])
        self.cp("dve", sm[:], sn[:], [sn], [sm])
        self.memset("dve", cosT[:, :, 0:1], 1.0, [cosT])
        self.memset("dve", sinT[:, :, 0:1], 0.0, [sinT])
        mark_ = self.aoff
        tA = sb([16, 512], F32)
        tB = sb([16, 512], F32)
        m = 1
        while m <= 512:
            w_ = min(m, 513 - m)

            def bc(ap, w_=w_):
                return ap.unsqueeze(2).to_broadcast([128, 16, w_])
            self.tt("dve", tA[:, :, 0:w_], cosT[:, :, 0:w_], bc(cm[:]), ALU.mult, [cosT, cm], [tA])
            self.tt("pool", tB[:, :, 0:w_], sinT[:, :, 0:w_], bc(sm[:]), ALU.mult, [sinT, sm], [tB])
            self.tt("dve", cosT[:, :, m:m + w_], tA[:, :, 0:w_], tB[:, :, 0:w_], ALU.subtract, [tA, tB], [cosT])
            self.tt("dve", tA[:, :, 0:w_], sinT[:, :, 0:w_], bc(cm[:]), ALU.mult, [sinT, cm], [tA])
            self.tt("pool", tB[:, :, 0:w_], cosT[:, :, 0:w_], bc(sm[:]), ALU.mult, [cosT, sm], [tB])
            self.tt("dve", sinT[:, :, m:m + w_], tA[:, :, 0:w_], tB[:, :, 0:w_], ALU.add, [tA, tB], [sinT])
            self.tt("dve", c2[:], cm[:], cm[:], ALU.mult, [cm], [c2])
            self.tt("dve", s2[:], sm[:], sm[:], ALU.mult, [sm], [s2])
            self.tt("dve", s2[:], c2[:], s2[:], ALU.subtract, [c2, s2], [s2])
            self.tt("dve", c2[:], cm[:], sm[:], ALU.mult, [cm, sm], [c2])
            self.ts("dve", sm[:], c2[:], 2.0, ALU.mult, [c2], [sm])
            self.cp("dve", cm[:], s2[:], [s2], [cm])
            m *= 2
        self.P.barrier()
        self.aoff = mark_
        bre = sb([16, 16], F32)
        bim = sb([16, 16], F32)
        self.dma("sp", bre[:], D_["ssm_b_re"][i].rearrange("(sc gl) p m -> (gl p) sc m", gl=2), w=[bre])
        self.dma("sp", bim[:], D_["ssm_b_im"][i].rearrange("(sc gl) p m -> (gl p) sc m", gl=2), w=[bim])
        bbr = sb([16, 16], F32)
        bbi = sb([16, 16], F32)
        tC = sb([16, 16], F32)

        def bcB(ap):
            return ap.unsqueeze(2).to_broadcast([128, 16, 16])
        self.tt("dve", bbr[:], bre[:], bcB(cr[:]), ALU.mult, [bre, cr], [bbr])
        self.tt("dve", tC[:], bim[:], bcB(ci[:]), ALU.mult, [bim, ci], [tC])
        self.tt("dve", bbr[:], bbr[:], tC[:], ALU.subtract, [bbr, tC], [bbr])
        self.tt("dve", bbi[:], bim[:], bcB(cr[:]), ALU.mult, [bim, cr], [bbi])
        self.tt("dve", tC[:], bre[:], bcB(ci[:]), ALU.mult, [bre, ci], [tC])
        self.tt("dve", bbi[:], bbi[:], tC[:], ALU.add, [bbi, tC], [bbi])
        blk = sb([16, 128], BF16)
        BTr = sb([16, 128], BF16)
        BTi = sb([16, 128], BF16)
        for (bb, BT) in ((bbr, BTr), (bbi, BTi)):
            self.memset("pool", blk[:], 0.0, [blk])
            for r in range(4):
                self.cp("dve", blk[0:64, r::4, r * 32:r * 32 + 16], bb[0:64, r::4, :], [bb], [blk])
                self.cp("dve", blk[64:128, r::4, r * 32 + 16:r * 32 + 32], bb[64:128, r::4, :], [bb], [blk])
            for g in range(2):
                pt = self.psum(g, BF16)
                pv = pt.ap.rearrange("p (a b) -> p a b", a=8)
                for j in range(8):
                    self.tr(pv[:, j, :], blk[:, 8 * g + j, :], self.ident[:], [blk, self.ident], [pt])
                self.cp("act", BT[:, 8 * g:8 * g + 8, :], pv, [pt], [BT])
        evm = sb([1], F32)
        odm = sb([1], F32)
        self.memset("pool", evm[:], 0.0, [evm])
        self.memset("pool", odm[:], 1.0, [odm])
        for p0 in (0, 32, 64, 96):
            self.memset("pool", evm[p0:p0 + 16, :], 1.0, [evm])
            self.memset("pool", odm[p0:p0 + 16, :], 0.0, [odm])
        Cl = sb([4, 64], F32)
        Cin = sb([4, 128], BF16)
        CT = sb([4, 128], BF16)
        CTr = sb([16, 128], BF16)
        CTi = sb([16, 128], BF16)
        for (name, CTm, sgn) in (("ssm_c_re", CTr, 1.0), ("ssm_c_im", CTi, -1.0)):
            self.dma("sp", Cl[:], D_[name][i].rearrange("(q g) m p -> (g m) q p", q=4), w=[Cl])
            self.ts("dve", Cin[:, :, 0:64], Cl[:], evm[:, 0:1], ALU.mult, [Cl, evm], [Cin], s2=sgn, op1=ALU.mult)
            self.ts("dve", Cin[:, :, 64:128], Cl[:], odm[:, 0:1], ALU.mult, [Cl, odm], [Cin], s2=sgn, op1=ALU.mult)
            pt = self.psum(2, BF16)
            pv = pt.ap.rearrange("p (a b) -> p a b", a=8)
            for q in range(4):
                self.tr(pv[:, q, :], Cin[:, q, :], self.ident[:], [Cin, self.ident], [pt])
            self.cp("act", CT[:], pv[:, 0:4, :], [pt], [CT])
            self.memset("pool", CTm[:], 0.0, [CTm])
            for r in range(4):
                self.cp("dve", CTm[:, r::4, r * 32:r * 32 + 32], CT[:, :, r * 32:r * 32 + 32], [CT], [CTm])
        dsk = sb([4], F32)
        self.dma("sp", dsk[:], D_["ssm_d"][i].rearrange("(q p) -> p q", p=128), w=[dsk], allow_slow_non_contiguous=True)
        glb = sb([4], F32)
        self.dma("sp", glb[:], D_["ssm_glu_b"][i].rearrange("(q p) -> p q", p=128), w=[glb], allow_slow_non_contiguous=True)
        glw = sb([4, 512], BF16)
        self.load_weight(D_["ssm_glu_w"][i], 512, 512, glw, None, stage_cols=512)
        uTs = [sb([4, 512], BF16), sb([4, 512], BF16)]
        t1s = [sb([512], F32) for _ in range(2)]
        t2s = [sb([512], F32) for _ in range(2)]
        t3s = [sb([512], F32) for _ in range(2)]
        t4s = [sb([512], F32) for _ in range(2)]
        wrs = [sb([512], F32) for _ in range(2)]
        wis = [sb([512], F32) for _ in range(2)]
        xrs = [sb([512], BF16) for _ in range(2)]
        xis = [sb([512], BF16) for _ in range(2)]
        yv = sb([512], F32)
        g1 = sb([512], F32)
        hg = sb([4, 512], BF16)
        sg = sb([512], F32)
        obs = [sb([512], BF16) for _ in range(2)]
        wlr, wli, inr, ini, t16a, t16b = (t16() for _ in range(6))
        UTv = S["UT"].rearrange("(q p) t -> p q t", p=128)
        it = 0
        for si, seg in enumerate(SEGS):
            r0, n, s, t0, pc0 = seg
            first = (t0 == 0)
            lastseg = (t0 + n == SEQLEN[s])
            uT = uTs[si % 2]
            self.dma("sp", uT[:, :, 0:n], UTv[:, :, pc0:pc0 + n], w=[uT])
            if first:
                if s < 2:
                    self.memset("dve", inr[:], 0.0, [inr])
                    self.memset("dve", ini[:], 0.0, [ini])
                else:
                    self.dma("sp", wlr[:], D_["st_re"][i].rearrange("(sc gl) p -> (gl p) sc", gl=2), w=[wlr], allow_slow_non_contiguous=True)
                    self.dma("sp", wli[:], D_["st_im"][i].rearrange("(sc gl) p -> (gl p) sc", gl=2), w=[wli], allow_slow_non_contiguous=True)
                    jprev = 1
            if not first or s == 2:
                jp = jprev if (first and s == 2) else nprev
                ec, es = cosT[:, :, jp], sinT[:, :, jp]
                self.tt("dve", t16a[:], wlr[:], ec, ALU.mult, [wlr, cosT], [t16a])
                self.tt("dve", t16b[:], wli[:], es, ALU.mult, [wli, sinT], [t16b])
                self.tt("dve", inr[:], t16a[:], t16b[:], ALU.subtract, [t16a, t16b], [inr])
                self.tt("dve", t16a[:], wli[:], ec, ALU.mult, [wli, cosT], [t16a])
                self.tt("dve", t16b[:], wlr[:], es, ALU.mult, [wlr, sinT], [t16b])
                self.tt("dve", ini[:], t16a[:], t16b[:], ALU.add, [t16a, t16b], [ini])
            nprev = n
            for q in range(4):
                py = self.psum(4 + q % 2)
                for sl in range(4):
                    sc = 4 * q + sl
                    k2 = it % 2
                    it += 1
                    t1, t2, t3, t4, wr, wi, xr, xi = t1s[k2], t2s[k2], t3s[k2], t4s[k2], wrs[k2], wis[k2], xrs[k2], xis[k2]
                    pbr = self.psum(0 + k2)
                    pbi = self.psum(2 + k2)
                    self.mm(pbr[:, 0:n], BTr[:, sc, :], uT[:, q, 0:n], True, True, [BTr, uT], [pbr])
                    self.mm(pbi[:, 0:n], BTi[:, sc, :], uT[:, q, 0:n], True, True, [BTi, uT], [pbi])
                    cT, sT = cosT[:, sc, 0:n], sinT[:, sc, 0:n]
                    self.tt("dve", t1[:, 0:n], pbr[:, 0:n], cT, ALU.mult, [pbr, cosT], [t1])
                    self.tt("dve", t2[:, 0:n], pbi[:, 0:n], sT, ALU.mult, [pbi, sinT], [t2])
                    self.tt("dve", t3[:, 0:n], pbi[:, 0:n], cT, ALU.mult, [pbi, cosT], [t3])
                    self.tt("dve", t4[:, 0:n], pbr[:, 0:n], sT, ALU.mult, [pbr, sinT], [t4])
                    self.tt("pool", t1[:, 0:n], t1[:, 0:n], t2[:, 0:n], ALU.add, [t1, t2], [t1])
                    self.tt("pool", t3[:, 0:n], t3[:, 0:n], t4[:, 0:n], ALU.subtract, [t3, t4], [t3])
                    mg = mag[:, sc:sc + 1].to_broadcast([128, n])
                    self.op("dve", lambda e, o_=wr[:, 0:n], d0=mg, d1=t1[:, 0:n], in_=inr[:, sc:sc + 1]:
                            e.tensor_tensor_scan(out=o_, data0=d0, data1=d1, initial=in_, op0=ALU.mult, op1=ALU.add),
                            [t1, mag, inr], [wr], cost=0.12 + 2 * n / 960.0)
                    self.op("dve", lambda e, o_=wi[:, 0:n], d0=mg, d1=t3[:, 0:n], in_=ini[:, sc:sc + 1]:
                            e.tensor_tensor_scan(out=o_, data0=d0, data1=d1, initial=in_, op0=ALU.mult, op1=ALU.add),
                            [t3, mag, ini], [wi], cost=0.12 + 2 * n / 960.0)
                    self.cp("pool", wlr[:, sc:sc + 1], wr[:, n - 1:n], [wr], [wlr])
                    self.cp("pool", wli[:, sc:sc + 1], wi[:, n - 1:n], [wi], [wli])
                    self.tt("pool", t2[:, 0:n], wr[:, 0:n], cT, ALU.mult, [wr, cosT], [t2])
                    self.tt("pool", t4[:, 0:n], wi[:, 0:n], sT, ALU.mult, [wi, sinT], [t4])
                    self.tt("dve", xr[:, 0:n], t2[:, 0:n], t4[:, 0:n], ALU.subtract, [t2, t4], [xr])
                    self.tt("pool", t2[:, 0:n], wi[:, 0:n], cT, ALU.mult, [wi, cosT], [t2])
                    self.tt("pool", t4[:, 0:n], wr[:, 0:n], sT, ALU.mult, [wr, sinT], [t4])
                    self.tt("dve", xi[:, 0:n], t2[:, 0:n], t4[:, 0:n], ALU.add, [t2, t4], [xi])
                    self.mm(py[:, 0:n], CTr[:, sc, :], xr[:, 0:n], sl == 0, False, [CTr, xr], [py])
                    self.mm(py[:, 0:n], CTi[:, sc, :], xi[:, 0:n], False, sl == 3, [CTi, xi], [py])
                self.stt(yv[:, 0:n], uT[:, q, 0:n], dsk[:, q:q + 1], py[:, 0:n], ALU.mult, ALU.add, [uT, dsk, py], [yv])
                self.tt("pool", g1[:, 0:n], yv[:, 0:n], yv[:, 0:n], ALU.mult, [yv], [g1])
                self.ts("dve", g1[:, 0:n], g1[:, 0:n], 0.044715, ALU.mult, [g1], [g1], s2=1.0, op1=ALU.add)
                self.tt("pool", g1[:, 0:n], g1[:, 0:n], yv[:, 0:n], ALU.mult, [g1, yv], [g1])
                self.act(g1[:, 0:n], g1[:, 0:n], AF.Sigmoid, [g1], [g1], scale=2.0 * math.sqrt(2.0 / math.pi))
                self.tt("dve", hg[:, q, 0:n], yv[:, 0:n], g1[:, 0:n], ALU.mult, [yv, g1], [hg])
            for qo in range(4):
                pg = self.psum(6 + qo % 2)
                for qi in range(4):
                    self.mm(pg[:, 0:n], glw[:, qi, qo * 128:(qo + 1) * 128], hg[:, qi, 0:n], qi == 0, qi == 3, [glw, hg], [pg])
                self.act(sg[:, 0:n], pg[:, 0:n], AF.Sigmoid, [pg, glb], [sg], bias=glb[:, qo:qo + 1])
                ob = obs[qo % 2]
                self.tt("dve", ob[:, 0:n], hg[:, qo, 0:n], sg[:, 0:n], ALU.mult, [hg, sg], [ob])
                self.dma("sp", S["MIXT"][1024 + qo * 128:1024 + (qo + 1) * 128, pc0:pc0 + n], ob[:, 0:n], r=[ob])
            if lastseg:
                ec, es = cosT[:, :, n - 1], sinT[:, :, n - 1]
                fr, fi = t16(), t16()
                self.tt("dve", t16a[:], wlr[:], ec, ALU.mult, [wlr, cosT], [t16a])
                self.tt("dve", t16b[:], wli[:], es, ALU.mult, [wli, sinT], [t16b])
                self.tt("dve", fr[:], t16a[:], t16b[:], ALU.subtract, [t16a, t16b], [fr])
                self.tt("dve", t16a[:], wli[:], ec, ALU.mult, [wli, cosT], [t16a])
                self.tt("dve", t16b[:], wlr[:], es, ALU.mult, [wlr, sinT], [t16b])
                self.tt("dve", fi[:], t16a[:], t16b[:], ALU.add, [t16a, t16b], [fi])
                if s < 2:
                    dr, di = self.dout["re_p"][i, s], self.dout["im_p"][i, s]
                else:
                    dr, di = self.dout["re_s"][i], self.dout["im_s"][i]
                self.dma("sp", dr.rearrange("(sc gl) p -> (gl p) sc", gl=2), fr[:], r=[fr], allow_slow_non_contiguous=True)
                self.dma("sp", di.rearrange("(sc gl) p -> (gl p) sc", gl=2), fi[:], r=[fi], allow_slow_non_contiguous=True)
        self.P.barrier()


    def phaseC_outproj(self, l, wdram, KC):
        self.reset_arena()
        sb = self.sb
        wo = sb([KC, D], BF16)
        self.load_weight(wdram, KC * 128, D, wo, None, stage_cols=1024)
        xts = [sb([4, D], F32), sb([4, D], F32)]
        mts = [sb([KC, 512], BF16), sb([KC, 512], BF16)]
        MXv = self.scr["MIXT"].rearrange("(c p) t -> p c t", p=128)
        k = 0
        for si, seg in enumerate(SEGS):
            r0, n, s_, t0, pc0 = seg
            xt, mt = xts[si % 2], mts[si % 2]
            self.load_x(r0, n, xt)
            self.dma("sp", mt[:, :, 0:n], MXv[:, 0:KC, pc0:pc0 + n], w=[mt])
            nsub = (n + 127) // 128
            pj = min(128, n)
            for j in range(nsub):
                for dh in range(2):
                    ps = self.psum(k % 8)
                    k += 1
                    for c in range(KC):
                        self.mm(ps[0:pj, :], mt[:, c, j * 128:j * 128 + pj], wo[:, c, dh * 512:(dh + 1) * 512], c == 0, c == KC - 1,
                                [mt, wo], [ps])
                    self.tt("dve", xt[0:pj, j, dh * 512:(dh + 1) * 512], xt[0:pj, j, dh * 512:(dh + 1) * 512], ps[0:pj, :], ALU.add,
                            [xt, ps], [xt])
            self.store_x(r0, n, xt)
        self.P.barrier()

    def phaseD_ffn(self, l):
        self.reset_arena()
        sb = self.sb
        self.epsc = sb([1], F32)
        self.memset("dve", self.epsc[:], RMS_EPS, [self.epsc])
        w1 = sb([8, FFN], BF16)
        w3 = sb([8, FFN], BF16)
        w2 = sb([22, D], BF16)
        gain = self.load_gain(self.din["norm_ffn"][l])
        mark_ = self.aoff
        self.load_weight(self.din["ffn_w1"][l], D, FFN, w1, gain, stage_cols=1408)
        self.load_weight(self.din["ffn_w3"][l], D, FFN, w3, gain, stage_cols=1408)
        self.load_weight(self.din["ffn_w2"][l], FFN, D, w2, None, stage_cols=1024)
        self.P.barrier()
        self.aoff = mark_
        final = (l == DEPTH - 1)
        if final:
            gfin = sb([D], F32)
            self.dma("sp", gfin[:], self.din["norm_final"].rearrange("(o d) -> o d", o=1).to_broadcast([128, D]), w=[gfin])
        xt = sb([2, D], F32)
        h = sb([2, D], BF16)
        junk = sb([D], BF16)
        ss = sb([4], F32)
        hT = sb([8, 256], BF16)
        aT = sb([22, 256], BF16)
        sgs = [sb([256], F32), sb([256], F32)]
        segs = []
        for (r0, n, s_, t0, pc0) in SEGS:
            for o_ in range(0, n, 256):
                segs.append((r0 + o_, min(256, n - o_), s_, t0 + o_))
        k = 0
        for si, seg in enumerate(segs):
            r0, n, s_, t0 = seg
            nsub = (n + 127) // 128
            pj = min(128, n)
            self.norm_transpose(seg, xt, h, junk, ss, hT, bank=[0, 1])
            for jh in range(22):
                pg = self.psum(2 + (jh % 2))
                pu = self.psum(4 + (jh % 2))
                for c in range(8):
                    self.mm(pg[:, 0:n], w1[:, c, jh * 128:(jh + 1) * 128], hT[:, c, 0:n], c == 0, c == 7, [w1, hT], [pg])
                for c in range(8):
                    self.mm(pu[:, 0:n], w3[:, c, jh * 128:(jh + 1) * 128], hT[:, c, 0:n], c == 0, c == 7, [w3, hT], [pu])
                sg = sgs[jh % 2]
                self.act(sg[:, 0:n], pg[:, 0:n], AF.Silu, [pg], [sg])
                self.tt("dve", aT[:, jh, 0:n], sg[:, 0:n], pu[:, 0:n], ALU.mult, [sg, pu], [aT])
            for j in range(nsub):
                for dh in range(2):
                    ps = self.psum(6 + k % 2)
                    k += 1
                    for jh in range(22):
                        self.mm(ps[0:pj, :], aT[:, jh, j * 128:j * 128 + pj], w2[:, jh, dh * 512:(dh + 1) * 512], jh == 0, jh == 21,
                                [aT, w2], [ps])
                    self.tt("dve", xt[0:pj, j, dh * 512:(dh + 1) * 512], xt[0:pj, j, dh * 512:(dh + 1) * 512], ps[0:pj, :], ALU.add,
                            [xt, ps], [xt])
            if not final:
                self.store_x(r0, n, xt)
            else:
                for j in range(nsub):
                    self.act(junk[0:pj, :], xt[0:pj, j, :], AF.Square, [xt], [junk, ss], accum=ss[0:pj, j:j + 1])
                self.act(ss[0:pj, 0:nsub], ss[0:pj, 0:nsub], AF.Sqrt, [ss], [ss], bias=self.epsc[0:pj, :], scale=1.0 / D)
                self.op("dve", lambda e, pj=pj, nsub=nsub: e.reciprocal(out=ss[0:pj, 0:nsub], in_=ss[0:pj, 0:nsub]), [ss], [ss])
                for j in range(nsub):
                    self.stt(xt[0:pj, j, :], xt[0:pj, j, :], ss[0:pj, j:j + 1], gfin[0:pj, :], ALU.mult, ALU.mult, [xt, ss, gfin], [xt])
                if s_ == 2:
                    self.dma("pool", self.dout["y_s"][t0:t0 + n, :], xt[0:n, 0, :], r=[xt])
                elif t0 >= 16:
                    self.dma("pool", self.dout["y_p"][s_, t0 - 16:t0 - 16 + n, :].rearrange("(j p) d -> p j d", p=128),
                             xt[:, 0:nsub, :], r=[xt])
        self.P.barrier()

    def phaseA_odd(self, l):
        i = l // 2
        self.reset_arena()
        sb = self.sb
        D_ = self.din
        S = self.scr
        self.epsc = sb([1], F32)
        self.memset("dve", self.epsc[:], RMS_EPS, [self.epsc])
        wbf = sb([8, ODD_PROJ], BF16)
        gain = self.load_gain(D_["norm_mix"][l])
        mark_ = self.aoff
        self.load_weight(D_["w_in_odd"][i], D, ODD_PROJ, wbf, gain, stage_cols=1540)
        self.P.barrier()
        self.aoff = mark_
        fb = sb([8], F32)
        self.dma("sp", fb[:], D_["fox_f_bias"][i:i + 1, :].to_broadcast([128, 8]), w=[fb])
        posf = sb([34], F32)
        self.op("pool", lambda e: e.iota(posf[:, 0:32], pattern=[[128, 32]], base=16, channel_multiplier=1,
                                         allow_small_or_imprecise_dtypes=True), (), [posf])
        self.op("pool", lambda e: e.iota(posf[:, 32:33], pattern=[[1, 1]], base=0, channel_multiplier=1,
                                         allow_small_or_imprecise_dtypes=True), (), [posf])
        self.op("pool", lambda e: e.iota(posf[:, 33:34], pattern=[[1, 1]], base=4096, channel_multiplier=1,
                                         allow_small_or_imprecise_dtypes=True), (), [posf])
        invf = sb([8], F32)
        for f_ in range(8):
            self.memset("dve", invf[:, f_:f_ + 1], 500000.0 ** (-f_ / 8.0), [invf])
        ang = sb([34, 8], F32)
        kq = sb([34, 8], F32)
        ki = sb([34, 8], mybir.dt.int32)
        msk = sb([34, 8], F32)
        cosR = sb([34, 8], F32)
        sinR = sb([34, 8], F32)
        self.tt("dve", ang[:], posf[:].unsqueeze(2).to_broadcast([128, 34, 8]), invf[:].unsqueeze(1).to_broadcast([128, 34, 8]),
                ALU.mult, [posf, invf], [ang])
        self.ts("dve", kq[:], ang[:], 1.0 / (2 * math.pi), ALU.mult, [ang], [kq])
        self.cp("dve", ki[:], kq[:], [kq], [ki])
        self.cp("dve", kq[:], ki[:], [ki], [kq])
        self.stt(ang[:], kq[:], -2 * math.pi, ang[:], ALU.mult, ALU.add, [kq, ang], [ang])
        self.ts("dve", msk[:], ang[:], math.pi, ALU.is_gt, [ang], [msk])
        self.stt(ang[:], msk[:], -2 * math.pi, ang[:], ALU.mult, ALU.add, [msk, ang], [ang])
        self.ts("dve", msk[:], ang[:], -math.pi, ALU.is_lt, [ang], [msk])
        self.stt(ang[:], msk[:], 2 * math.pi, ang[:], ALU.mult, ALU.add, [msk, ang], [ang])
        self.act(sinR[:], ang[:], AF.Sin, [ang], [sinR])
        self.act(cosR[:], ang[:], AF.Sin, [ang], [cosR], scale=0.5)
        self.tt("dve", cosR[:], cosR[:], cosR[:], ALU.mult, [cosR], [cosR])
        self.ts("dve", cosR[:], cosR[:], -2.0, ALU.mult, [cosR], [cosR], s2=1.0, op1=ALU.add)
        xts = [sb([4, D], F32), sb([4, D], F32)]
        h = sb([4, D], BF16)
        junk = sb([D], BF16)
        sss = [sb([4], F32), sb([4], F32)]
        hTs = [sb([8, 512], BF16), sb([8, 512], BF16)]
        oks = [sb([512], F32) for _ in range(4)]
        okbs = [sb([512], BF16) for _ in range(4)]
        kts = [sb([4, 128], BF16) for _ in range(4)]
        rts = [sb([8, 8], F32) for _ in range(3)]
        lfs = [sb([8], F32) for _ in range(2)]
        cms = [sb([8], F32) for _ in range(2)]
        carry = sb([8], F32)
        cst = [sb([512], F32) for _ in range(2)]
        cache_names = (("c_fk", "KFT", True), ("c_dk", "KDT", True), ("c_fv", "VF", False), ("c_dv", "VD", False))
        st = dict(ko=0, kb=0, kt=0, cc=0)

        def cum_step(lf, pj, row0):
            cm = cms[st["cc"] % 2]
            ps = self.psum(6 + st["cc"] % 2)
            st["cc"] += 1
            self.mm(ps[0:pj, 0:8], self.trif[0:pj, 0:pj], lf[0:pj, :], True, False, [self.trif, lf], [ps])
            self.mm(ps[0:pj, 0:8], self.identf[0:pj, 0:pj], carry[0:pj, :], False, True, [self.identf, carry], [ps])
            self.mm(ps[:, 8:16], self.onesf[0:pj, :], lf[0:pj, :], True, True, [self.onesf, lf], [ps])
            self.cp("dve", cm[0:pj, :], ps[0:pj, 0:8], [ps], [cm])
            self.dma("pool", S["CUM"][row0:row0 + pj, :], cm[0:pj, :], r=[cm])
            self.tt("dve", carry[:], carry[:], ps[:, 8:16], ALU.add, [carry, ps], [carry])

        def to_featmajor(okb, pj, dst, col0):
            pT = self.psum(st["kt"] % 2, BF16)
            kt = kts[st["kt"] % 4]
            st["kt"] += 1
            pv = pT.ap.rearrange("p (c t) -> p c t", c=8)
            for c in range(4):
                self.tr(pv[:, c, 0:pj], okb[0:pj, c * 128:(c + 1) * 128], self.ident[0:pj, 0:pj], [okb, self.ident], [pT])
            self.cp("dve", kt[:, :, 0:pj], pv[:, 0:4, 0:pj], [pT], [kt])
            self.dma("sp", dst.rearrange("(c p) t -> p c t", p=128)[:, :, col0:col0 + pj], kt[:, :, 0:pj], r=[kt])

        for si, seg in enumerate(SEGS):
            r0, n, s_, t0, pc0 = seg
            if t0 == 0:
                self.memset("dve", carry[:], 0.0, [carry])
            if s_ == 2:
                for m in range(32):
                    for (cn, dn, isk) in cache_names:
                        ok = oks[st["ko"] % 4]
                        okb = okbs[st["ko"] % 4]
                        st["ko"] += 1
                        self.dma("sp", ok[:], D_[cn][i, 128 * m:128 * (m + 1)].rearrange("t a b -> t (a b)"), w=[ok])
                        self.cp("act" if isk else "pool", okb[:], ok[:], [ok], [okb])
                        if isk:
                            to_featmajor(okb, 128, S[dn], AK[2] + 128 * m)
                        else:
                            self.dma("pool", S[dn][AK[2] + 128 * m:AK[2] + 128 * (m + 1), :], okb[:], r=[okb])
                    lf = lfs[m % 2]
                    self.dma("sp", lf[:], D_["c_fl"][i, 128 * m:128 * (m + 1), :], w=[lf])
                    cum_step(lf, 128, CB[2] + 1 + 128 * m)
            koff = 4096 if s_ == 2 else 0
            xt, ss, hT = xts[si % 2], sss[si % 2], hTs[si % 2]
            self.norm_transpose(seg, xt, h, junk, ss, hT, bank=[0, 1])
            nsub = (n + 127) // 128
            pj = min(128, n)
            for j in range(nsub):
                tcol = 33 if s_ == 2 else (32 if t0 == 0 else (t0 - 16) // 128 + j)
                tok0 = t0 + 128 * j
                for (name, col, rope, kind) in (("qf", 0, False, "q"), ("fk", 512, False, "k"), ("fv", 1024, False, "v"),
                                                ("qd", 1544, True, "q"), ("dk", 2056, True, "k"), ("dv", 2568, False, "v")):
                    ps = self.psum(2 + st["ko"] % 4)
                    ok = oks[st["ko"] % 4]
                    okb = okbs[st["ko"] % 4]
                    st["ko"] += 1
                    for c in range(8):
                        self.mm(ps[0:pj, :], hT[:, c, j * 128:j * 128 + pj], wbf[:, c, col:col + 512], c == 0, c == 7, [hT, wbf], [ps])
                    self.cp("act", ok[0:pj, :], ps[0:pj, :], [ps], [ok])
                    if rope:
                        okv = ok.ap.rearrange("p (h d) -> p h d", h=8)
                        cb = cosR[0:pj, tcol, :].unsqueeze(1).to_broadcast([pj, 8, 8])
                        sbb = sinR[0:pj, tcol, :].unsqueeze(1).to_broadcast([pj, 8, 8])
                        x1, x2 = okv[0:pj, :, 0:8], okv[0:pj, :, 8:16]
                        rt, r1, r2 = rts
                        self.tt("dve", rt[0:pj], x1, cb, ALU.mult, [ok, cosR], [rt])
                        self.tt("dve", r1[0:pj], x2, sbb, ALU.mult, [ok, sinR], [r1])
                        self.tt("dve", r2[0:pj], x2, cb, ALU.mult, [ok, cosR], [r2])
                        self.tt("dve", x2, x1, sbb, ALU.mult, [ok, sinR], [ok])
                        self.tt("dve", x2, x2, r2[0:pj], ALU.add, [ok, r2], [ok])
                        self.tt("dve", x1, rt[0:pj], r1[0:pj], ALU.subtract, [rt, r1], [ok])
                    if kind != "q":
                        if s_ < 2:
                            dst = self.dout[name + "_p"][i, s_, tok0:tok0 + pj]
                        else:
                            dst = self.dout[name + "_s"][i, tok0:tok0 + pj]
                        self.dma("pool", dst.rearrange("t a b -> t (a b)"), ok[0:pj, :], r=[ok])
                    if kind == "q":
                        self.act(okb[0:pj, :], ok[0:pj, :], AF.Copy, [ok], [okb], scale=0.125)
                        to_featmajor(okb, pj, S["QFT" if name == "qf" else "QDT"], SEQROW[s_] + tok0)
                    elif kind == "k":
                        self.cp("act", okb[0:pj, :], ok[0:pj, :], [ok], [okb])
                        to_featmajor(okb, pj, S["KFT" if name == "fk" else "KDT"], AK[s_] + koff + tok0)
                    else:
                        self.cp("pool", okb[0:pj, :], ok[0:pj, :], [ok], [okb])
                        dn = "VF" if name == "fv" else "VD"
                        self.dma("pool", S[dn][AK[s_] + koff + tok0:AK[s_] + koff + tok0 + pj, :], okb[0:pj, :], r=[okb])
                ps = self.psum(6 + j % 2)
                lf = lfs[j % 2]
                for c in range(8):
                    self.mm(ps[0:pj, 0:8], hT[:, c, j * 128:j * 128 + pj], wbf[:, c, 1536:1544], c == 0, c == 7, [hT, wbf], [ps])
                self.tt("dve", lf[0:pj, :], ps[0:pj, 0:8], fb[0:pj, :], ALU.add, [ps, fb], [lf])
                self.act(lf[0:pj, :], lf[0:pj, :], AF.Exp, [lf], [lf], scale=-1.0)
                self.act(lf[0:pj, :], lf[0:pj, :], AF.Ln, [lf, self.onec], [lf], bias=self.onec[0:pj, :])
                self.ts("dve", lf[0:pj, :], lf[0:pj, :], -1.0, ALU.mult, [lf], [lf])
                if s_ < 2:
                    dst = self.dout["fl_p"][i, s_, tok0:tok0 + pj, :]
                else:
                    dst = self.dout["fl_s"][i, tok0:tok0 + pj, :]
                self.dma("pool", dst, lf[0:pj, :], r=[lf])
                cum_step(lf, pj, CB[s_] + 1 + koff + tok0)
        self.P.barrier()

    def phaseB_attn(self, l):
        i = l // 2
        self.reset_arena()
        sb = self.sb
        D_ = self.din
        S = self.scr
        lam_init = 0.8 - 0.6 * math.exp(-0.3 * l)
        epsc = sb([1], F32)
        self.memset("dve", epsc[:], RMS_EPS, [epsc])
        dl = sb([4, 64], F32)
        self.dma("sp", dl[:], D_["diff_lambda"][i:i + 1].to_broadcast([128, 4, 64]), w=[dl])
        pr = sb([2, 64], F32)
        lsum = sb([2], F32)
        nlam = sb([1], F32)
        self.tt("dve", pr[:, 0, :], dl[:, 0, :], dl[:, 1, :], ALU.mult, [dl], [pr])
        self.tt("dve", pr[:, 1, :], dl[:, 2, :], dl[:, 3, :], ALU.mult, [dl], [pr])
        self.op("dve", lambda e: e.tensor_reduce(out=lsum[:], in_=pr[:], axis=AX.X, op=ALU.add), [pr], [lsum])
        self.act(lsum[:], lsum[:], AF.Exp, [lsum], [lsum])
        self.tt("dve", nlam[:], lsum[:, 1:2], lsum[:, 0:1], ALU.subtract, [lsum], [nlam])
        self.ts("dve", nlam[:], nlam[:], -lam_init, ALU.add, [nlam], [nlam])
        dg = sb([1], F32)
        self.dma("sp", dg[:], D_["diff_out_norm"][i].rearrange("(p o) -> p o", o=1), w=[dg])
        self.ts("dve", dg[:], dg[:], 1.0 - lam_init, ALU.mult, [dg], [dg])
        mfox = [sb([512], BF16) for _ in range(4)]
        mdif = [sb([512], BF16) for _ in range(4)]
        for kt in range(4):
            m_ = mfox[kt]
            self.memset("pool", m_[:], 1.0, [m_])
            self.op("pool", lambda e, m_=m_, kt=kt: e.affine_select(out=m_[:], in_=m_[:], pattern=[[1, 512]], compare_op=ALU.is_ge,
                                                                    fill=0.0, base=-128 * kt, channel_multiplier=-1), [m_], [m_])
            d_ = mdif[kt]
            self.memset("pool", d_[:], 1.0, [d_])
            if kt > 0:
                self.memset("pool", d_[:, 0:128 * kt], 0.0, [d_])
            self.memset("pool", d_[64:128, 128 * kt:128 * kt + 64], 0.0, [d_])
        KTs = [sb([4128], BF16) for _ in range(2)]
        QTs = [sb([4112], BF16) for _ in range(2)]
        Vs = [sb([33, 128], BF16) for _ in range(2)]
        cumT = sb([33, 8], F32)
        crefb = sb([8], F32)
        biasTs = [sb([33, 8], F32) for _ in range(9)]
        pTs = [sb([512], BF16) for _ in range(4)]
        rls = [sb([512], F32) for _ in range(2)]
        o1 = sb([512], F32)
        o2 = sb([512], F32)
        sqb = sb([512], BF16)
        rstd = sb([512], F32)
        mixs = [sb([512], BF16) for _ in range(2)]
        cnt = dict(ld=0, pt=0, mx=0)
        for s_ in range(3):
            nkeys = LP if s_ < 2 else 4096 + LS
            if s_ < 2:
                ktiles = [(0, 16)] + [(16 + 128 * m, 128) for m in range(32)]
            else:
                ktiles = [(128 * m, 128) for m in range(32)] + [(4096, 32)]
            ntile = len(ktiles)
            qsegs = [sg for sg in SEGS if sg[2] == s_]
            for m, (k0, nk) in enumerate(ktiles):
                if nk == 128 and (m == 0 or ktiles[m - 1][1] != 128):
                    m_end = m
                    while m_end < ntile and ktiles[m_end][1] == 128:
                        m_end += 1
                    self.dma("sp", cumT[:, m:m_end, :],
                             S["CUM"][CB[s_] + 1 + k0:CB[s_] + 1 + k0 + 128 * (m_end - m), :].rearrange("(m p) h -> p m h", p=128), w=[cumT])
                elif nk != 128:
                    self.dma("sp", cumT[0:nk, m, :], S["CUM"][CB[s_] + 1 + k0:CB[s_] + 1 + k0 + nk, :], w=[cumT])
            for qi, (r0, nq, _s, t0, pc0) in enumerate(qsegs):
                qkey0 = t0 + (4096 if s_ == 2 else 0)
                qref = qkey0 + nq // 2
                self.dma("sp", crefb[:], S["CUM"][CB[s_] + qref:CB[s_] + qref + 1, :].to_broadcast([128, 8]), w=[crefb])
                self.tt("dve", biasTs[qi][:], crefb[:].unsqueeze(1).to_broadcast([128, 33, 8]), cumT[:], ALU.subtract, [crefb, cumT], [biasTs[qi]])
                self.ts("dve", biasTs[qi][:], biasTs[qi][:], 70.0, ALU.min, [biasTs[qi]], [biasTs[qi]])
            for kind in ("fox", "dif"):
                for c in range(4):
                    KT, QT, V = KTs[cnt["ld"] % 2], QTs[cnt["ld"] % 2], Vs[cnt["ld"] % 2]
                    cnt["ld"] += 1
                    ksrc = S["KFT" if kind == "fox" else "KDT"]
                    qsrc = S["QFT" if kind == "fox" else "QDT"]
                    vsrc = S["VF" if kind == "fox" else "VD"]
                    self.dma("sp", KT[:, 0:nkeys], ksrc[128 * c:128 * (c + 1), AK[s_]:AK[s_] + nkeys], w=[KT])
                    self.dma("sp", QT[:, 0:SEQLEN[s_]], qsrc[128 * c:128 * (c + 1), SEQROW[s_]:SEQROW[s_] + SEQLEN[s_]], w=[QT])
                    for m, (k0, nk) in enumerate(ktiles):
                        if nk == 128 and (m == 0 or ktiles[m - 1][1] != 128):
                            m_end = m
                            while m_end < ntile and ktiles[m_end][1] == 128:
                                m_end += 1
                            self.dma("pool", V[:, m:m_end, :],
                                     vsrc[AK[s_] + k0:AK[s_] + k0 + 128 * (m_end - m), 128 * c:128 * (c + 1)].rearrange("(m p) d -> p m d", p=128), w=[V])
                        elif nk != 128:
                            self.dma("pool", V[0:nk, m, :], vsrc[AK[s_] + k0:AK[s_] + k0 + nk, 128 * c:128 * (c + 1)], w=[V])
                    for qi, (r0, nq, _s, t0, pc0) in enumerate(qsegs):
                        qkey0 = t0 + (4096 if s_ == 2 else 0)
                        tiles = [(m, k0, nk) for m, (k0, nk) in enumerate(ktiles) if k0 < qkey0 + nq]
                        if kind == "fox":
                            accO, accL = self.psum(4), self.psum(5)
                        else:
                            accO, accL, accO2, accL2 = self.psum(4), self.psum(5), self.psum(6), self.psum(7)
                        for ti, (m, k0, nk) in enumerate(tiles):
                            first, lastt = (ti == 0), (ti == len(tiles) - 1)
                            diag = (k0 >= qkey0)
                            pts = []
                            for hl in range(2):
                                pss = self.psum(cnt["pt"] % 4)
                                pT = pTs[cnt["pt"] % 4]
                                cnt["pt"] += 1
                                hp = slice(64 * hl, 64 * hl + 64)
                                self.mm(pss[0:nk, 0:nq], KT[hp, k0:k0 + nk], QT[hp, t0:t0 + nq], True, True, [KT, QT], [pss])
                                if kind == "fox":
                                    hh = 2 * c + hl
                                    self.act(pT[0:nk, 0:nq], pss[0:nk, 0:nq], AF.Exp, [pss, biasTs[qi]], [pT], bias=biasTs[qi][0:nk, m, hh:hh + 1])
                                else:
                                    self.act(pT[0:nk, 0:nq], pss[0:nk, 0:nq], AF.Exp, [pss], [pT])
                                if diag:
                                    kt_ = (k0 - qkey0) // 128
                                    if kind == "fox":
                                        mk = mfox[kt_]
                                    elif nq == 512:
                                        mk = mdif[kt_]
                                    else:
                                        mk = None
                                    if mk is not None:
                                        self.tt("pool", pT[0:nk, 0:nq], pT[0:nk, 0:nq], mk[0:nk, 0:nq], ALU.mult, [pT, mk], [pT])
                                pts.append(pT)
                            if kind == "fox":
                                for hl in range(2):
                                    hp = slice(64 * hl, 64 * hl + 64)
                                    self.mm(accO[hp, 0:nq], V[0:nk, m, hp], pts[hl][0:nk, 0:nq], first, lastt, [V, pts[hl]], [accO])
                                    self.mm(accL[hp, 0:nq], self.ones_bf[0:nk, 0:64], pts[hl][0:nk, 0:nq], first, lastt, [self.ones_bf, pts[hl]], [accL])
                            else:
                                self.mm(accO[:, 0:nq], V[0:nk, m, :], pts[0][0:nk, 0:nq], first, lastt, [V, pts[0]], [accO])
                                self.mm(accL[:, 0:nq], self.ones_bf[0:nk, :], pts[0][0:nk, 0:nq], first, lastt, [self.ones_bf, pts[0]], [accL])
                                self.mm(accO2[:, 0:nq], V[0:nk, m, :], pts[1][0:nk, 0:nq], first, lastt, [V, pts[1]], [accO2])
                                self.mm(accL2[:, 0:nq], self.ones_bf[0:nk, :], pts[1][0:nk, 0:nq], first, lastt, [self.ones_bf, pts[1]], [accL2])
                        mix = mixs[cnt["mx"] % 2]
                        cnt["mx"] += 1
                        rl = rls[0]
                        self.op("dve", lambda e, rl=rl, accL=accL, nq=nq: e.reciprocal(out=rl[:, 0:nq], in_=accL[:, 0:nq]), [accL], [rl], cost=0.12 + nq / 960.0)
                        if kind == "fox":
                            self.tt("dve", mix[:, 0:nq], accO[:, 0:nq], rl[:, 0:nq], ALU.mult, [accO, rl], [mix])
                            row0 = 128 * c
                        else:
                            rl2 = rls[1]
                            self.op("dve", lambda e, rl2=rl2, accL2=accL2, nq=nq: e.reciprocal(out=rl2[:, 0:nq], in_=accL2[:, 0:nq]), [accL2], [rl2], cost=0.12 + nq / 960.0)
                            self.tt("dve", o1[:, 0:nq], accO[:, 0:nq], rl[:, 0:nq], ALU.mult, [accO, rl], [o1])
                            self.tt("dve", o2[:, 0:nq], accO2[:, 0:nq], rl2[:, 0:nq], ALU.mult, [accO2, rl2], [o2])
                            self.stt(o1[:, 0:nq], o2[:, 0:nq], nlam[:, 0:1], o1[:, 0:nq], ALU.mult, ALU.add, [o2, nlam, o1], [o1])
                            self.tt("pool", sqb[:, 0:nq], o1[:, 0:nq], o1[:, 0:nq], ALU.mult, [o1], [sqb])
                            pq = self.psum(cnt["pt"] % 4)
                            cnt["pt"] += 1
                            self.mm(pq[:, 0:nq], self.ones_bf[:], sqb[:, 0:nq], True, True, [self.ones_bf, sqb], [pq])
                            self.act(rstd[:, 0:nq], pq[:, 0:nq], AF.Sqrt, [pq, epsc], [rstd], bias=epsc[:], scale=1.0 / 128)
                            self.op("dve", lambda e, nq=nq: e.reciprocal(out=rstd[:, 0:nq], in_=rstd[:, 0:nq]), [rstd], [rstd], cost=0.12 + nq / 960.0)
                            self.stt(mix[:, 0:nq], o1[:, 0:nq], dg[:, 0:1], rstd[:, 0:nq], ALU.mult, ALU.mult, [o1, dg, rstd], [mix])
                            row0 = 512 + 128 * c
                        self.dma("sp", S["MIXT"][row0:row0 + 128, pc0:pc0 + nq], mix[:, 0:nq], r=[mix])
        self.P.barrier()

_CACHE = {}


def get_program(stages):
    key = tuple(sorted(stages))
    if key not in _CACHE:
        b = Builder(set(stages))
        nc = b.build()
        _CACHE[key] = (b, nc)
    return _CACHE[key]


STAGES = ("L0", "L1", "L2", "L3")

W_NAMES = ["norm_mix", "norm_ffn", "norm_final", "w_in_even", "w_out_even", "conv_w", "gdn_a_log", "gdn_dt_bias",
           "gdn_out_norm", "ssm_lambda_re", "ssm_lambda_im", "ssm_log_dt", "ssm_b_re", "ssm_b_im", "ssm_c_re",
           "ssm_c_im", "ssm_d", "ssm_glu_w", "ssm_glu_b", "w_in_odd", "w_out_odd", "fox_f_bias", "diff_lambda",
           "diff_out_norm", "ffn_w1", "ffn_w3", "ffn_w2"]


def kernel(**inp):
    b, nc = get_program(STAGES)
    f = lambda a: np.ascontiguousarray(np.asarray(a, dtype=np.float32))
    shared = {k: f(inp[k]) for k in W_NAMES}
    meta = f(inp["meta_tokens"])
    in_maps = []
    ncore = int(os.environ.get("MK_DEV_CORES", NCORES))
    for c in range(ncore):
        m = dict(shared)
        m["xp"] = f(inp["x_prompt"][2 * c:2 * c + 2])
        m["xs"] = f(inp["x_sample"][c])
        m["meta"] = meta
        m["st_conv"] = f(inp["state_conv"][:, c])
        m["st_delta"] = f(inp["state_delta"][:, c])
        m["st_re"] = f(inp["state_ssm_re"][:, c])
        m["st_im"] = f(inp["state_ssm_im"][:, c])
        m["c_fk"] = f(inp["cache_fox_k"][:, c])
        m["c_fv"] = f(inp["cache_fox_v"][:, c])
        m["c_fl"] = f(inp["cache_fox_logf"][:, c])
        m["c_dk"] = f(inp["cache_diff_k"][:, c])
        m["c_dv"] = f(inp["cache_diff_v"][:, c])
        in_maps.append(m)
    res = run_bass_kernel_spmd(nc, in_maps, core_ids=list(range(ncore)))
    R = list(res.results)
    while len(R) < NCORES:
        R.append({k: np.zeros_like(np.asarray(v)) for k, v in R[0].items()})
    cat = lambda name, ax: np.concatenate([np.asarray(r[name]) for r in R], axis=ax)
    stk = lambda name, ax: np.stack([np.asarray(r[name]) for r in R], axis=ax)
    outs = (
        cat("y_p", 0), stk("y_s", 0),
        cat("conv_p", 1), stk("conv_s", 1),
        cat("delta_p", 1), stk("delta_s", 1),
        cat("re_p", 1), stk("re_s", 1), cat("im_p", 1), stk("im_s", 1),
        cat("fk_p", 1), stk("fk_s", 1), cat("fv_p", 1), stk("fv_s", 1),
        cat("fl_p", 1), stk("fl_s", 1), cat("dk_p", 1), stk("dk_s", 1),
        cat("dv_p", 1), stk("dv_s", 1),
    )
    return tuple(np.ascontiguousarray(o, dtype=np.float32) for o in outs)
```

```python
import contextlib
import math
import os
import numpy as np
import concourse.bass as bass
import concourse.mybir as mybir
from concourse.bass_utils import run_bass_kernel_spmd

F32 = mybir.dt.float32
BF16 = mybir.dt.bfloat16
AF = mybir.ActivationFunctionType
ALU = mybir.AluOpType
AX = mybir.AxisListType

NCORES = 8
D = 1024
NE = 2
NO = 2
DEPTH = 4
LP = 4112
LS = 32
NROWS = 2 * LP + LS
SEQROW = [0, LP, 2 * LP]
SEQLEN = [LP, LP, LS]
PCOL = [0, 4160, 8320, 8384]
TOTP = 8448
EVEN_PROJ = 4624
ODD_PROJ = 3080
FFN = 2816
RMS_EPS = 1e-6
AK = [0, LP, 2 * LP]
NK = 2 * LP + 4096 + LS
CB = [AK[0], AK[1] + 1, AK[2] + 2]
DTSIZE = {F32: 4, BF16: 2, mybir.dt.int32: 4}

SEGS = []
for _s in range(3):
    if _s < 2:
        SEGS.append((SEQROW[_s], 16, _s, 0, PCOL[_s]))
        for _k in range(8):
            SEGS.append((SEQROW[_s] + 16 + 512 * _k, 512, _s, 16 + 512 * _k, PCOL[_s] + 64 + 512 * _k))
    else:
        SEGS.append((SEQROW[_s], 32, _s, 0, PCOL[_s]))


class Dep:
    __slots__ = ("w", "r")

    def __init__(self):
        self.w = None
        self.r = []


class Tl:
    __slots__ = ("ap", "dep")

    def __init__(self, ap, dep=None):
        self.ap = ap
        self.dep = dep if dep is not None else Dep()

    def __getitem__(self, k):
        return self.ap[k]


class Prog:
    ENGS = ("pe", "act", "dve", "pool", "sp")
    EPOCH = 8000
    NDMA = 20
    LOOK = 24

    def __init__(self, nc):
        self.nc = nc
        self.ops = []
        self.barriers = []

    def op(self, eng, fn, reads=(), writes=(), dma=False, busy=0.3, lat=None):
        idx = len(self.ops)
        cls = eng + ("_dma" if dma else "")
        raw = set()
        oth = set()
        for d in reads:
            if d.w is not None:
                raw.add(d.w)
        for d in writes:
            if d.w is not None:
                oth.add(d.w)
            oth.update(d.r)
        order = raw | oth
        order.discard(idx)
        deps = set(raw)
        for j in oth:
            if j in raw:
                continue
            if self.ops[j][5] == cls and not dma:
                continue
            deps.add(j)
        if cls == "pe":
            deps = {j for j in deps if self.ops[j][5] != "pe"}
        deps.discard(idx)
        for d in reads:
            d.r.append(idx)
        for d in writes:
            d.w = idx
            d.r = []
        self.ops.append([eng, fn, deps, dma, False, cls, order, busy, busy if lat is None else lat])
        return idx

    def barrier(self):
        if not self.barriers or self.barriers[-1] != len(self.ops):
            self.barriers.append(len(self.ops))

    def schedule_block(self, lo, hi, streams):
        import bisect
        ops = self.ops
        nosched = bool(os.environ.get("MK_NOSCHED"))
        if nosched:
            for i in range(lo, hi):
                streams[ops[i][0]].append(i)
            return
        npred = {}
        succs = {}
        pin = set((os.environ.get("MK_PIN", "")).split(","))
        prev_on = {}
        for i in range(lo, hi):
            e_ = ops[i][0]
            if e_ in pin:
                if e_ in prev_on:
                    ops[i][6].add(prev_on[e_])
                prev_on[e_] = i
        for i in range(lo, hi):
            c = 0
            for j in ops[i][6]:
                if j >= lo:
                    c += 1
                    succs.setdefault(j, []).append(i)
            npred[i] = c
        ready = {}
        avail = {e: [] for e in self.ENGS}
        for i in range(lo, hi):
            if npred[i] == 0:
                avail[ops[i][0]].append(i)
        free_at = {e: 0.0 for e in self.ENGS}
        left = hi - lo
        LOOK = self.LOOK
        while left:
            best = None
            for e in self.ENGS:
                av = avail[e]
                if not av:
                    continue
                fa = free_at[e]
                for c in av[:LOOK]:
                    st = ready.get(c, 0.0)
                    if st < fa:
                        st = fa
                    key = (int(st * 20), c)
                    if best is None or key < best[0]:
                        best = (key, e, c, st)
            _, e, c, st = best
            avail[e].remove(c)
            o = ops[c]
            free_at[e] = st + o[7]
            fin = st + o[8]
            streams[e].append(c)
            left -= 1
            for sidx in succs.get(c, ()):
                if ready.get(sidx, 0.0) < fin:
                    ready[sidx] = fin
                npred[sidx] -= 1
                if npred[sidx] == 0:
                    bisect.insort(avail[ops[sidx][0]], sidx)

    def emit(self, stack):
        nc = self.nc
        ops = self.ops
        streams = {e: [] for e in self.ENGS}
        bounds = [0] + [b for b in self.barriers if 0 < b < len(ops)] + [len(ops)]
        extra = {}
        for bi in range(len(bounds) - 1):
            lo, hi = bounds[bi], bounds[bi + 1]
            if lo == hi:
                continue
            pos = {e: len(streams[e]) for e in self.ENGS}
            pend = set()
            if bi > 0:
                for e in self.ENGS:
                    comp = None
                    nd = 0
                    for i in reversed(streams[e]):
                        if ops[i][3]:
                            if nd < self.NDMA:
                                pend.add(i)
                                nd += 1
                        elif comp is None:
                            comp = i
                            pend.add(i)
                        if comp is not None and nd >= self.NDMA:
                            break
            self.schedule_block(lo, hi, streams)
            if pend:
                for e in self.ENGS:
                    if len(streams[e]) > pos[e]:
                        extra[streams[e][pos[e]]] = pend
        for i, o in enumerate(ops):
            d = o[2]
            if i in extra:
                d = d | extra[i]
                d.discard(i)
                o[2] = d
            for j in d:
                ops[j][4] = True
        self.streams = streams
        ticket = {}
        comp_count = {e: 0 for e in self.ENGS}
        dma_count = {e: 0 for e in self.ENGS}
        dma_prev = {}
        for e in self.ENGS:
            for idx in streams[e]:
                o = ops[idx]
                if o[3]:
                    k = dma_count[e]
                    dma_count[e] += 1
                    key = (e, "d", k % self.NDMA)
                    val = 16 * (k // self.NDMA + 1)
                    ticket[idx] = (key, val)
                    dma_prev[idx] = (key, val - 16)
                elif o[4]:
                    comp_count[e] += 1
                    c = comp_count[e]
                    ep = (c - 1) // self.EPOCH
                    ticket[idx] = ((e, "c", ep), c - ep * self.EPOCH)
        sems = {}
        for k in sorted(set(t[0] for t in ticket.values())):
            sems[k] = stack.enter_context(nc.semaphore("s_%s_%s_%d" % k))
        self.ticket, self.dma_prev, self.nsems = ticket, dma_prev, len(sems)
        block = stack.enter_context(nc.Block())

        def build(e, engine):
            waited = {}
            last = {}
            for idx in streams[e]:
                o = ops[idx]
                need = {}
                for j in o[2]:
                    if j not in ticket:
                        continue
                    k, v = ticket[j]
                    if need.get(k, 0) < v:
                        need[k] = v
                if o[3]:
                    k, v = dma_prev[idx]
                    if v > 0 and need.get(k, 0) < v:
                        need[k] = v
                for k, v in need.items():
                    if waited.get(k, 0) >= v:
                        continue
                    engine.wait_ge(sems[k], v)
                    waited[k] = v
                inst = o[1](engine)
                if o[3]:
                    k, v = ticket[idx]
                    inst.then_inc(sems[k], 16)
                    last[k] = v
                elif o[4]:
                    inst.then_inc(sems[ticket[idx][0]], 1)
            for k, v in last.items():
                if waited.get(k, 0) < v:
                    engine.wait_ge(sems[k], v)

        @block.tensor
        def _(eng):
            build("pe", eng)

        @block.scalar
        def _(eng):
            build("act", eng)

        @block.vector
        def _(eng):
            build("dve", eng)

        @block.gpsimd
        def _(eng):
            build("pool", eng)

        @block.sync
        def _(eng):
            build("sp", eng)


class Builder:
    ARENA_F32 = 49152

    def __init__(self, stages):
        self.stages = stages
        self.nc = bass.Bass("TRN2", target_bir_lowering=False)
        self.P = Prog(self.nc)
        self.stack = contextlib.ExitStack()
        self.din = {}
        self.dout = {}
        self.scr = {}

    def inp(self, name, shape):
        self.din[name] = self.nc.dram_tensor(name, list(shape), F32, kind="ExternalInput").ap()
        return self.din[name]

    def outp(self, name, shape):
        self.dout[name] = self.nc.dram_tensor(name, list(shape), F32, kind="ExternalOutput").ap()
        return self.dout[name]

    def scratch(self, name, shape, dt):
        self.scr[name] = self.nc.dram_tensor(name, list(shape), dt, kind="Internal").ap()
        return self.scr[name]

    def reset_arena(self):
        self.aoff = self.aperm

    def sb(self, free, dt, parts=128):
        n = 1
        for f in free:
            n *= f
        nbytes = n * DTSIZE[dt]
        nb = (nbytes + 31) // 32 * 32
        assert self.aoff + nb <= self.ARENA_F32 * 4, "arena overflow %d" % (self.aoff + nb)
        v = self.arena[:, self.aoff // 4:(self.aoff + nbytes + 3) // 4]
        self.aoff += nb
        if dt != F32:
            v = v.bitcast(dt)
        v = v[:, 0:n]
        if len(free) == 2:
            v = v.rearrange("p (a b) -> p a b", a=free[0])
        elif len(free) == 3:
            v = v.rearrange("p (a b c) -> p a b c", a=free[0], b=free[1])
        if parts != 128:
            v = v[0:parts]
        return Tl(v)

    def psum(self, bank, dt=F32):
        t = self.banks[bank]
        ap = t.ap if dt == F32 else t.ap.bitcast(dt)
        return Tl(ap, t.dep)

    @staticmethod
    def _fe(ap):
        n = 1
        for d in tuple(ap.shape)[1:]:
            n *= int(d)
        return n

    def op(self, eng, fn, r=(), w=(), cost=None):
        if cost is None:
            cost = 0.3
        self.P.op(eng, fn, [t.dep for t in r], [t.dep for t in w], busy=cost)

    def dma(self, q, out, in_, r=(), w=(), **kw):
        nb = 1
        for d in tuple(out.shape):
            nb *= int(d)
        nb *= DTSIZE.get(out.dtype, 4)
        busy = 0.08 if q == "sp" else 0.7
        self.P.op(q, lambda e: e.dma_start(out=out, in_=in_, **kw), [t.dep for t in r], [t.dep for t in w], dma=True,
                  busy=busy, lat=2.0 + nb / 120000.0)

    def mm(self, out, lhsT, rhs, start, stop, r, w):
        c = 0.035 + self._fe(rhs) * (4 if rhs.dtype == F32 else 1) / 2400.0
        self.op("pe", lambda e: e.matmul(out, lhsT=lhsT, rhs=rhs, start=start, stop=stop), r, w, cost=c)

    def tr(self, out, in_, ident, r, w):
        c = 0.06 + self._fe(in_) * (4 if in_.dtype == F32 else 1) / 2400.0
        self.op("pe", lambda e: e.transpose(out, in_, ident), r, w, cost=c)

    def act(self, out, in_, func, r, w, bias=None, scale=None, accum=None):
        kw = {}
        if bias is not None:
            kw["bias"] = bias
        if scale is not None:
            kw["scale"] = scale
        if accum is not None:
            kw["accum_out"] = accum
        c = 0.2 + self._fe(in_) / 1400.0
        self.op("act", lambda e: e.activation(out=out, in_=in_, func=func, **kw), r, w, cost=c)

    def _vc(self, eng, n, two=False):
        if eng == "pool":
            return 0.2 + n / 600.0
        return 0.12 + n * (2 if two else 1) / 960.0

    def tt(self, eng, out, in0, in1, op, r, w):
        self.op(eng, lambda e: e.tensor_tensor(out=out, in0=in0, in1=in1, op=op), r, w, cost=self._vc(eng, self._fe(out), True))

    def ts(self, eng, out, in0, s1, op0, r, w, s2=None, op1=None):
        c = self._vc(eng, self._fe(out))
        if op1 is None:
            self.op(eng, lambda e: e.tensor_scalar(out=out, in0=in0, scalar1=s1, scalar2=None, op0=op0), r, w, cost=c)
        else:
            self.op(eng, lambda e: e.tensor_scalar(out=out, in0=in0, scalar1=s1, scalar2=s2, op0=op0, op1=op1), r, w, cost=c)

    def stt(self, out, in0, scalar, in1, op0, op1, r, w):
        self.op("dve", lambda e: e.scalar_tensor_tensor(out=out, in0=in0, scalar=scalar, in1=in1, op0=op0, op1=op1), r, w,
                cost=self._vc("dve", self._fe(out), True))

    def cp(self, eng, out, in_, r, w):
        if eng == "act":
            self.op("act", lambda e: e.copy(out=out, in_=in_), r, w, cost=0.2 + self._fe(out) / 1400.0)
        else:
            self.op(eng, lambda e: e.tensor_copy(out=out, in_=in_), r, w, cost=self._vc(eng, self._fe(out)))

    def memset(self, eng, ap, val, w):
        self.op(eng, lambda e: e.memset(ap, val), (), w, cost=self._vc(eng, self._fe(ap)))

    def declare(self):
        i = self.inp
        i("xp", [2, 4096, D]); i("xs", [LS, D]); i("meta", [16, D])
        i("st_conv", [NE, 3, 3072]); i("st_delta", [NE, 8, 128, 128])
        i("st_re", [NE, 32, 64]); i("st_im", [NE, 32, 64])
        i("c_fk", [NO, 4096, 8, 64]); i("c_fv", [NO, 4096, 8, 64]); i("c_fl", [NO, 4096, 8])
        i("c_dk", [NO, 4096, 8, 64]); i("c_dv", [NO, 4096, 4, 128])
        i("norm_mix", [DEPTH, D]); i("norm_ffn", [DEPTH, D]); i("norm_final", [D])
        i("w_in_even", [NE, D, EVEN_PROJ]); i("w_out_even", [NE, 1536, D]); i("conv_w", [NE, 4, 3072])
        i("gdn_a_log", [NE, 8]); i("gdn_dt_bias", [NE, 8]); i("gdn_out_norm", [NE, 128])
        i("ssm_lambda_re", [NE, 32, 64]); i("ssm_lambda_im", [NE, 32, 64]); i("ssm_log_dt", [NE, 32])
        i("ssm_b_re", [NE, 32, 64, 16]); i("ssm_b_im", [NE, 32, 64, 16])
        i("ssm_c_re", [NE, 32, 16, 64]); i("ssm_c_im", [NE, 32, 16, 64])
        i("ssm_d", [NE, 512]); i("ssm_glu_w", [NE, 512, 512]); i("ssm_glu_b", [NE, 512])
        i("w_in_odd", [NO, D, ODD_PROJ]); i("w_out_odd", [NO, D, D])
        i("fox_f_bias", [NO, 8]); i("diff_lambda", [NO, 4, 64]); i("diff_out_norm", [NO, 128])
        i("ffn_w1", [DEPTH, D, FFN]); i("ffn_w3", [DEPTH, D, FFN]); i("ffn_w2", [DEPTH, FFN, D])
        o = self.outp
        o("y_p", [2, 4096, D]); o("y_s", [LS, D])
        o("conv_p", [NE, 2, 3, 3072]); o("conv_s", [NE, 3, 3072])
        o("delta_p", [NE, 2, 8, 128, 128]); o("delta_s", [NE, 8, 128, 128])
        o("re_p", [NE, 2, 32, 64]); o("re_s", [NE, 32, 64]); o("im_p", [NE, 2, 32, 64]); o("im_s", [NE, 32, 64])
        o("fk_p", [NO, 2, LP, 8, 64]); o("fk_s", [NO, LS, 8, 64])
        o("fv_p", [NO, 2, LP, 8, 64]); o("fv_s", [NO, LS, 8, 64])
        o("fl_p", [NO, 2, LP, 8]); o("fl_s", [NO, LS, 8])
        o("dk_p", [NO, 2, LP, 8, 64]); o("dk_s", [NO, LS, 8, 64])
        o("dv_p", [NO, 2, LP, 4, 128]); o("dv_s", [NO, LS, 4, 128])
        s = self.scratch
        s("Xres", [NROWS, D], F32)
        s("QT", [1024, TOTP], BF16); s("KT", [1024, TOTP], BF16); s("VT", [1024, TOTP], BF16)
        s("ZT", [1024, TOTP], BF16); s("UT", [512, TOTP], BF16)
        s("Btok", [TOTP, 8], F32); s("GCtok", [TOTP, 8], F32); s("GCT", [8, TOTP], F32)
        s("MIXT", [1536, TOTP], BF16)
        s("KFT", [512, NK], BF16); s("KDT", [512, NK], BF16)
        s("QFT", [512, NROWS], BF16); s("QDT", [512, NROWS], BF16)
        s("VF", [NK, 512], BF16); s("VD", [NK, 512], BF16)
        s("CUM", [NK + 3, 8], F32)

    def build(self):
        nc = self.nc
        self.declare()
        st = self.stack
        self.arena = st.enter_context(nc.sbuf_tensor("arena", [128, self.ARENA_F32], F32))
        self.banks = [Tl(st.enter_context(nc.psum_tensor("bank%d" % b, [128, 512], F32))[:]) for b in range(8)]
        self.aoff = 0
        self.aperm = 0
        self.ident = self.sb([128], BF16)
        self.identf = self.sb([128], F32)
        self.ones_bf = self.sb([128], BF16)
        self.tri = self.sb([128], F32)
        self.zeros = self.sb([512], F32)
        self.onec = self.sb([1], F32)
        self.trif = self.sb([128], F32)
        self.onesf = self.sb([128], F32)
        self.aperm = self.aoff
        self.consts()
        self.init_x()
        for l in range(DEPTH):
            if ("L%d" % l) not in self.stages:
                continue
            i_ = l // 2
            if l % 2 == 0:
                self.phaseA_even(l)
                self.phaseB_gdn(l)
                self.phaseB_s5(l)
                self.phaseC_outproj(l, self.din["w_out_even"][i_], 12)
            else:
                self.phaseA_odd(l)
                self.phaseB_attn(l)
                self.phaseC_outproj(l, self.din["w_out_odd"][i_], 8)
            self.phaseD_ffn(l)
        self.P.emit(st)
        return nc

    def consts(self):
        idb, idf, tri = self.ident, self.identf, self.tri
        self.memset("pool", idb[:], 1.0, [idb])
        self.op("pool", lambda e: e.affine_select(out=idb[:], in_=idb[:], pattern=[[-1, 128]], compare_op=ALU.is_equal,
                                                  fill=0.0, base=0, channel_multiplier=1), [idb], [idb])
        self.memset("pool", idf[:], 1.0, [idf])
        self.op("pool", lambda e: e.affine_select(out=idf[:], in_=idf[:], pattern=[[-1, 128]], compare_op=ALU.is_equal,
                                                  fill=0.0, base=0, channel_multiplier=1), [idf], [idf])
        self.memset("dve", self.ones_bf[:], 1.0, [self.ones_bf])
        self.memset("dve", self.zeros[:], 0.0, [self.zeros])
        self.memset("dve", self.onec[:], 1.0, [self.onec])
        self.memset("pool", tri[:], 1.0, [tri])
        self.op("pool", lambda e: e.affine_select(out=tri[:], in_=tri[:], pattern=[[1, 128]], compare_op=ALU.is_ge,
                                                  fill=0.0, base=0, channel_multiplier=-1), [tri], [tri])
        trif = self.trif
        self.memset("pool", trif[:], 1.0, [trif])
        self.op("pool", lambda e: e.affine_select(out=trif[:], in_=trif[:], pattern=[[1, 128]], compare_op=ALU.is_ge,
                                                  fill=0.0, base=0, channel_multiplier=-1), [trif], [trif])
        self.memset("pool", self.onesf[:], 1.0, [self.onesf])
        self.memset("pool", tri[64:128, 0:64], 0.0, [tri])
        self.memset("pool", tri[0:64, 64:128], 0.0, [tri])

    def init_x(self):
        X = self.scr["Xres"]
        for s in range(2):
            self.dma("sp", X[SEQROW[s]:SEQROW[s] + 16, :], self.din["meta"])
            for q in range(4):
                self.dma("sp", X[SEQROW[s] + 16 + 1024 * q:SEQROW[s] + 16 + 1024 * (q + 1), :],
                         self.din["xp"][s, 1024 * q:1024 * (q + 1), :])
        self.dma("sp", X[SEQROW[2]:SEQROW[2] + LS, :], self.din["xs"])
        z = self.zeros
        for name in ("QT", "KT", "VT"):
            A = self.scr[name].rearrange("(h p) t -> p h t", p=128)
            zb = z.ap.bitcast(BF16)
            for s, (c0, c1) in ((0, (16, 64)), (1, (16, 64)), (2, (32, 64)), (3, (0, 64))):
                w = c1 - c0
                src = zb[:, 0:8 * w].rearrange("p (h t) -> p h t", h=8)
                self.dma("sp", A[:, :, PCOL[s] + c0:PCOL[s] + c1], src, r=[z])
        for name in ("Btok", "GCtok"):
            A = self.scr[name]
            for s, (c0, c1) in ((0, (16, 64)), (1, (16, 64)), (2, (32, 64)), (3, (0, 64))):
                self.dma("sp", A[PCOL[s] + c0:PCOL[s] + c1, :], z[0:c1 - c0, 0:8], r=[z])
        A = self.scr["GCT"]
        for s, (c0, c1) in ((0, (16, 64)), (1, (16, 64)), (2, (32, 64)), (3, (0, 64))):
            self.dma("sp", A[:, PCOL[s] + c0:PCOL[s] + c1], z[0:8, 0:c1 - c0], r=[z])
        for s_ in range(3):
            self.dma("sp", self.scr["CUM"][CB[s_]:CB[s_] + 1, :], z[0:1, 0:8], r=[z])
        self.P.barrier()

    def load_weight(self, wdram, K, N, dst, gain=None, stage_cols=2312):
        kc = K // 128
        stg = [self.sb([stage_cols], F32), self.sb([stage_cols], F32)]
        i = 0
        for c in range(kc):
            for n0 in range(0, N, stage_cols):
                n1 = min(N, n0 + stage_cols)
                s = stg[i % 2]
                i += 1
                self.dma("pool" if i % 2 else "sp", s[:, 0:n1 - n0], wdram[c * 128:(c + 1) * 128, n0:n1], w=[s])
                if gain is not None:
                    self.act(dst[:, c, n0:n1], s[:, 0:n1 - n0], AF.Copy, [s, gain], [dst], scale=gain[:, c:c + 1])
                else:
                    self.cp("act", dst[:, c, n0:n1], s[:, 0:n1 - n0], [s], [dst])

    def load_gain(self, vec_ap):
        g = self.sb([8], F32)
        self.dma("sp", g[:], vec_ap.rearrange("(c p) -> p c", p=128), w=[g], allow_slow_non_contiguous=True)
        return g

    def load_x(self, r0, n, xt):
        X = self.scr["Xres"]
        nsub = (n + 127) // 128
        if n >= 128:
            self.dma("sp", xt[:, 0:nsub, :], X[r0:r0 + n, :].rearrange("(j p) d -> p j d", p=128), w=[xt])
        else:
            self.dma("sp", xt[0:n, 0, :], X[r0:r0 + n, :], w=[xt])

    def store_x(self, r0, n, xt):
        X = self.scr["Xres"]
        nsub = (n + 127) // 128
        if n >= 128:
            self.dma("pool", X[r0:r0 + n, :].rearrange("(j p) d -> p j d", p=128), xt[:, 0:nsub, :], r=[xt])
        else:
            self.dma("pool", X[r0:r0 + n, :], xt[0:n, 0, :], r=[xt])

    def norm_transpose(self, seg, xt, h, junk, ss, hT, bank, load=True):
        r0, n = seg[0], seg[1]
        X = self.scr["Xres"]
        nsub = (n + 127) // 128
        if not load:
            pass
        elif n >= 128:
            self.dma("sp", xt[:, 0:nsub, :], X[r0:r0 + n, :].rearrange("(j p) d -> p j d", p=128), w=[xt])
        else:
            self.dma("sp", xt[0:n, 0, :], X[r0:r0 + n, :], w=[xt])
        pj = min(128, n)
        for j in range(nsub):
            self.act(junk[0:pj, :], xt[0:pj, j, :], AF.Square, [xt], [junk, ss], accum=ss[0:pj, j:j + 1])
        self.act(ss[0:pj, 0:nsub], ss[0:pj, 0:nsub], AF.Sqrt, [ss], [ss], bias=self.epsc[0:pj, :], scale=1.0 / D)
        self.op("dve", lambda e: e.reciprocal(out=ss[0:pj, 0:nsub], in_=ss[0:pj, 0:nsub]), [ss], [ss])
        for j in range(nsub):
            self.ts("dve", h[0:pj, j, :], xt[0:pj, j, :], ss[0:pj, j:j + 1], ALU.mult, [xt, ss], [h])
        for j in range(nsub):
            pT = self.psum(bank[j % len(bank)], BF16)
            pv = pT.ap.rearrange("p (c t) -> p c t", c=8)
            for c in range(8):
                self.tr(pv[:, c, 0:pj], h[0:pj, j, c * 128:(c + 1) * 128], self.ident[0:pj, 0:pj], [h, self.ident], [pT])
            self.cp("act" if j % 2 else "dve", hT[:, :, j * 128:j * 128 + pj], pv[:, :, 0:pj], [pT], [hT])

    def phaseA_even(self, l):
        i = l // 2
        self.reset_arena()
        S = self.scr
        self.epsc = self.sb([1], F32)
        self.memset("dve", self.epsc[:], RMS_EPS, [self.epsc])
        eps6 = self.sb([1], F32)
        self.memset("dve", eps6[:], 1e-6, [eps6])
        wbf = self.sb([8, EVEN_PROJ], BF16)
        gain = self.load_gain(self.din["norm_mix"][l])
        save = self.aoff
        self.load_weight(self.din["w_in_even"][i], D, EVEN_PROJ, wbf, gain)
        self.P.barrier()
        self.aoff = save
        cw = self.sb([24, 4], F32)
        for j in range(4):
            self.dma("sp", cw[:, :, j], self.din["conv_w"][i, j].rearrange("(m p) -> p m", p=128), w=[cw],
                     allow_slow_non_contiguous=True)
        dtb = self.sb([8], F32)
        self.dma("sp", dtb[:], self.din["gdn_dt_bias"][i:i + 1, :].to_broadcast([128, 8]), w=[dtb])
        nea = self.sb([8], F32)
        self.dma("sp", nea[:], self.din["gdn_a_log"][i:i + 1, :].to_broadcast([128, 8]), w=[nea])
        self.act(nea[:], nea[:], AF.Exp, [nea], [nea])
        self.ts("dve", nea[:], nea[:], -1.0, ALU.mult, [nea], [nea])
        halo = self.sb([24, 3], F32)
        xts = [self.sb([4, D], F32), self.sb([4, D], F32)]
        h = self.sb([4, D], BF16)
        junk = self.sb([D], BF16)
        sss = [self.sb([4], F32), self.sb([4], F32)]
        hTs = [self.sb([8, 512], BF16), self.sb([8, 512], BF16)]
        pcs = [self.sb([516], F32) for _ in range(4)]
        accs = [self.sb([512], F32) for _ in range(4)]
        svs = [self.sb([512], F32) for _ in range(4)]
        sqs = [self.sb([512], BF16) for _ in range(4)]
        rss = [self.sb([512], F32) for _ in range(4)]
        obs = [self.sb([512], BF16) for _ in range(6)]
        tks = [self.sb([16], F32) for _ in range(2)]
        gcs = [self.sb([8], F32) for _ in range(2)]
        gct = self.sb([512], F32)
        ob_i = 0
        for si, seg in enumerate(SEGS):
            r0, n, s, t0, pc0 = seg
            xt, ss, hT = xts[si % 2], sss[si % 2], hTs[si % 2]
            self.norm_transpose(seg, xt, h, junk, ss, hT, bank=[0, 1])
            nsub = (n + 127) // 128
            pj = min(128, n)
            if t0 == 0:
                if s < 2:
                    self.memset("pool", halo[:], 0.0, [halo])
                else:
                    for tt_ in range(3):
                        self.dma("sp", halo[:, :, tt_], self.din["st_conv"][i, tt_].rearrange("(m p) -> p m", p=128), w=[halo],
                                 allow_slow_non_contiguous=True)
            last = (t0 + n == SEQLEN[s])
            chunks = [("qkv", m, m * 128) for m in range(24)] + [("z", m, 3072 + m * 128) for m in range(8)] + \
                     [("u", m, 4112 + m * 128) for m in range(4)]
            for ci, (kind, m, col) in enumerate(chunks):
                bk = 2 + ci % 4
                ps = self.psum(bk)
                for c in range(8):
                    self.mm(ps[:, 0:n], wbf[:, c, col:col + 128], hT[:, c, 0:n], c == 0, c == 7, [wbf, hT], [ps])
                ob = obs[ob_i % 6]
                ob_i += 1
                if kind == "qkv":
                    pc, acc, sv, sq, rs = pcs[ci % 4], accs[ci % 4], svs[ci % 4], sqs[ci % 4], rss[ci % 4]
                    self.cp("act", pc[:, 3:3 + n], ps[:, 0:n], [ps], [pc])
                    self.cp("pool", pc[:, 0:3], halo[:, m, :], [halo], [pc])
                    if last:
                        if s < 2:
                            dst = self.dout["conv_p"][i, s, :, m * 128:(m + 1) * 128]
                        else:
                            dst = self.dout["conv_s"][i, :, m * 128:(m + 1) * 128]
                        self.dma("pool", dst.rearrange("t c -> c t"), pc[:, n:n + 3], r=[pc], allow_slow_non_contiguous=True)
                    else:
                        self.cp("pool", halo[:, m, :], pc[:, n:n + 3], [pc], [halo])
                    self.ts("dve", acc[:, 0:n], pc[:, 3:3 + n], cw[:, m, 3:4], ALU.mult, [pc, cw], [acc])
                    for j in (2, 1, 0):
                        self.stt(acc[:, 0:n], pc[:, j:j + n], cw[:, m, j:j + 1], acc[:, 0:n], ALU.mult, ALU.add, [pc, cw, acc], [acc])
                    if m >= 16:
                        self.act(ob[:, 0:n], acc[:, 0:n], AF.Silu, [acc], [ob])
                        dstA = S["VT"]
                    else:
                        self.act(sv[:, 0:n], acc[:, 0:n], AF.Silu, [acc], [sv])
                        self.tt("pool", sq[:, 0:n], sv[:, 0:n], sv[:, 0:n], ALU.mult, [sv], [sq])
                        p2 = self.psum(6 + ci % 2)
                        self.mm(p2[:, 0:n], self.ones_bf[:], sq[:, 0:n], True, True, [self.ones_bf, sq], [p2])
                        self.act(rs[:, 0:n], p2[:, 0:n], AF.Sqrt, [p2, eps6], [rs], bias=eps6[:])
                        self.op("dve", lambda e, rs=rs, n=n: e.reciprocal(out=rs[:, 0:n], in_=rs[:, 0:n]), [rs], [rs], cost=0.12 + n / 960.0)
                        sc = (128.0 ** -0.5) if m < 8 else 1.0
                        self.stt(ob[:, 0:n], sv[:, 0:n], sc, rs[:, 0:n], ALU.mult, ALU.mult, [sv, rs], [ob])
                        dstA = S["QT"] if m < 8 else S["KT"]
                    mm_ = m % 8
                elif kind == "z":
                    self.cp("act", ob[:, 0:n], ps[:, 0:n], [ps], [ob])
                    dstA = S["ZT"]
                    mm_ = m
                else:
                    self.cp("act", ob[:, 0:n], ps[:, 0:n], [ps], [ob])
                    dstA = S["UT"]
                    mm_ = m
                self.dma("sp", dstA[mm_ * 128:(mm_ + 1) * 128, pc0:pc0 + n], ob[:, 0:n], r=[ob])
            for j in range(nsub):
                tk, gc = tks[j % 2], gcs[j % 2]
                ps = self.psum(6 + j % 2)
                for c in range(8):
                    self.mm(ps[0:pj, 0:16], hT[:, c, j * 128:j * 128 + pj], wbf[:, c, 4096:4112], c == 0, c == 7, [hT, wbf], [ps])
                self.act(tk[0:pj, 0:8], ps[0:pj, 0:8], AF.Sigmoid, [ps], [tk])
                self.dma("pool", S["Btok"][pc0 + j * 128:pc0 + j * 128 + pj, :], tk[0:pj, 0:8], r=[tk])
                self.tt("dve", tk[0:pj, 8:16], ps[0:pj, 8:16], dtb[0:pj, :], ALU.add, [ps, dtb], [tk])
                self.act(tk[0:pj, 8:16], tk[0:pj, 8:16], AF.Exp, [tk], [tk])
                self.act(tk[0:pj, 8:16], tk[0:pj, 8:16], AF.Ln, [tk, self.onec], [tk], bias=self.onec[0:pj, :])
                self.tt("dve", tk[0:pj, 8:16], tk[0:pj, 8:16], nea[0:pj, :], ALU.mult, [tk, nea], [tk])
                pm = max(pj, 64)
                ps3 = self.psum(2 + j % 2)
                self.mm(ps3[0:pm, 0:8], self.tri[0:pj, 0:pm], tk[0:pj, 8:16], True, True, [self.tri, tk], [ps3])
                self.cp("dve", gc[0:pm, :], ps3[0:pm, 0:8], [ps3], [gc])
                self.dma("pool", S["GCtok"][pc0 + j * 128:pc0 + j * 128 + pm, :], gc[0:pm, :], r=[gc])
                ps4 = self.psum(4 + j % 2)
                self.tr(ps4[0:8, 0:pm], gc[0:pm, :], self.identf[0:pm, 0:pm], [gc, self.identf], [ps4])
                self.cp("act", gct[0:8, j * 128:j * 128 + pm], ps4[0:8, 0:pm], [ps4], [gct])
            self.dma("pool", S["GCT"][:, pc0:pc0 + max(n, 64)], gct[0:8, 0:max(n, 64)], r=[gct])
        self.P.barrier()


    def phaseB_gdn(self, l):
        i = l // 2
        self.reset_arena()
        S = self.scr
        NEG = -1.0e5
        sb = self.sb
        epsc = sb([1], F32)
        self.memset("dve", epsc[:], RMS_EPS, [epsc])
        mSL = sb([128], F32)
        mUI = sb([128], F32)
        self.memset("pool", mSL[:], 0.0, [mSL])
        self.op("pool", lambda e: e.affine_select(out=mSL[:], in_=mSL[:], pattern=[[-1, 128]], compare_op=ALU.is_gt,
                                                  fill=NEG, base=0, channel_multiplier=1), [mSL], [mSL])
        self.memset("pool", mUI[:], 0.0, [mUI])
        self.op("pool", lambda e: e.affine_select(out=mUI[:], in_=mUI[:], pattern=[[1, 128]], compare_op=ALU.is_ge,
                                                  fill=NEG, base=0, channel_multiplier=-1), [mUI], [mUI])
        for m_ in (mSL, mUI):
            self.memset("pool", m_[64:128, 0:64], NEG, [m_])
            self.memset("pool", m_[0:64, 64:128], NEG, [m_])
        gain = sb([1], F32)
        self.dma("sp", gain[:], self.din["gdn_out_norm"][i].rearrange("(p o) -> p o", o=1), w=[gain])

        def T3(dt):
            return sb([8, 128], dt)
        ld = [dict(kT=T3(BF16), qT=T3(BF16), vT=T3(BF16), zT=T3(BF16), gcrow=T3(F32), btok=sb([8], F32), gctok=sb([8], F32))
              for _ in range(2)]
        diff, e1, e2 = T3(F32), T3(F32), T3(F32)
        A, B = T3(F32), T3(F32)
        Xs, Ys, Ps, Pts = [T3(F32), T3(F32)], [T3(F32), T3(F32)], [T3(F32), T3(F32)], [T3(F32), T3(F32)]
        IXs, IYs = [T3(F32), T3(F32)], [T3(F32), T3(F32)]
        Tt, Kb, Kd, Vb, QKm = T3(BF16), T3(BF16), T3(BF16), T3(BF16), T3(BF16)
        Tts, Kbs, Kds, Vbs, QKms = [Tt, T3(BF16)], [Kb, T3(BF16)], [Kd, T3(BF16)], [Vb, T3(BF16)], [QKm, T3(BF16)]
        diffs, e1s, e2s, As, Bs = [diff, T3(F32)], [e1, T3(F32)], [e2, T3(F32)], [A, T3(F32)], [B, T3(F32)]
        nWTa, nWTb, vn, qsa, qsb = T3(BF16), T3(BF16), T3(BF16), T3(BF16), T3(BF16)
        EG, o, sq = T3(F32), T3(F32), T3(F32)
        on, gsz, mixed = T3(BF16), T3(BF16), T3(BF16)
        Sa, Sb, Sabf, Sbbf = T3(F32), T3(F32), T3(BF16), T3(BF16)
        glast, sc1, sc2, eglA, eglB, ssq = (sb([8], F32) for _ in range(6))
        for t_ in (nWTa, nWTb, qsa, qsb):
            self.memset("pool", t_[:], 0.0, [t_])

        def bcl(ap):
            return ap.unsqueeze(2).to_broadcast([128, 8, 128])

        def bcm(ap):
            return ap.unsqueeze(1).to_broadcast([128, 8, 128])

        def v4(t, g):
            return t.ap.rearrange("p (a b) -> p a b", a=4)

        def v8(t):
            return t.ap.rearrange("p (a b) -> p a b", a=8)

        KTv = S["KT"].rearrange("(h p) t -> p h t", p=128)
        QTv = S["QT"].rearrange("(h p) t -> p h t", p=128)
        VTv = S["VT"].rearrange("(h p) t -> p h t", p=128)
        ZTv = S["ZT"].rearrange("(h p) t -> p h t", p=128)
        MXv = S["MIXT"].rearrange("(h p) t -> p h t", p=128)

        def init_state(sample):
            for (F, Fb) in ((Sa, Sabf), (Sb, Sbbf)):
                self.memset("pool", F[:], 0.0, [F])
                self.memset("pool", Fb[:], 0.0, [Fb])
            if sample:
                self.dma("sp", Sa[:], self.din["st_delta"][i].rearrange("h k v -> k h v"), w=[Sa])
                self.cp("act", Sabf[:], Sa[:], [Sa], [Sabf])

        packs = [(False, PCOL[0] + 64 * c, PCOL[1] + 64 * c) for c in range(65)] + [(True, PCOL[2], PCOL[3])]
        for pi, (sample, ca, cb) in enumerate(packs):
            if pi == 0 or sample:
                init_state(sample)
            L = ld[pi % 2]
            Tt, Kb, Kd, Vb, QKm = Tts[pi % 2], Kbs[pi % 2], Kds[pi % 2], Vbs[pi % 2], QKms[pi % 2]
            diff, e1, e2, A, B = diffs[pi % 2], e1s[pi % 2], e2s[pi % 2], As[pi % 2], Bs[pi % 2]
            kT, qT, vT, zT, gcrow, btok, gctok = (L[k] for k in ("kT", "qT", "vT", "zT", "gcrow", "btok", "gctok"))
            for slot, col in ((0, ca), (1, cb)):
                fs = slice(64 * slot, 64 * slot + 64)
                self.dma("sp", kT[:, :, fs], KTv[:, :, col:col + 64], w=[kT])
                self.dma("sp", qT[:, :, fs], QTv[:, :, col:col + 64], w=[qT])
                self.dma("sp", vT[:, :, fs], VTv[:, :, col:col + 64], w=[vT])
                self.dma("pool", zT[:, :, fs], ZTv[:, :, col:col + 64], w=[zT])
                self.dma("pool", gcrow[:, :, fs], S["GCT"][:, col:col + 64].partition_broadcast(128), w=[gcrow])
                self.dma("pool", btok[fs, :], S["Btok"][col:col + 64, :], w=[btok])
                self.dma("pool", gctok[fs, :], S["GCtok"][col:col + 64, :], w=[gctok])
            self.tt("dve", diff[:], bcl(gctok[:]), gcrow[:], ALU.subtract, [gctok, gcrow], [diff])
            self.tt("pool", e1[:], diff[:], bcm(mSL[:]), ALU.add, [diff, mSL], [e1])
            self.tt("dve", e2[:], bcm(mUI[:]), diff[:], ALU.subtract, [diff, mUI], [e2])
            self.act(e1[:], e1[:], AF.Exp, [e1], [e1])
            self.act(e2[:], e2[:], AF.Exp, [e2], [e2])
            self.tt("pool", e1[:], e1[:], bcl(btok[:]), ALU.mult, [e1, btok], [e1])
            self.cp("pool", glast[0:64, :], gcrow[0:64, :, 63], [gcrow], [glast])
            self.cp("pool", glast[64:128, :], gcrow[64:128, :, 127], [gcrow], [glast])
            self.act(sc1[:], gctok[:], AF.Exp, [gctok], [sc1])
            self.tt("dve", sc1[:], sc1[:], btok[:], ALU.mult, [sc1, btok], [sc1])
            self.tt("dve", sc2[:], glast[:], gctok[:], ALU.subtract, [glast, gctok], [sc2])
            self.act(sc2[:], sc2[:], AF.Exp, [sc2], [sc2])
            self.act(eglA[:], gcrow[:, :, 63], AF.Exp, [gcrow], [eglA])
            self.act(eglB[:], gcrow[:, :, 127], AF.Exp, [gcrow], [eglB])
            self.act(EG[:], gcrow[:], AF.Exp, [gcrow], [EG])
            for g in range(2):
                pk = self.psum(g)
                pq = self.psum(2 + g)
                for hh in range(4):
                    h = 4 * g + hh
                    self.mm(v4(pk, 0)[:, hh, :], kT[:, h, :], kT[:, h, :], True, True, [kT], [pk])
                    self.mm(v4(pq, 0)[:, hh, :], kT[:, h, :], qT[:, h, :], True, True, [kT, qT], [pq])
                hs = slice(4 * g, 4 * g + 4)
                self.tt("dve", A[:, hs, :], v4(pk, 0), e1[:, hs, :], ALU.mult, [pk, e1], [A])
                self.tt("dve", QKm[:, hs, :], v4(pq, 0), e2[:, hs, :], ALU.mult, [pq, e2], [QKm])
            for g in range(2):
                pb = self.psum(4 + g)
                for hh in range(4):
                    self.tr(v4(pb, 0)[:, hh, :], A[:, 4 * g + hh, :], self.identf[:], [A, self.identf], [pb])
                self.cp("act", B[:, 4 * g:4 * g + 4, :], v4(pb, 0), [pb], [B])
            X, Y = A, B
            P0, Pt0 = Ps[0], Pts[0]
            self.tt("pool", P0[:], bcm(self.identf[:]), A[:], ALU.subtract, [A, self.identf], [P0])
            self.tt("pool", Pt0[:], bcm(self.identf[:]), B[:], ALU.subtract, [B, self.identf], [Pt0])
            Pc, Ptc = P0, Pt0
            for k in range(1, 6):
                lastk = (k == 5)
                Xn, Yn = Xs[k % 2], Ys[k % 2]
                Pn, Ptn = Ps[k % 2], Pts[k % 2]
                for g in range(2):
                    hs = slice(4 * g, 4 * g + 4)
                    if not lastk:
                        px = self.psum(g)
                        for hh in range(4):
                            h = 4 * g + hh
                            self.mm(v4(px, 0)[:, hh, :], Y[:, h, :], X[:, h, :], True, True, [X, Y], [px])
                        self.cp("act", Xn[:, hs, :], v4(px, 0), [px], [Xn])
                    py = self.psum(2 + g)
                    for hh in range(4):
                        h = 4 * g + hh
                        self.mm(v4(py, 0)[:, hh, :], X[:, h, :], Y[:, h, :], True, True, [X, Y], [py])
                    self.cp("dve", Yn[:, hs, :], v4(py, 0), [py], [Yn])
                for g in range(2):
                    hs = slice(4 * g, 4 * g + 4)
                    pt = self.psum(4 + g)
                    for hh in range(4):
                        h = 4 * g + hh
                        self.mm(v4(pt, 0)[:, hh, :], Pc[:, h, :], self.identf[:], True, False, [Pc, self.identf], [pt])
                        self.mm(v4(pt, 0)[:, hh, :], Pc[:, h, :], Yn[:, h, :], False, True, [Pc, Yn], [pt])
                    if lastk:
                        self.cp("act", Tt[:, hs, :], v4(pt, 0), [pt], [Tt])
                    else:
                        self.cp("act", Ptn[:, hs, :], v4(pt, 0), [pt], [Ptn])
                        pp = self.psum(6 + g)
                        for hh in range(4):
                            h = 4 * g + hh
                            self.mm(v4(pp, 0)[:, hh, :], Ptc[:, h, :], self.identf[:], True, False, [Ptc, self.identf], [pp])
                            self.mm(v4(pp, 0)[:, hh, :], Ptc[:, h, :], Xn[:, h, :], False, True, [Ptc, Xn], [pp])
                        self.cp("dve", Pn[:, hs, :], v4(pp, 0), [pp], [Pn])
                X, Y, Pc, Ptc = Xn, Yn, Pn, Ptn
            pK = self.psum(0, BF16)
            pV = self.psum(1, BF16)
            for h in range(8):
                self.tr(v8(pK)[:, h, :], kT[:, h, :], self.ident[:], [kT, self.ident], [pK])
                self.tr(v8(pV)[:, h, :], vT[:, h, :], self.ident[:], [vT, self.ident], [pV])
            self.tt("dve", Kb[:], v8(pK), bcl(sc1[:]), ALU.mult, [pK, sc1], [Kb])
            self.tt("dve", Kd[:], v8(pK), bcl(sc2[:]), ALU.mult, [pK, sc2], [Kd])
            self.tt("dve", Vb[:], v8(pV), bcl(btok[:]), ALU.mult, [pV, btok], [Vb])
            for g in range(2):
                pw = self.psum(2 + g)
                for hh in range(4):
                    h = 4 * g + hh
                    self.mm(v4(pw, 0)[:, hh, :], Kb[:, h, :], Tt[:, h, :], True, True, [Kb, Tt], [pw])
                hs = slice(4 * g, 4 * g + 4)
                self.ts("dve", nWTa[:, hs, 0:64], v4(pw, 0)[:, :, 0:64], -1.0, ALU.mult, [pw], [nWTa])
                self.op("act", lambda e, o_=nWTb[:, hs, 64:128], i_=v4(pw, 0)[:, :, 64:128]: e.mul(out=o_, in_=i_, mul=-1.0)
                        if False else e.activation(out=o_, in_=i_, func=AF.Copy, scale=-1.0), [pw], [nWTb])
            for g in range(2):
                pv = self.psum(4 + g)
                for hh in range(4):
                    h = 4 * g + hh
                    self.mm(v4(pv, 0)[:, hh, :], Tt[:, h, :], Vb[:, h, :], True, False, [Tt, Vb], [pv])
                    self.mm(v4(pv, 0)[:, hh, :], nWTa[:, h, :], Sabf[:, h, :], False, False, [nWTa, Sabf], [pv])
                    self.mm(v4(pv, 0)[:, hh, :], nWTb[:, h, :], Sbbf[:, h, :], False, True, [nWTb, Sbbf], [pv])
                self.cp("act", vn[:, 4 * g:4 * g + 4, :], v4(pv, 0), [pv], [vn])
            self.tt("dve", qsa[:, :, 0:64], qT[:, :, 0:64], EG[:, :, 0:64], ALU.mult, [qT, EG], [qsa])
            self.tt("pool", qsb[:, :, 64:128], qT[:, :, 64:128], EG[:, :, 64:128], ALU.mult, [qT, EG], [qsb])
            for g in range(2):
                po = self.psum(6 + g)
                for hh in range(4):
                    h = 4 * g + hh
                    self.mm(v4(po, 0)[:, hh, :], QKm[:, h, :], vn[:, h, :], True, False, [QKm, vn], [po])
                    self.mm(v4(po, 0)[:, hh, :], qsa[:, h, :], Sabf[:, h, :], False, False, [qsa, Sabf], [po])
                    self.mm(v4(po, 0)[:, hh, :], qsb[:, h, :], Sbbf[:, h, :], False, True, [qsb, Sbbf], [po])
                self.cp("act", o[:, 4 * g:4 * g + 4, :], v4(po, 0), [po], [o])
            for (F, Fb, egl, rows, bk) in ((Sa, Sabf, eglA, slice(0, 64), 0), (Sb, Sbbf, eglB, slice(64, 128), 2)):
                self.tt("pool", F[:], F[:], bcl(egl[:]), ALU.mult, [F, egl], [F])
                for g in range(2):
                    pss = self.psum(bk + g)
                    for hh in range(4):
                        h = 4 * g + hh
                        self.mm(v4(pss, 0)[:, hh, :], Kd[rows, h, :], vn[rows, h, :], True, True, [Kd, vn], [pss])
                    hs = slice(4 * g, 4 * g + 4)
                    self.tt("dve", F[:, hs, :], F[:, hs, :], v4(pss, 0), ALU.add, [F, pss], [F])
                self.cp("act", Fb[:], F[:], [F], [Fb])
            self.tt("pool", sq[:], o[:], o[:], ALU.mult, [o], [sq])
            self.op("dve", lambda e: e.tensor_reduce(out=ssq[:], in_=sq[:], axis=AX.X, op=ALU.add), [sq], [ssq])
            self.act(ssq[:], ssq[:], AF.Sqrt, [ssq, epsc], [ssq], bias=epsc[:], scale=1.0 / 128)
            self.op("dve", lambda e: e.reciprocal(out=ssq[:], in_=ssq[:]), [ssq], [ssq])
            self.tt("dve", on[:], o[:], bcl(ssq[:]), ALU.mult, [o, ssq], [on])
            pT = self.psum(4, BF16)
            for h in range(8):
                self.tr(v8(pT)[:, h, :], on[:, h, :], self.ident[:], [on, self.ident], [pT])
            self.act(gsz[:], zT[:], AF.Silu, [zT], [gsz])
            self.stt(mixed[:], v8(pT), gain[:, 0:1], gsz[:], ALU.mult, ALU.mult, [pT, gain, gsz], [mixed])
            for slot, col in ((0, ca), (1, cb)):
                if sample and slot == 1:
                    continue
                self.dma("sp", MXv[:, 0:8, col:col + 64], mixed[:, :, 64 * slot:64 * slot + 64], r=[mixed])
            if pi == 64:
                self.dma("sp", self.dout["delta_p"][i, 0].rearrange("h k v -> k h v"), Sa[:], r=[Sa])
                self.dma("sp", self.dout["delta_p"][i, 1].rearrange("h k v -> k h v"), Sb[:], r=[Sb])
            if sample:
                self.dma("sp", self.dout["delta_s"][i].rearrange("h k v -> k h v"), Sa[:], r=[Sa])
        self.P.barrier()


    def phaseB_s5(self, l):
        i = l // 2
        self.reset_arena()
        S = self.scr
        sb = self.sb
        TW = 516
        cosT = sb([16, TW], F32)
        sinT = sb([16, TW], F32)

        def t16():
            return sb([16], F32)
        lr, li, dt, mag, th, kk, tmp, sn, ch, cs, ar, ai, den, cr, ci, t2_ = (t16() for _ in range(16))
        D_ = self.din
        self.dma("sp", lr[:], D_["ssm_lambda_re"][i].rearrange("(sc gl) p -> (gl p) sc", gl=2), w=[lr], allow_slow_non_contiguous=True)
        self.dma("sp", li[:], D_["ssm_lambda_im"][i].rearrange("(sc gl) p -> (gl p) sc", gl=2), w=[li], allow_slow_non_contiguous=True)
        ldv = D_["ssm_log_dt"][i].rearrange("(sc gl) -> gl sc", gl=2)
        for gl in range(2):
            self.dma("sp", dt[64 * gl:64 * gl + 64, :], ldv[gl:gl + 1, :].to_broadcast([64, 16]), w=[dt], allow_slow_non_contiguous=True)
        self.act(dt[:], dt[:], AF.Exp, [dt], [dt])
        self.ts("dve", lr[:], lr[:], -1e-4, ALU.min, [lr], [lr])
        self.tt("dve", mag[:], lr[:], dt[:], ALU.mult, [lr, dt], [mag])
        self.act(mag[:], mag[:], AF.Exp, [mag], [mag])
        self.tt("dve", th[:], li[:], dt[:], ALU.mult, [li, dt], [th])
        self.memset("dve", kk[:], 0.0, [kk])
        for n_ in range(10):
            self.ts("dve", tmp[:], th[:], (n_ + 0.5) * 2 * math.pi, ALU.is_gt, [th], [tmp])
            self.tt("dve", kk[:], kk[:], tmp[:], ALU.add, [kk, tmp], [kk])
        self.stt(th[:], kk[:], -2 * math.pi, th[:], ALU.mult, ALU.add, [kk, th], [th])
        self.act(sn[:], th[:], AF.Sin, [th], [sn])
        self.act(ch[:], th[:], AF.Sin, [th], [ch], scale=0.5)
        self.tt("dve", cs[:], ch[:], ch[:], ALU.mult, [ch], [cs])
        self.ts("dve", cs[:], cs[:], -2.0, ALU.mult, [cs], [cs], s2=1.0, op1=ALU.add)
        self.tt("dve", ar[:], mag[:], cs[:], ALU.mult, [mag, cs], [ar])
        self.tt("dve", ai[:], mag[:], sn[:], ALU.mult, [mag, sn], [ai])
        self.tt("dve", den[:], lr[:], lr[:], ALU.mult, [lr], [den])
        self.tt("dve", tmp[:], li[:], li[:], ALU.mult, [li], [tmp])
        self.tt("dve", den[:], den[:], tmp[:], ALU.add, [den, tmp], [den])
        self.op("dve", lambda e: e.reciprocal(out=den[:], in_=den[:]), [den], [den])
        nr = t2_
        self.ts("dve", nr[:], ar[:], -1.0, ALU.add, [ar], [nr])
        self.tt("dve", cr[:], nr[:], lr[:], ALU.mult, [nr, lr], [cr])
        self.tt("dve", tmp[:], ai[:], li[:], ALU.mult, [ai, li], [tmp])
        self.tt("dve", cr[:], cr[:], tmp[:], ALU.add, [cr, tmp], [cr])
        self.tt("dve", cr[:], cr[:], den[:], ALU.mult, [cr, den], [cr])
        self.tt("dve", ci[:], ai[:], lr[:], ALU.mult, [ai, lr], [ci])
        self.tt("dve", tmp[:], nr[:], li[:], ALU.mult, [nr, li], [tmp])
        self.tt("dve", ci[:], ci[:], tmp[:], ALU.subtract, [ci, tmp], [ci])
        self.tt("dve", ci[:], ci[:], den[:], ALU.mult, [ci, den], [ci])
        cm, sm, c2, s2 = t16(), t16(), t16(), t16()
        self.cp("dve", cm[:], cs[:], [cs], [cm])
        self.cp("dve", sm[:], sn[:], [sn], [sm])
        self.memset("dve", cosT[:, :, 0:1], 1.0, [cosT])
        self.memset("dve", sinT[:, :, 0:1], 0.0, [sinT])
        mark_ = self.aoff
        tA = sb([16, 512], F32)
        tB = sb([16, 512], F32)
        m = 1
        while m <= 512:
            w_ = min(m, 513 - m)

            def bc(ap, w_=w_):
                return ap.unsqueeze(2).to_broadcast([128, 16, w_])
            self.tt("dve", tA[:, :, 0:w_], cosT[:, :, 0:w_], bc(cm[:]), ALU.mult, [cosT, cm], [tA])
            self.tt("pool", tB[:, :, 0:w_], sinT[:, :, 0:w_], bc(sm[:]), ALU.mult, [sinT, sm], [tB])
            self.tt("dve", cosT[:, :, m:m + w_], tA[:, :, 0:w_], tB[:, :, 0:w_], ALU.subtract, [tA, tB], [cosT])
            self.tt("dve", tA[:, :, 0:w_], sinT[:, :, 0:w_], bc(cm[:]), ALU.mult, [sinT, cm], [tA])
            self.tt("pool", tB[:, :, 0:w_], cosT[:, :, 0:w_], bc(sm[:]), ALU.mult, [cosT, sm], [tB])
            self.tt("dve", sinT[:, :, m:m + w_], tA[:, :, 0:w_], tB[:, :, 0:w_], ALU.add, [tA, tB], [sinT])
            self.tt("dve", c2[:], cm[:], cm[:], ALU.mult, [cm], [c2])
            self.tt("dve", s2[:], sm[:], sm[:], ALU.mult, [sm], [s2])
            self.tt("dve", s2[:], c2[:], s2[:], ALU.subtract, [c2, s2], [s2])
            self.tt("dve", c2[:], cm[:], sm[:], ALU.mult, [cm, sm], [c2])
            self.ts("dve", sm[:], c2[:], 2.0, ALU.mult, [c2], [sm])
            self.cp("dve", cm[:], s2[:], [s2], [cm])
            m *= 2
        self.P.barrier()
        self.aoff = mark_
        bre = sb([16, 16], F32)
        bim = sb([16, 16], F32)
        self.dma("sp", bre[:], D_["ssm_b_re"][i].rearrange("(sc gl) p m -> (gl p) sc m", gl=2), w=[bre])
        self.dma("sp", bim[:], D_["ssm_b_im"][i].rearrange("(sc gl) p m -> (gl p) sc m", gl=2), w=[bim])
        bbr = sb([16, 16], F32)
        bbi = sb([16, 16], F32)
        tC = sb([16, 16], F32)

        def bcB(ap):
            return ap.unsqueeze(2).to_broadcast([128, 16, 16])
        self.tt("dve", bbr[:], bre[:], bcB(cr[:]), ALU.mult, [bre, cr], [bbr])
        self.tt("dve", tC[:], bim[:], bcB(ci[:]), ALU.mult, [bim, ci], [tC])
        self.tt("dve", bbr[:], bbr[:], tC[:], ALU.subtract, [bbr, tC], [bbr])
        self.tt("dve", bbi[:], bim[:], bcB(cr[:]), ALU.mult, [bim, cr], [bbi])
        self.tt("dve", tC[:], bre[:], bcB(ci[:]), ALU.mult, [bre, ci], [tC])
        self.tt("dve", bbi[:], bbi[:], tC[:], ALU.add, [bbi, tC], [bbi])
        blk = sb([16, 128], BF16)
        BTr = sb([16, 128], BF16)
        BTi = sb([16, 128], BF16)
        for (bb, BT) in ((bbr, BTr), (bbi, BTi)):
            self.memset("pool", blk[:], 0.0, [blk])
            for r in range(4):
                self.cp("dve", blk[0:64, r::4, r * 32:r * 32 + 16], bb[0:64, r::4, :], [bb], [blk])
                self.cp("dve", blk[64:128, r::4, r * 32 + 16:r * 32 + 32], bb[64:128, r::4, :], [bb], [blk])
            for g in range(2):
                pt = self.psum(g, BF16)
                pv = pt.ap.rearrange("p (a b) -> p a b", a=8)
                for j in range(8):
                    self.tr(pv[:, j, :], blk[:, 8 * g + j, :], self.ident[:], [blk, self.ident], [pt])
                self.cp("act", BT[:, 8 * g:8 * g + 8, :], pv, [pt], [BT])
        evm = sb([1], F32)
        odm = sb([1], F32)
        self.memset("pool", evm[:], 0.0, [evm])
        self.memset("pool", odm[:], 1.0, [odm])
        for p0 in (0, 32, 64, 96):
            self.memset("pool", evm[p0:p0 + 16, :], 1.0, [evm])
            self.memset("pool", odm[p0:p0 + 16, :], 0.0, [odm])
        Cl = sb([4, 64], F32)
        Cin = sb([4, 128], BF16)
        CT = sb([4, 128], BF16)
        CTr = sb([16, 128], BF16)
        CTi = sb([16, 128], BF16)
        for (name, CTm, sgn) in (("ssm_c_re", CTr, 1.0), ("ssm_c_im", CTi, -1.0)):
            self.dma("sp", Cl[:], D_[name][i].rearrange("(q g) m p -> (g m) q p", q=4), w=[Cl])
            self.ts("dve", Cin[:, :, 0:64], Cl[:], evm[:, 0:1], ALU.mult, [Cl, evm], [Cin], s2=sgn, op1=ALU.mult)
            self.ts("dve", Cin[:, :, 64:128], Cl[:], odm[:, 0:1], ALU.mult, [Cl, odm], [Cin], s2=sgn, op1=ALU.mult)
            pt = self.psum(2, BF16)
            pv = pt.ap.rearrange("p (a b) -> p a b", a=8)
            for q in range(4):
                self.tr(pv[:, q, :], Cin[:, q, :], self.ident[:], [Cin, self.ident], [pt])
            self.cp("act", CT[:], pv[:, 0:4, :], [pt], [CT])
            self.memset("pool", CTm[:], 0.0, [CTm])
            for r in range(4):
                self.cp("dve", CTm[:, r::4, r * 32:r * 32 + 32], CT[:, :, r * 32:r * 32 + 32], [CT], [CTm])
        dsk = sb([4], F32)
        self.dma("sp", dsk[:], D_["ssm_d"][i].rearrange("(q p) -> p q", p=128), w=[dsk], allow_slow_non_contiguous=True)
        glb = sb([4], F32)
        self.dma("sp", glb[:], D_["ssm_glu_b"][i].rearrange("(q p) -> p q", p=128), w=[glb], allow_slow_non_contiguous=True)
        glw = sb([4, 512], BF16)
        self.load_weight(D_["ssm_glu_w"][i], 512, 512, glw, None, stage_cols=512)
        uTs = [sb([4, 512], BF16), sb([4, 512], BF16)]
        t1s = [sb([512], F32) for _ in range(2)]
        t2s = [sb([512], F32) for _ in range(2)]
        t3s = [sb([512], F32) for _ in range(2)]
        t4s = [sb([512], F32) for _ in range(2)]
        wrs = [sb([512], F32) for _ in range(2)]
        wis = [sb([512], F32) for _ in range(2)]
        xrs = [sb([512], BF16) for _ in range(2)]
        xis = [sb([512], BF16) for _ in range(2)]
        yv = sb([512], F32)
        g1 = sb([512], F32)
        hg = sb([4, 512], BF16)
        sg = sb([512], F32)
        obs = [sb([512], BF16) for _ in range(2)]
        wlr, wli, inr, ini, t16a, t16b = (t16() for _ in range(6))
        UTv = S["UT"].rearrange("(q p) t -> p q t", p=128)
        it = 0
        for si, seg in enumerate(SEGS):
            r0, n, s, t0, pc0 = seg
            first = (t0 == 0)
            lastseg = (t0 + n == SEQLEN[s])
            uT = uTs[si % 2]
            self.dma("sp", uT[:, :, 0:n], UTv[:, :, pc0:pc0 + n], w=[uT])
            if first:
                if s < 2:
                    self.memset("dve", inr[:], 0.0, [inr])
                    self.memset("dve", ini[:], 0.0, [ini])
                else:
                    self.dma("sp", wlr[:], D_["st_re"][i].rearrange("(sc gl) p -> (gl p) sc", gl=2), w=[wlr], allow_slow_non_contiguous=True)
                    self.dma("sp", wli[:], D_["st_im"][i].rearrange("(sc gl) p -> (gl p) sc", gl=2), w=[wli], allow_slow_non_contiguous=True)
                    jprev = 1
            if not first or s == 2:
                jp = jprev if (first and s == 2) else nprev
                ec, es = cosT[:, :, jp], sinT[:, :, jp]
                self.tt("dve", t16a[:], wlr[:], ec, ALU.mult, [wlr, cosT], [t16a])
                self.tt("dve", t16b[:], wli[:], es, ALU.mult, [wli, sinT], [t16b])
                self.tt("dve", inr[:], t16a[:], t16b[:], ALU.subtract, [t16a, t16b], [inr])
                self.tt("dve", t16a[:], wli[:], ec, ALU.mult, [wli, cosT], [t16a])
                self.tt("dve", t16b[:], wlr[:], es, ALU.mult, [wlr, sinT], [t16b])
                self.tt("dve", ini[:], t16a[:], t16b[:], ALU.add, [t16a, t16b], [ini])
            nprev = n
            for q in range(4):
                py = self.psum(4 + q % 2)
                for sl in range(4):
                    sc = 4 * q + sl
                    k2 = it % 2
                    it += 1
                    t1, t2, t3, t4, wr, wi, xr, xi = t1s[k2], t2s[k2], t3s[k2], t4s[k2], wrs[k2], wis[k2], xrs[k2], xis[k2]
                    pbr = self.psum(0 + k2)
                    pbi = self.psum(2 + k2)
                    self.mm(pbr[:, 0:n], BTr[:, sc, :], uT[:, q, 0:n], True, True, [BTr, uT], [pbr])
                    self.mm(pbi[:, 0:n], BTi[:, sc, :], uT[:, q, 0:n], True, True, [BTi, uT], [pbi])
                    cT, sT = cosT[:, sc, 0:n], sinT[:, sc, 0:n]
                    self.tt("dve", t1[:, 0:n], pbr[:, 0:n], cT, ALU.mult, [pbr, cosT], [t1])
                    self.tt("dve", t2[:, 0:n], pbi[:, 0:n], sT, ALU.mult, [pbi, sinT], [t2])
                    self.tt("dve", t3[:, 0:n], pbi[:, 0:n], cT, ALU.mult, [pbi, cosT], [t3])
                    self.tt("dve", t4[:, 0:n], pbr[:, 0:n], sT, ALU.mult, [pbr, sinT], [t4])
                    self.tt("pool", t1[:, 0:n], t1[:, 0:n], t2[:, 0:n], ALU.add, [t1, t2], [t1])
                    self.tt("pool", t3[:, 0:n], t3[:, 0:n], t4[:, 0:n], ALU.subtract, [t3, t4], [t3])
                    mg = mag[:, sc:sc + 1].to_broadcast([128, n])
                    self.op("dve", lambda e, o_=wr[:, 0:n], d0=mg, d1=t1[:, 0:n], in_=inr[:, sc:sc + 1]:
                            e.tensor_tensor_scan(out=o_, data0=d0, data1=d1, initial=in_, op0=ALU.mult, op1=ALU.add),
                            [t1, mag, inr], [wr], cost=0.12 + 2 * n / 960.0)
                    self.op("dve", lambda e, o_=wi[:, 0:n], d0=mg, d1=t3[:, 0:n], in_=ini[:, sc:sc + 1]:
                            e.tensor_tensor_scan(out=o_, data0=d0, data1=d1, initial=in_, op0=ALU.mult, op1=ALU.add),
                            [t3, mag, ini], [wi], cost=0.12 + 2 * n / 960.0)
                    self.cp("pool", wlr[:, sc:sc + 1], wr[:, n - 1:n], [wr], [wlr])
                    self.cp("pool", wli[:, sc:sc + 1], wi[:, n - 1:n], [wi], [wli])
                    self.tt("pool", t2[:, 0:n], wr[:, 0:n], cT, ALU.mult, [wr, cosT], [t2])
                    self.tt("pool", t4[:, 0:n], wi[:, 0:n], sT, ALU.mult, [wi, sinT], [t4])
                    self.tt("dve", xr[:, 0:n], t2[:, 0:n], t4[:, 0:n], ALU.subtract, [t2, t4], [xr])
                    self.tt("pool", t2[:, 0:n], wi[:, 0:n], cT, ALU.mult, [wi, cosT], [t2])
                    self.tt("pool", t4[:, 0:n], wr[:, 0:n], sT, ALU.mult, [wr, sinT], [t4])
                    self.tt("dve", xi[:, 0:n], t2[:, 0:n], t4[:, 0:n], ALU.add, [t2, t4], [xi])
                    self.mm(py[:, 0:n], CTr[:, sc, :], xr[:, 0:n], sl == 0, False, [CTr, xr], [py])
                    self.mm(py[:, 0:n], CTi[:, sc, :], xi[:, 0:n], False, sl == 3, [CTi, xi], [py])
                self.stt(yv[:, 0:n], uT[:, q, 0:n], dsk[:, q:q + 1], py[:, 0:n], ALU.mult, ALU.add, [uT, dsk, py], [yv])
                self.tt("pool", g1[:, 0:n], yv[:, 0:n], yv[:, 0:n], ALU.mult, [yv], [g1])
                self.ts("dve", g1[:, 0:n], g1[:, 0:n], 0.044715, ALU.mult, [g1], [g1], s2=1.0, op1=ALU.add)
                self.tt("pool", g1[:, 0:n], g1[:, 0:n], yv[:, 0:n], ALU.mult, [g1, yv], [g1])
                self.act(g1[:, 0:n], g1[:, 0:n], AF.Sigmoid, [g1], [g1], scale=2.0 * math.sqrt(2.0 / math.pi))
                self.tt("dve", hg[:, q, 0:n], yv[:, 0:n], g1[:, 0:n], ALU.mult, [yv, g1], [hg])
            for qo in range(4):
                pg = self.psum(6 + qo % 2)
                for qi in range(4):
                    self.mm(pg[:, 0:n], glw[:, qi, qo * 128:(qo + 1) * 128], hg[:, qi, 0:n], qi == 0, qi == 3, [glw, hg], [pg])
                self.act(sg[:, 0:n], pg[:, 0:n], AF.Sigmoid, [pg, glb], [sg], bias=glb[:, qo:qo + 1])
                ob = obs[qo % 2]
                self.tt("dve", ob[:, 0:n], hg[:, qo, 0:n], sg[:, 0:n], ALU.mult, [hg, sg], [ob])
                self.dma("sp", S["MIXT"][1024 + qo * 128:1024 + (qo + 1) * 128, pc0:pc0 + n], ob[:, 0:n], r=[ob])
            if lastseg:
                ec, es = cosT[:, :, n - 1], sinT[:, :, n - 1]
                fr, fi = t16(), t16()
                self.tt("dve", t16a[:], wlr[:], ec, ALU.mult, [wlr, cosT], [t16a])
                self.tt("dve", t16b[:], wli[:], es, ALU.mult, [wli, sinT], [t16b])
                self.tt("dve", fr[:], t16a[:], t16b[:], ALU.subtract, [t16a, t16b], [fr])
                self.tt("dve", t16a[:], wli[:], ec, ALU.mult, [wli, cosT], [t16a])
                self.tt("dve", t16b[:], wlr[:], es, ALU.mult, [wlr, sinT], [t16b])
                self.tt("dve", fi[:], t16a[:], t16b[:], ALU.add, [t16a, t16b], [fi])
                if s < 2:
                    dr, di = self.dout["re_p"][i, s], self.dout["im_p"][i, s]
                else:
                    dr, di = self.dout["re_s"][i], self.dout["im_s"][i]
                self.dma("sp", dr.rearrange("(sc gl) p -> (gl p) sc", gl=2), fr[:], r=[fr], allow_slow_non_contiguous=True)
                self.dma("sp", di.rearrange("(sc gl) p -> (gl p) sc", gl=2), fi[:], r=[fi], allow_slow_non_contiguous=True)
        self.P.barrier()


    def phaseC_outproj(self, l, wdram, KC):
        self.reset_arena()
        sb = self.sb
        wo = sb([KC, D], BF16)
        self.load_weight(wdram, KC * 128, D, wo, None, stage_cols=1024)
        xts = [sb([4, D], F32), sb([4, D], F32)]
        mts = [sb([KC, 512], BF16), sb([KC, 512], BF16)]
        MXv = self.scr["MIXT"].rearrange("(c p) t -> p c t", p=128)
        k = 0
        for si, seg in enumerate(SEGS):
            r0, n, s_, t0, pc0 = seg
            xt, mt = xts[si % 2], mts[si % 2]
            self.load_x(r0, n, xt)
            self.dma("sp", mt[:, :, 0:n], MXv[:, 0:KC, pc0:pc0 + n], w=[mt])
            nsub = (n + 127) // 128
            pj = min(128, n)
            for j in range(nsub):
                for dh in range(2):
                    ps = self.psum(k % 8)
                    k += 1
                    for c in range(KC):
                        self.mm(ps[0:pj, :], mt[:, c, j * 128:j * 128 + pj], wo[:, c, dh * 512:(dh + 1) * 512], c == 0, c == KC - 1,
                                [mt, wo], [ps])
                    self.tt("dve", xt[0:pj, j, dh * 512:(dh + 1) * 512], xt[0:pj, j, dh * 512:(dh + 1) * 512], ps[0:pj, :], ALU.add,
                            [xt, ps], [xt])
            self.store_x(r0, n, xt)
        self.P.barrier()

    def phaseD_ffn(self, l):
        self.reset_arena()
        sb = self.sb
        self.epsc = sb([1], F32)
        self.memset("dve", self.epsc[:], RMS_EPS, [self.epsc])
        w1 = sb([8, FFN], BF16)
        w3 = sb([8, FFN], BF16)
        w2 = sb([22, D], BF16)
        gain = self.load_gain(self.din["norm_ffn"][l])
        mark_ = self.aoff
        self.load_weight(self.din["ffn_w1"][l], D, FFN, w1, gain, stage_cols=1408)
        self.load_weight(self.din["ffn_w3"][l], D, FFN, w3, gain, stage_cols=1408)
        self.load_weight(self.din["ffn_w2"][l], FFN, D, w2, None, stage_cols=1024)
        self.P.barrier()
        self.aoff = mark_
        final = (l == DEPTH - 1)
        if final:
            gfin = sb([D], F32)
            self.dma("sp", gfin[:], self.din["norm_final"].rearrange("(o d) -> o d", o=1).to_broadcast([128, D]), w=[gfin])
        xts_ = [sb([2, D], F32), sb([2, D], F32)]
        hs_ = [sb([2, D], BF16), sb([2, D], BF16)]
        junk = sb([D], BF16)
        sss_ = [sb([4], F32), sb([4], F32)]
        hTs_ = [sb([8, 256], BF16), sb([8, 256], BF16)]
        aT = sb([22, 256], BF16)
        sgs = [sb([256], F32), sb([256], F32)]
        segs = []
        for (r0, n, s_, t0, pc0) in SEGS:
            for o_ in range(0, n, 256):
                segs.append((r0 + o_, min(256, n - o_), s_, t0 + o_))
        k = 0
        for si, seg in enumerate(segs):
            r0, n, s_, t0 = seg
            nsub = (n + 127) // 128
            pj = min(128, n)
            xt, h, ss, hT = xts_[si % 2], hs_[si % 2], sss_[si % 2], hTs_[si % 2]
            self.norm_transpose(seg, xt, h, junk, ss, hT, bank=[0, 1])
            for jh in range(22):
                pg = self.psum(2 + (jh % 2))
                pu = self.psum(4 + (jh % 2))
                for c in range(8):
                    self.mm(pg[:, 0:n], w1[:, c, jh * 128:(jh + 1) * 128], hT[:, c, 0:n], c == 0, c == 7, [w1, hT], [pg])
                for c in range(8):
                    self.mm(pu[:, 0:n], w3[:, c, jh * 128:(jh + 1) * 128], hT[:, c, 0:n], c == 0, c == 7, [w3, hT], [pu])
                sg = sgs[jh % 2]
                self.act(sg[:, 0:n], pg[:, 0:n], AF.Silu, [pg], [sg])
                self.tt("dve", aT[:, jh, 0:n], sg[:, 0:n], pu[:, 0:n], ALU.mult, [sg, pu], [aT])
            for j in range(nsub):
                for dh in range(2):
                    ps = self.psum(6 + k % 2)
                    k += 1
                    for jh in range(22):
                        self.mm(ps[0:pj, :], aT[:, jh, j * 128:j * 128 + pj], w2[:, jh, dh * 512:(dh + 1) * 512], jh == 0, jh == 21,
                                [aT, w2], [ps])
                    self.tt("dve", xt[0:pj, j, dh * 512:(dh + 1) * 512], xt[0:pj, j, dh * 512:(dh + 1) * 512], ps[0:pj, :], ALU.add,
                            [xt, ps], [xt])
            if not final:
                self.store_x(r0, n, xt)
            else:
                for j in range(nsub):
                    self.act(junk[0:pj, :], xt[0:pj, j, :], AF.Square, [xt], [junk, ss], accum=ss[0:pj, j:j + 1])
                self.act(ss[0:pj, 0:nsub], ss[0:pj, 0:nsub], AF.Sqrt, [ss], [ss], bias=self.epsc[0:pj, :], scale=1.0 / D)
                self.op("dve", lambda e, ss=ss, pj=pj, nsub=nsub: e.reciprocal(out=ss[0:pj, 0:nsub], in_=ss[0:pj, 0:nsub]), [ss], [ss])
                for j in range(nsub):
                    self.stt(xt[0:pj, j, :], xt[0:pj, j, :], ss[0:pj, j:j + 1], gfin[0:pj, :], ALU.mult, ALU.mult, [xt, ss, gfin], [xt])
                if s_ == 2:
                    self.dma("pool", self.dout["y_s"][t0:t0 + n, :], xt[0:n, 0, :], r=[xt])
                elif t0 >= 16:
                    self.dma("pool", self.dout["y_p"][s_, t0 - 16:t0 - 16 + n, :].rearrange("(j p) d -> p j d", p=128),
                             xt[:, 0:nsub, :], r=[xt])
        self.P.barrier()

    def phaseA_odd(self, l):
        i = l // 2
        self.reset_arena()
        sb = self.sb
        D_ = self.din
        S = self.scr
        self.epsc = sb([1], F32)
        self.memset("dve", self.epsc[:], RMS_EPS, [self.epsc])
        wbf = sb([8, ODD_PROJ], BF16)
        gain = self.load_gain(D_["norm_mix"][l])
        mark_ = self.aoff
        self.load_weight(D_["w_in_odd"][i], D, ODD_PROJ, wbf, gain, stage_cols=1540)
        self.P.barrier()
        self.aoff = mark_
        fb = sb([8], F32)
        self.dma("sp", fb[:], D_["fox_f_bias"][i:i + 1, :].to_broadcast([128, 8]), w=[fb])
        posf = sb([34], F32)
        self.op("pool", lambda e: e.iota(posf[:, 0:32], pattern=[[128, 32]], base=16, channel_multiplier=1,
                                         allow_small_or_imprecise_dtypes=True), (), [posf])
        self.op("pool", lambda e: e.iota(posf[:, 32:33], pattern=[[1, 1]], base=0, channel_multiplier=1,
                                         allow_small_or_imprecise_dtypes=True), (), [posf])
        self.op("pool", lambda e: e.iota(posf[:, 33:34], pattern=[[1, 1]], base=4096, channel_multiplier=1,
                                         allow_small_or_imprecise_dtypes=True), (), [posf])
        invf = sb([8], F32)
        for f_ in range(8):
            self.memset("dve", invf[:, f_:f_ + 1], 500000.0 ** (-f_ / 8.0), [invf])
        ang = sb([34, 8], F32)
        kq = sb([34, 8], F32)
        ki = sb([34, 8], mybir.dt.int32)
        msk = sb([34, 8], F32)
        cosR = sb([34, 8], F32)
        sinR = sb([34, 8], F32)
        self.tt("dve", ang[:], posf[:].unsqueeze(2).to_broadcast([128, 34, 8]), invf[:].unsqueeze(1).to_broadcast([128, 34, 8]),
                ALU.mult, [posf, invf], [ang])
        self.ts("dve", kq[:], ang[:], 1.0 / (2 * math.pi), ALU.mult, [ang], [kq])
        self.cp("dve", ki[:], kq[:], [kq], [ki])
        self.cp("dve", kq[:], ki[:], [ki], [kq])
        self.stt(ang[:], kq[:], -2 * math.pi, ang[:], ALU.mult, ALU.add, [kq, ang], [ang])
        self.ts("dve", msk[:], ang[:], math.pi, ALU.is_gt, [ang], [msk])
        self.stt(ang[:], msk[:], -2 * math.pi, ang[:], ALU.mult, ALU.add, [msk, ang], [ang])
        self.ts("dve", msk[:], ang[:], -math.pi, ALU.is_lt, [ang], [msk])
        self.stt(ang[:], msk[:], 2 * math.pi, ang[:], ALU.mult, ALU.add, [msk, ang], [ang])
        self.act(sinR[:], ang[:], AF.Sin, [ang], [sinR])
        self.act(cosR[:], ang[:], AF.Sin, [ang], [cosR], scale=0.5)
        self.tt("dve", cosR[:], cosR[:], cosR[:], ALU.mult, [cosR], [cosR])
        self.ts("dve", cosR[:], cosR[:], -2.0, ALU.mult, [cosR], [cosR], s2=1.0, op1=ALU.add)
        xts = [sb([4, D], F32), sb([4, D], F32)]
        h = sb([4, D], BF16)
        junk = sb([D], BF16)
        sss = [sb([4], F32), sb([4], F32)]
        hTs = [sb([8, 512], BF16), sb([8, 512], BF16)]
        oks = [sb([512], F32) for _ in range(4)]
        okbs = [sb([512], BF16) for _ in range(4)]
        kts = [sb([4, 128], BF16) for _ in range(4)]
        rts = [sb([8, 8], F32) for _ in range(3)]
        lfs = [sb([8], F32) for _ in range(2)]
        cms = [sb([8], F32) for _ in range(2)]
        carry = sb([8], F32)
        cst = [sb([512], F32) for _ in range(2)]
        cache_names = (("c_fk", "KFT", True), ("c_dk", "KDT", True), ("c_fv", "VF", False), ("c_dv", "VD", False))
        st = dict(ko=0, kb=0, kt=0, cc=0)

        def cum_step(lf, pj, row0):
            cm = cms[st["cc"] % 2]
            ps = self.psum(6 + st["cc"] % 2)
            st["cc"] += 1
            self.mm(ps[0:pj, 0:8], self.trif[0:pj, 0:pj], lf[0:pj, :], True, False, [self.trif, lf], [ps])
            self.mm(ps[0:pj, 0:8], self.identf[0:pj, 0:pj], carry[0:pj, :], False, True, [self.identf, carry], [ps])
            self.mm(ps[:, 8:16], self.onesf[0:pj, :], lf[0:pj, :], True, True, [self.onesf, lf], [ps])
            self.cp("dve", cm[0:pj, :], ps[0:pj, 0:8], [ps], [cm])
            self.dma("pool", S["CUM"][row0:row0 + pj, :], cm[0:pj, :], r=[cm])
            self.tt("dve", carry[:], carry[:], ps[:, 8:16], ALU.add, [carry, ps], [carry])

        def to_featmajor(okb, pj, dst, col0):
            pT = self.psum(st["kt"] % 2, BF16)
            kt = kts[st["kt"] % 4]
            st["kt"] += 1
            pv = pT.ap.rearrange("p (c t) -> p c t", c=8)
            for c in range(4):
                self.tr(pv[:, c, 0:pj], okb[0:pj, c * 128:(c + 1) * 128], self.ident[0:pj, 0:pj], [okb, self.ident], [pT])
            self.cp("dve", kt[:, :, 0:pj], pv[:, 0:4, 0:pj], [pT], [kt])
            self.dma("sp", dst.rearrange("(c p) t -> p c t", p=128)[:, :, col0:col0 + pj], kt[:, :, 0:pj], r=[kt])

        for si, seg in enumerate(SEGS):
            r0, n, s_, t0, pc0 = seg
            if t0 == 0:
                self.memset("dve", carry[:], 0.0, [carry])
            if s_ == 2:
                for m in range(32):
                    for (cn, dn, isk) in cache_names:
                        ok = oks[st["ko"] % 4]
                        okb = okbs[st["ko"] % 4]
                        st["ko"] += 1
                        self.dma("sp", ok[:], D_[cn][i, 128 * m:128 * (m + 1)].rearrange("t a b -> t (a b)"), w=[ok])
                        self.cp("act" if isk else "pool", okb[:], ok[:], [ok], [okb])
                        if isk:
                            to_featmajor(okb, 128, S[dn], AK[2] + 128 * m)
                        else:
                            self.dma("pool", S[dn][AK[2] + 128 * m:AK[2] + 128 * (m + 1), :], okb[:], r=[okb])
                    lf = lfs[m % 2]
                    self.dma("sp", lf[:], D_["c_fl"][i, 128 * m:128 * (m + 1), :], w=[lf])
                    cum_step(lf, 128, CB[2] + 1 + 128 * m)
            koff = 4096 if s_ == 2 else 0
            xt, ss, hT = xts[si % 2], sss[si % 2], hTs[si % 2]
            self.norm_transpose(seg, xt, h, junk, ss, hT, bank=[0, 1])
            nsub = (n + 127) // 128
            pj = min(128, n)
            for j in range(nsub):
                tcol = 33 if s_ == 2 else (32 if t0 == 0 else (t0 - 16) // 128 + j)
                tok0 = t0 + 128 * j
                for (name, col, rope, kind) in (("qf", 0, False, "q"), ("fk", 512, False, "k"), ("fv", 1024, False, "v"),
                                                ("qd", 1544, True, "q"), ("dk", 2056, True, "k"), ("dv", 2568, False, "v")):
                    ps = self.psum(2 + st["ko"] % 4)
                    ok = oks[st["ko"] % 4]
                    okb = okbs[st["ko"] % 4]
                    st["ko"] += 1
                    for c in range(8):
                        self.mm(ps[0:pj, :], hT[:, c, j * 128:j * 128 + pj], wbf[:, c, col:col + 512], c == 0, c == 7, [hT, wbf], [ps])
                    self.cp("act", ok[0:pj, :], ps[0:pj, :], [ps], [ok])
                    if rope:
                        okv = ok.ap.rearrange("p (h d) -> p h d", h=8)
                        cb = cosR[0:pj, tcol, :].unsqueeze(1).to_broadcast([pj, 8, 8])
                        sbb = sinR[0:pj, tcol, :].unsqueeze(1).to_broadcast([pj, 8, 8])
                        x1, x2 = okv[0:pj, :, 0:8], okv[0:pj, :, 8:16]
                        rt, r1, r2 = rts
                        self.tt("dve", rt[0:pj], x1, cb, ALU.mult, [ok, cosR], [rt])
                        self.tt("dve", r1[0:pj], x2, sbb, ALU.mult, [ok, sinR], [r1])
                        self.tt("dve", r2[0:pj], x2, cb, ALU.mult, [ok, cosR], [r2])
                        self.tt("dve", x2, x1, sbb, ALU.mult, [ok, sinR], [ok])
                        self.tt("dve", x2, x2, r2[0:pj], ALU.add, [ok, r2], [ok])
                        self.tt("dve", x1, rt[0:pj], r1[0:pj], ALU.subtract, [rt, r1], [ok])
                    if kind != "q":
                        if s_ < 2:
                            dst = self.dout[name + "_p"][i, s_, tok0:tok0 + pj]
                        else:
                            dst = self.dout[name + "_s"][i, tok0:tok0 + pj]
                        self.dma("pool", dst.rearrange("t a b -> t (a b)"), ok[0:pj, :], r=[ok])
                    if kind == "q":
                        self.act(okb[0:pj, :], ok[0:pj, :], AF.Copy, [ok], [okb], scale=0.125)
                        to_featmajor(okb, pj, S["QFT" if name == "qf" else "QDT"], SEQROW[s_] + tok0)
                    elif kind == "k":
                        self.cp("act", okb[0:pj, :], ok[0:pj, :], [ok], [okb])
                        to_featmajor(okb, pj, S["KFT" if name == "fk" else "KDT"], AK[s_] + koff + tok0)
                    else:
                        self.cp("pool", okb[0:pj, :], ok[0:pj, :], [ok], [okb])
                        dn = "VF" if name == "fv" else "VD"
                        self.dma("pool", S[dn][AK[s_] + koff + tok0:AK[s_] + koff + tok0 + pj, :], okb[0:pj, :], r=[okb])
                ps = self.psum(6 + j % 2)
                lf = lfs[j % 2]
                for c in range(8):
                    self.mm(ps[0:pj, 0:8], hT[:, c, j * 128:j * 128 + pj], wbf[:, c, 1536:1544], c == 0, c == 7, [hT, wbf], [ps])
                self.tt("dve", lf[0:pj, :], ps[0:pj, 0:8], fb[0:pj, :], ALU.add, [ps, fb], [lf])
                self.act(lf[0:pj, :], lf[0:pj, :], AF.Exp, [lf], [lf], scale=-1.0)
                self.act(lf[0:pj, :], lf[0:pj, :], AF.Ln, [lf, self.onec], [lf], bias=self.onec[0:pj, :])
                self.ts("dve", lf[0:pj, :], lf[0:pj, :], -1.0, ALU.mult, [lf], [lf])
                if s_ < 2:
                    dst = self.dout["fl_p"][i, s_, tok0:tok0 + pj, :]
                else:
                    dst = self.dout["fl_s"][i, tok0:tok0 + pj, :]
                self.dma("pool", dst, lf[0:pj, :], r=[lf])
                cum_step(lf, pj, CB[s_] + 1 + koff + tok0)
        self.P.barrier()

    def phaseB_attn(self, l):
        i = l // 2
        self.reset_arena()
        sb = self.sb
        D_ = self.din
        S = self.scr
        lam_init = 0.8 - 0.6 * math.exp(-0.3 * l)
        epsc = sb([1], F32)
        self.memset("dve", epsc[:], RMS_EPS, [epsc])
        dl = sb([4, 64], F32)
        self.dma("sp", dl[:], D_["diff_lambda"][i:i + 1].to_broadcast([128, 4, 64]), w=[dl])
        pr = sb([2, 64], F32)
        lsum = sb([2], F32)
        nlam = sb([1], F32)
        self.tt("dve", pr[:, 0, :], dl[:, 0, :], dl[:, 1, :], ALU.mult, [dl], [pr])
        self.tt("dve", pr[:, 1, :], dl[:, 2, :], dl[:, 3, :], ALU.mult, [dl], [pr])
        self.op("dve", lambda e: e.tensor_reduce(out=lsum[:], in_=pr[:], axis=AX.X, op=ALU.add), [pr], [lsum])
        self.act(lsum[:], lsum[:], AF.Exp, [lsum], [lsum])
        self.tt("dve", nlam[:], lsum[:, 1:2], lsum[:, 0:1], ALU.subtract, [lsum], [nlam])
        self.ts("dve", nlam[:], nlam[:], -lam_init, ALU.add, [nlam], [nlam])
        dg = sb([1], F32)
        self.dma("sp", dg[:], D_["diff_out_norm"][i].rearrange("(p o) -> p o", o=1), w=[dg])
        self.ts("dve", dg[:], dg[:], 1.0 - lam_init, ALU.mult, [dg], [dg])
        mfox = [sb([512], BF16) for _ in range(4)]
        mdif = [sb([512], BF16) for _ in range(4)]
        for kt in range(4):
            m_ = mfox[kt]
            self.memset("pool", m_[:], 1.0, [m_])
            self.op("pool", lambda e, m_=m_, kt=kt: e.affine_select(out=m_[:], in_=m_[:], pattern=[[1, 512]], compare_op=ALU.is_ge,
                                                                    fill=0.0, base=-128 * kt, channel_multiplier=-1), [m_], [m_])
            d_ = mdif[kt]
            self.memset("pool", d_[:], 1.0, [d_])
            if kt > 0:
                self.memset("pool", d_[:, 0:128 * kt], 0.0, [d_])
            self.memset("pool", d_[64:128, 128 * kt:128 * kt + 64], 0.0, [d_])
        KTs = [sb([4128], BF16) for _ in range(2)]
        QTs = [sb([4112], BF16) for _ in range(2)]
        Vs = [sb([33, 128], BF16) for _ in range(2)]
        cumT = sb([33, 8], F32)
        crefb = sb([8], F32)
        biasTs = [sb([33, 8], F32) for _ in range(9)]
        pTs = [sb([512], BF16) for _ in range(4)]
        rls = [sb([512], F32) for _ in range(2)]
        o1 = sb([512], F32)
        o2 = sb([512], F32)
        sqb = sb([512], BF16)
        rstd = sb([512], F32)
        mixs = [sb([512], BF16) for _ in range(2)]
        cnt = dict(ld=0, pt=0, mx=0)
        for s_ in range(3):
            nkeys = LP if s_ < 2 else 4096 + LS
            if s_ < 2:
                ktiles = [(0, 16)] + [(16 + 128 * m, 128) for m in range(32)]
            else:
                ktiles = [(128 * m, 128) for m in range(32)] + [(4096, 32)]
            ntile = len(ktiles)
            qsegs = [sg for sg in SEGS if sg[2] == s_]
            for m, (k0, nk) in enumerate(ktiles):
                if nk == 128 and (m == 0 or ktiles[m - 1][1] != 128):
                    m_end = m
                    while m_end < ntile and ktiles[m_end][1] == 128:
                        m_end += 1
                    self.dma("sp", cumT[:, m:m_end, :],
                             S["CUM"][CB[s_] + 1 + k0:CB[s_] + 1 + k0 + 128 * (m_end - m), :].rearrange("(m p) h -> p m h", p=128), w=[cumT])
                elif nk != 128:
                    self.dma("sp", cumT[0:nk, m, :], S["CUM"][CB[s_] + 1 + k0:CB[s_] + 1 + k0 + nk, :], w=[cumT])
            for qi, (r0, nq, _s, t0, pc0) in enumerate(qsegs):
                qkey0 = t0 + (4096 if s_ == 2 else 0)
                qref = qkey0 + nq // 2
                self.dma("sp", crefb[:], S["CUM"][CB[s_] + qref:CB[s_] + qref + 1, :].to_broadcast([128, 8]), w=[crefb])
                self.tt("dve", biasTs[qi][:], crefb[:].unsqueeze(1).to_broadcast([128, 33, 8]), cumT[:], ALU.subtract, [crefb, cumT], [biasTs[qi]])
                self.ts("dve", biasTs[qi][:], biasTs[qi][:], 70.0, ALU.min, [biasTs[qi]], [biasTs[qi]])
            for kind in ("fox", "dif"):
                for c in range(4):
                    KT, QT, V = KTs[cnt["ld"] % 2], QTs[cnt["ld"] % 2], Vs[cnt["ld"] % 2]
                    cnt["ld"] += 1
                    ksrc = S["KFT" if kind == "fox" else "KDT"]
                    qsrc = S["QFT" if kind == "fox" else "QDT"]
                    vsrc = S["VF" if kind == "fox" else "VD"]
                    self.dma("sp", KT[:, 0:nkeys], ksrc[128 * c:128 * (c + 1), AK[s_]:AK[s_] + nkeys], w=[KT])
                    self.dma("sp", QT[:, 0:SEQLEN[s_]], qsrc[128 * c:128 * (c + 1), SEQROW[s_]:SEQROW[s_] + SEQLEN[s_]], w=[QT])
                    for m, (k0, nk) in enumerate(ktiles):
                        if nk == 128 and (m == 0 or ktiles[m - 1][1] != 128):
                            m_end = m
                            while m_end < ntile and ktiles[m_end][1] == 128:
                                m_end += 1
                            self.dma("pool", V[:, m:m_end, :],
                                     vsrc[AK[s_] + k0:AK[s_] + k0 + 128 * (m_end - m), 128 * c:128 * (c + 1)].rearrange("(m p) d -> p m d", p=128), w=[V])
                        elif nk != 128:
                            self.dma("pool", V[0:nk, m, :], vsrc[AK[s_] + k0:AK[s_] + k0 + nk, 128 * c:128 * (c + 1)], w=[V])
                    for qi, (r0, nq, _s, t0, pc0) in enumerate(qsegs):
                        qkey0 = t0 + (4096 if s_ == 2 else 0)
                        tiles = [(m, k0, nk) for m, (k0, nk) in enumerate(ktiles) if k0 < qkey0 + nq]
                        if kind == "fox":
                            accO, accL = (self.psum(4), self.psum(5)) if qi % 2 == 0 else (self.psum(6), self.psum(7))
                        else:
                            accO, accL, accO2, accL2 = self.psum(4), self.psum(5), self.psum(6), self.psum(7)
                        for ti, (m, k0, nk) in enumerate(tiles):
                            first, lastt = (ti == 0), (ti == len(tiles) - 1)
                            diag = (k0 >= qkey0)
                            pts = []
                            for hl in range(2):
                                pss = self.psum(cnt["pt"] % 4)
                                pT = pTs[cnt["pt"] % 4]
                                cnt["pt"] += 1
                                hp = slice(64 * hl, 64 * hl + 64)
                                self.mm(pss[0:nk, 0:nq], KT[hp, k0:k0 + nk], QT[hp, t0:t0 + nq], True, True, [KT, QT], [pss])
                                if kind == "fox":
                                    hh = 2 * c + hl
                                    self.act(pT[0:nk, 0:nq], pss[0:nk, 0:nq], AF.Exp, [pss, biasTs[qi]], [pT], bias=biasTs[qi][0:nk, m, hh:hh + 1])
                                else:
                                    self.act(pT[0:nk, 0:nq], pss[0:nk, 0:nq], AF.Exp, [pss], [pT])
                                if diag:
                                    kt_ = (k0 - qkey0) // 128
                                    if kind == "fox":
                                        mk = mfox[kt_]
                                    elif nq == 512:
                                        mk = mdif[kt_]
                                    else:
                                        mk = None
                                    if mk is not None:
                                        self.tt("pool", pT[0:nk, 0:nq], pT[0:nk, 0:nq], mk[0:nk, 0:nq], ALU.mult, [pT, mk], [pT])
                                pts.append(pT)
                            if kind == "fox":
                                for hl in range(2):
                                    hp = slice(64 * hl, 64 * hl + 64)
                                    self.mm(accO[hp, 0:nq], V[0:nk, m, hp], pts[hl][0:nk, 0:nq], first, lastt, [V, pts[hl]], [accO])
                                    self.mm(accL[hp, 0:nq], self.ones_bf[0:nk, 0:64], pts[hl][0:nk, 0:nq], first, lastt, [self.ones_bf, pts[hl]], [accL])
                            else:
                                self.mm(accO[:, 0:nq], V[0:nk, m, :], pts[0][0:nk, 0:nq], first, lastt, [V, pts[0]], [accO])
                                self.mm(accL[:, 0:nq], self.ones_bf[0:nk, :], pts[0][0:nk, 0:nq], first, lastt, [self.ones_bf, pts[0]], [accL])
                                self.mm(accO2[:, 0:nq], V[0:nk, m, :], pts[1][0:nk, 0:nq], first, lastt, [V, pts[1]], [accO2])
                                self.mm(accL2[:, 0:nq], self.ones_bf[0:nk, :], pts[1][0:nk, 0:nq], first, lastt, [self.ones_bf, pts[1]], [accL2])
                        mix = mixs[cnt["mx"] % 2]
                        cnt["mx"] += 1
                        rl = rls[0]
                        self.op("dve", lambda e, rl=rl, accL=accL, nq=nq: e.reciprocal(out=rl[:, 0:nq], in_=accL[:, 0:nq]), [accL], [rl], cost=0.12 + nq / 960.0)
                        if kind == "fox":
                            self.tt("dve", mix[:, 0:nq], accO[:, 0:nq], rl[:, 0:nq], ALU.mult, [accO, rl], [mix])
                            row0 = 128 * c
                        else:
                            rl2 = rls[1]
                            self.op("dve", lambda e, rl2=rl2, accL2=accL2, nq=nq: e.reciprocal(out=rl2[:, 0:nq], in_=accL2[:, 0:nq]), [accL2], [rl2], cost=0.12 + nq / 960.0)
                            self.tt("dve", o1[:, 0:nq], accO[:, 0:nq], rl[:, 0:nq], ALU.mult, [accO, rl], [o1])
                            self.tt("dve", o2[:, 0:nq], accO2[:, 0:nq], rl2[:, 0:nq], ALU.mult, [accO2, rl2], [o2])
                            self.stt(o1[:, 0:nq], o2[:, 0:nq], nlam[:, 0:1], o1[:, 0:nq], ALU.mult, ALU.add, [o2, nlam, o1], [o1])
                            self.tt("pool", sqb[:, 0:nq], o1[:, 0:nq], o1[:, 0:nq], ALU.mult, [o1], [sqb])
                            pq = self.psum(cnt["pt"] % 4)
                            cnt["pt"] += 1
                            self.mm(pq[:, 0:nq], self.ones_bf[:], sqb[:, 0:nq], True, True, [self.ones_bf, sqb], [pq])
                            self.act(rstd[:, 0:nq], pq[:, 0:nq], AF.Sqrt, [pq, epsc], [rstd], bias=epsc[:], scale=1.0 / 128)
                            self.op("dve", lambda e, nq=nq: e.reciprocal(out=rstd[:, 0:nq], in_=rstd[:, 0:nq]), [rstd], [rstd], cost=0.12 + nq / 960.0)
                            self.stt(mix[:, 0:nq], o1[:, 0:nq], dg[:, 0:1], rstd[:, 0:nq], ALU.mult, ALU.mult, [o1, dg, rstd], [mix])
                            row0 = 512 + 128 * c
                        self.dma("sp", S["MIXT"][row0:row0 + 128, pc0:pc0 + nq], mix[:, 0:nq], r=[mix])
        self.P.barrier()

_CACHE = {}


def get_program(stages):
    key = tuple(sorted(stages))
    if key not in _CACHE:
        b = Builder(set(stages))
        nc = b.build()
        _CACHE[key] = (b, nc)
    return _CACHE[key]


STAGES = ("L0", "L1", "L2", "L3")

W_NAMES = ["norm_mix", "norm_ffn", "norm_final", "w_in_even", "w_out_even", "conv_w", "gdn_a_log", "gdn_dt_bias",
           "gdn_out_norm", "ssm_lambda_re", "ssm_lambda_im", "ssm_log_dt", "ssm_b_re", "ssm_b_im", "ssm_c_re",
           "ssm_c_im", "ssm_d", "ssm_glu_w", "ssm_glu_b", "w_in_odd", "w_out_odd", "fox_f_bias", "diff_lambda",
           "diff_out_norm", "ffn_w1", "ffn_w3", "ffn_w2"]


def kernel(**inp):
    b, nc = get_program(STAGES)
    f = lambda a: np.ascontiguousarray(np.asarray(a, dtype=np.float32))
    shared = {k: f(inp[k]) for k in W_NAMES}
    meta = f(inp["meta_tokens"])
    in_maps = []
    ncore = int(os.environ.get("MK_DEV_CORES", NCORES))
    for c in range(ncore):
        m = dict(shared)
        m["xp"] = f(inp["x_prompt"][2 * c:2 * c + 2])
        m["xs"] = f(inp["x_sample"][c])
        m["meta"] = meta
        m["st_conv"] = f(inp["state_conv"][:, c])
        m["st_delta"] = f(inp["state_delta"][:, c])
        m["st_re"] = f(inp["state_ssm_re"][:, c])
        m["st_im"] = f(inp["state_ssm_im"][:, c])
        m["c_fk"] = f(inp["cache_fox_k"][:, c])
        m["c_fv"] = f(inp["cache_fox_v"][:, c])
        m["c_fl"] = f(inp["cache_fox_logf"][:, c])
        m["c_dk"] = f(inp["cache_diff_k"][:, c])
        m["c_dv"] = f(inp["cache_diff_v"][:, c])
        in_maps.append(m)
    res = run_bass_kernel_spmd(nc, in_maps, core_ids=list(range(ncore)))
    R = list(res.results)
    while len(R) < NCORES:
        R.append({k: np.zeros_like(np.asarray(v)) for k, v in R[0].items()})
    cat = lambda name, ax: np.concatenate([np.asarray(r[name]) for r in R], axis=ax)
    stk = lambda name, ax: np.stack([np.asarray(r[name]) for r in R], axis=ax)
    outs = (
        cat("y_p", 0), stk("y_s", 0),
        cat("conv_p", 1), stk("conv_s", 1),
        cat("delta_p", 1), stk("delta_s", 1),
        cat("re_p", 1), stk("re_s", 1), cat("im_p", 1), stk("im_s", 1),
        cat("fk_p", 1), stk("fk_s", 1), cat("fv_p", 1), stk("fv_s", 1),
        cat("fl_p", 1), stk("fl_s", 1), cat("dk_p", 1), stk("dk_s", 1),
        cat("dv_p", 1), stk("dv_s", 1),
    )
    return tuple(np.ascontiguousarray(o, dtype=np.float32) for o in outs)
```
